# Optimizing a Trainium2 kernel written in Bass

```python
import math
import jax, jax.numpy as jnp
from jax import lax
import numpy as np

D_MODEL = 1024
BATCH = 16
SEQ = 256
DEPTH = 4
DEC_BATCH = 8
DEC_SEQ = 1024
PAST_LEN = 512

GRID_W = 64
N_EVEN = (DEPTH + 1) // 2
N_ODD = DEPTH // 2
EPS = 1e-6
ROPE_BASE = 10000.0
Q_BLOCK = 128
S5_WIDTH = D_MODEL // 2
S5_GROUP = 16
S5_GROUPS = S5_WIDTH // S5_GROUP
S5_STATE = 64
DIFF_HEAD_DIM = 64
DIFF_HEADS = (D_MODEL // 2) // (2 * DIFF_HEAD_DIM)
DIFF_WIDTH = DIFF_HEADS * 2 * DIFF_HEAD_DIM
EVEN_IN = S5_WIDTH + 3 * DIFF_WIDTH
EVEN_OUT = S5_WIDTH + DIFF_WIDTH
MLA_HEADS = 16
MLA_NOPE = 64
MLA_ROPE = 32
MLA_V = 64
MLA_Q_RANK = 256
MLA_KV_RANK = 128
ODD_IN = MLA_Q_RANK + MLA_KV_RANK + MLA_ROPE
D_FF = ((8 * D_MODEL // 3 + 255) // 256) * 256

kernel_name = 'hybrid_s5_diffattn_mla_flow_step'


def rmsnorm(x, g):
    xf = x.astype(jnp.float32)
    y = xf * lax.rsqrt(jnp.mean(xf * xf, axis=-1, keepdims=True) + EPS)
    return (y * g.astype(jnp.float32)).astype(x.dtype)


def axial_rope(n_tokens, rot_dim):
    rows = n_tokens // GRID_W
    row = jnp.repeat(jnp.arange(rows, dtype=jnp.float32), GRID_W)
    col = jnp.tile(jnp.arange(GRID_W, dtype=jnp.float32), rows)
    n_freq = rot_dim // 4
    inv = ROPE_BASE ** (-jnp.arange(n_freq, dtype=jnp.float32) / n_freq)
    ang = jnp.concatenate([row[:, None] * inv, col[:, None] * inv], axis=-1)
    return jnp.cos(ang), jnp.sin(ang)


def apply_rope(x, cos, sin):
    half = x.shape[-1] // 2
    shape = (x.shape[1],) + (1,) * (x.ndim - 3) + (half,)
    c = cos.reshape(shape).astype(x.dtype)
    s = sin.reshape(shape).astype(x.dtype)
    x1, x2 = x[..., :half], x[..., half:]
    return jnp.concatenate([x1 * c - x2 * s, x2 * c + x1 * s], axis=-1)


def map_query_blocks(fn, *qs):
    b, n = qs[0].shape[:2]
    nb = n // Q_BLOCK
    blocked = tuple(jnp.moveaxis(q.reshape((b, nb, Q_BLOCK) + q.shape[2:]), 1, 0) for q in qs)
    out = lax.map(lambda qb: fn(*qb), blocked)
    return jnp.moveaxis(out, 0, 1).reshape((b, n) + out.shape[3:])


def complex_affine_combine(e1, e2):
    a1r, a1i, b1r, b1i = e1
    a2r, a2i, b2r, b2i = e2
    ar = a2r * a1r - a2i * a1i
    ai = a2r * a1i + a2i * a1r
    br = a2r * b1r - a2i * b1i + b2r
    bi = a2r * b1i + a2i * b1r + b2i
    return ar, ai, br, bi


def s5_direction(u, h0_re, h0_im, lam_re, lam_im, log_dt, b_re, b_im, c_re, c_im, reverse):
    f32 = jnp.float32
    lr, li = lam_re.astype(f32), lam_im.astype(f32)
    dt = jnp.exp(log_dt.astype(f32))[:, None]
    mag = jnp.exp(lr * dt)
    a_re, a_im = mag * jnp.cos(li * dt), mag * jnp.sin(li * dt)
    den = lr * lr + li * li
    f_re = ((a_re - 1.0) * lr + a_im * li) / den
    f_im = (a_im * lr - (a_re - 1.0) * li) / den
    br, bi = b_re.astype(f32), b_im.astype(f32)
    bb_re = f_re[..., None] * br - f_im[..., None] * bi
    bb_im = f_re[..., None] * bi + f_im[..., None] * br
    bu_re = jnp.einsum('bngh,gph->bngp', u, bb_re)
    bu_im = jnp.einsum('bngh,gph->bngp', u, bb_im)
    edge = -1 if reverse else 0
    h0r, h0i = h0_re.astype(f32), h0_im.astype(f32)
    bu_re = bu_re.at[:, edge].add(a_re * h0r - a_im * h0i)
    bu_im = bu_im.at[:, edge].add(a_re * h0i + a_im * h0r)
    ar = jnp.broadcast_to(a_re, bu_re.shape)
    ai = jnp.broadcast_to(a_im, bu_re.shape)
    _, _, hr, hi = lax.associative_scan(complex_affine_combine, (ar, ai, bu_re, bu_im), reverse=reverse, axis=1)
    y = jnp.einsum('bngp,ghp->bngh', hr, c_re.astype(f32)) - jnp.einsum('bngp,ghp->bngh', hi, c_im.astype(f32))
    fin = 0 if reverse else -1
    return y, hr[:, fin], hi[:, fin]


def s5_mixer(u, h0_re, h0_im, lam_re, lam_im, log_dt, b_re, b_im, c_re, c_im, d, glu_w, glu_b):
    b, n, _ = u.shape
    uf = u.astype(jnp.float32).reshape(b, n, S5_GROUPS, S5_GROUP)
    y_f, fr_f, fi_f = s5_direction(uf, h0_re[:, 0], h0_im[:, 0], lam_re[0], lam_im[0], log_dt[0],
                                   b_re[0], b_im[0], c_re[0], c_im[0], False)
    y_b, fr_b, fi_b = s5_direction(uf, h0_re[:, 1], h0_im[:, 1], lam_re[1], lam_im[1], log_dt[1],
                                   b_re[1], b_im[1], c_re[1], c_im[1], True)
    y = y_f + y_b + d.astype(jnp.float32).reshape(S5_GROUPS, S5_GROUP) * uf
    g = jax.nn.gelu(y.reshape(b, n, S5_WIDTH))
    out = g * jax.nn.sigmoid(g @ glu_w.astype(jnp.float32) + glu_b.astype(jnp.float32))
    fin_re = jnp.stack([fr_f, fr_b], axis=1)
    fin_im = jnp.stack([fi_f, fi_b], axis=1)
    return out.astype(u.dtype), fin_re, fin_im


def diff_attention(q, k, v, lam, lam_init, subln_g):
    b, n = q.shape[:2]
    scale = DIFF_HEAD_DIM ** -0.5
    kf, vf = k.astype(jnp.float32), v.astype(jnp.float32)

    def attend(qb):
        s = jnp.einsum('bqhcd,bkhcd->bhcqk', qb.astype(jnp.float32), kf) * scale
        p = jax.nn.softmax(s, axis=-1)
        w = p[:, :, 0] - lam * p[:, :, 1]
        return jnp.einsum('bhqk,bkhe->bqhe', w, vf)

    o = map_query_blocks(attend, q)
    o = rmsnorm(o, subln_g) * (1.0 - lam_init)
    return o.reshape(b, n, DIFF_WIDTH).astype(q.dtype)


def mla_attention(qn, qr, kn, kr, v):
    b, n = qn.shape[:2]
    scale = (MLA_NOPE + MLA_ROPE) ** -0.5
    knf, krf, vf = kn.astype(jnp.float32), kr.astype(jnp.float32), v.astype(jnp.float32)

    def attend(qnb, qrb):
        s = (jnp.einsum('bqhd,bkhd->bhqk', qnb.astype(jnp.float32), knf)
             + jnp.einsum('bqhr,bkr->bhqk', qrb.astype(jnp.float32), krf)) * scale
        p = jax.nn.softmax(s, axis=-1)
        return jnp.einsum('bhqk,bkhd->bqhd', p, vf)

    o = map_query_blocks(attend, qn, qr)
    return o.reshape(b, n, MLA_HEADS * MLA_V).astype(qn.dtype)


def block(x, cond, w_mod, b_mod, g_pre_mix, g_post_mix, g_pre_ffn, g_post_ffn, w_gate, w_up, w_down, mixer):
    mod = (jax.nn.silu(cond) @ w_mod + b_mod)[:, None, :]
    sh1, sc1, g1, sh2, sc2, g2 = jnp.split(mod, 6, axis=-1)
    h = rmsnorm(x, g_pre_mix) * (1.0 + sc1) + sh1
    y, aux = mixer(h)
    x = x + g1 * rmsnorm(y, g_post_mix)
    h = rmsnorm(x, g_pre_ffn) * (1.0 + sc2) + sh2
    y = (jax.nn.silu(h @ w_gate) * (h @ w_up)) @ w_down
    x = x + g2 * rmsnorm(y, g_post_ffn)
    return x, aux


def setup_inputs(seed: int = 0) -> dict:
    key = jax.random.key(seed)
    ks = iter(jax.random.split(key, 64))
    f32 = jnp.float32

    def nrm(shape, scale=1.0):
        return jax.random.normal(next(ks), shape, f32) * scale

    def gain(shape):
        return 1.0 + nrm(shape, 0.01)

    lam_re = -0.5 + nrm((N_EVEN, 2, S5_GROUPS, S5_STATE), 0.01)
    lam_im = math.pi * jnp.arange(S5_STATE, dtype=f32) + nrm((N_EVEN, 2, S5_GROUPS, S5_STATE), 0.01)
    log_dt = jax.random.uniform(next(ks), (N_EVEN, 2, S5_GROUPS), f32, math.log(1e-3), math.log(1e-1))
    return {
        'x_prompt': nrm((BATCH, SEQ, D_MODEL)),
        'x_sample': nrm((DEC_BATCH, DEC_SEQ, D_MODEL)),
        'state_s5_re': nrm((DEC_BATCH, N_EVEN, 2, S5_GROUPS, S5_STATE), 0.5),
        'state_s5_im': nrm((DEC_BATCH, N_EVEN, 2, S5_GROUPS, S5_STATE), 0.5),
        'cache_diff_k': nrm((DEC_BATCH, N_EVEN, PAST_LEN, DIFF_HEADS, 2, DIFF_HEAD_DIM)),
        'cache_diff_v': nrm((DEC_BATCH, N_EVEN, PAST_LEN, DIFF_HEADS, 2 * DIFF_HEAD_DIM)),
        'cache_mla_ckv': nrm((DEC_BATCH, N_ODD, PAST_LEN, MLA_KV_RANK)),
        'cache_mla_krope': nrm((DEC_BATCH, N_ODD, PAST_LEN, MLA_ROPE)),
        'c': nrm((DEC_BATCH, D_MODEL)),
        'c_ctx': nrm((D_MODEL,)),
        'w_mod': nrm((DEPTH, D_MODEL, 6 * D_MODEL), D_MODEL ** -0.5),
        'b_mod': nrm((DEPTH, 6 * D_MODEL), 0.01),
        'g_pre_mix': gain((DEPTH, D_MODEL)),
        'g_post_mix': gain((DEPTH, D_MODEL)),
        'g_pre_ffn': gain((DEPTH, D_MODEL)),
        'g_post_ffn': gain((DEPTH, D_MODEL)),
        'w_ffn_gate': nrm((DEPTH, D_MODEL, D_FF), D_MODEL ** -0.5),
        'w_ffn_up': nrm((DEPTH, D_MODEL, D_FF), D_MODEL ** -0.5),
        'w_ffn_down': nrm((DEPTH, D_FF, D_MODEL), D_FF ** -0.5),
        'w_in_even': nrm((N_EVEN, D_MODEL, EVEN_IN), D_MODEL ** -0.5),
        'w_out_even': nrm((N_EVEN, EVEN_OUT, D_MODEL), EVEN_OUT ** -0.5),
        's5_lam_re': lam_re,
        's5_lam_im': lam_im,
        's5_log_dt': log_dt,
        's5_b_re': nrm((N_EVEN, 2, S5_GROUPS, S5_STATE, S5_GROUP), (2 * S5_GROUP) ** -0.5),
        's5_b_im': nrm((N_EVEN, 2, S5_GROUPS, S5_STATE, S5_GROUP), (2 * S5_GROUP) ** -0.5),
        's5_c_re': nrm((N_EVEN, 2, S5_GROUPS, S5_GROUP, S5_STATE), (2 * S5_STATE) ** -0.5),
        's5_c_im': nrm((N_EVEN, 2, S5_GROUPS, S5_GROUP, S5_STATE), (2 * S5_STATE) ** -0.5),
        's5_d': nrm((N_EVEN, S5_WIDTH)),
        's5_glu_w': nrm((N_EVEN, S5_WIDTH, S5_WIDTH), S5_WIDTH ** -0.5),
        's5_glu_b': nrm((N_EVEN, S5_WIDTH), 0.01),
        'diff_lam_q1': nrm((N_EVEN, DIFF_HEAD_DIM), 0.1),
        'diff_lam_k1': nrm((N_EVEN, DIFF_HEAD_DIM), 0.1),
        'diff_lam_q2': nrm((N_EVEN, DIFF_HEAD_DIM), 0.1),
        'diff_lam_k2': nrm((N_EVEN, DIFF_HEAD_DIM), 0.1),
        'diff_subln_g': gain((N_EVEN, 2 * DIFF_HEAD_DIM)),
        'w_in_odd': nrm((N_ODD, D_MODEL, ODD_IN), D_MODEL ** -0.5),
        'mla_q_norm_g': gain((N_ODD, MLA_Q_RANK)),
        'mla_w_q_up': nrm((N_ODD, MLA_Q_RANK, MLA_HEADS * (MLA_NOPE + MLA_ROPE)), MLA_Q_RANK ** -0.5),
        'mla_kv_norm_g': gain((N_ODD, MLA_KV_RANK)),
        'mla_w_kv_up': nrm((N_ODD, MLA_KV_RANK, MLA_HEADS * (MLA_NOPE + MLA_V)), MLA_KV_RANK ** -0.5),
        'w_out_odd': nrm((N_ODD, MLA_HEADS * MLA_V, D_MODEL), (MLA_HEADS * MLA_V) ** -0.5),
    }


def reference(x_prompt, x_sample, state_s5_re, state_s5_im, cache_diff_k, cache_diff_v, cache_mla_ckv,
              cache_mla_krope, c, c_ctx, w_mod, b_mod, g_pre_mix, g_post_mix, g_pre_ffn, g_post_ffn,
              w_ffn_gate, w_ffn_up, w_ffn_down, w_in_even, w_out_even, s5_lam_re, s5_lam_im, s5_log_dt,
              s5_b_re, s5_b_im, s5_c_re, s5_c_im, s5_d, s5_glu_w, s5_glu_b, diff_lam_q1, diff_lam_k1,
              diff_lam_q2, diff_lam_k2, diff_subln_g, w_in_odd, mla_q_norm_g, mla_w_q_up, mla_kv_norm_g,
              mla_w_kv_up, w_out_odd):
    n_lat = x_sample.shape[1]
    cos_d, sin_d = axial_rope(n_lat, DIFF_HEAD_DIM)
    cos_m, sin_m = axial_rope(n_lat, MLA_ROPE)
    cond_ctx = c_ctx[None, :]

    def even_mixer(h, e, lam_init, ctx):
        b, n, _ = h.shape
        proj = h @ w_in_even[e]
        u = proj[..., :S5_WIDTH]
        q = proj[..., S5_WIDTH:S5_WIDTH + DIFF_WIDTH].reshape(b, n, DIFF_HEADS, 2, DIFF_HEAD_DIM)
        k = proj[..., S5_WIDTH + DIFF_WIDTH:S5_WIDTH + 2 * DIFF_WIDTH].reshape(b, n, DIFF_HEADS, 2, DIFF_HEAD_DIM)
        v = proj[..., S5_WIDTH + 2 * DIFF_WIDTH:].reshape(b, n, DIFF_HEADS, 2 * DIFF_HEAD_DIM)
        if ctx is None:
            h0_re = jnp.zeros((b, 2, S5_GROUPS, S5_STATE), jnp.float32)
            h0_im = h0_re
            k_all, v_all = k, v
        else:
            h0_re, h0_im, ck, cv = ctx
            q = apply_rope(q, cos_d, sin_d)
            k = apply_rope(k, cos_d, sin_d)
            k_all = jnp.concatenate([ck.astype(k.dtype), k], axis=1)
            v_all = jnp.concatenate([cv.astype(v.dtype), v], axis=1)
        s5_out, fin_re, fin_im = s5_mixer(u, h0_re, h0_im, s5_lam_re[e], s5_lam_im[e], s5_log_dt[e],
                                          s5_b_re[e], s5_b_im[e], s5_c_re[e], s5_c_im[e], s5_d[e],
                                          s5_glu_w[e], s5_glu_b[e])
        lam = (jnp.exp(jnp.sum(diff_lam_q1[e].astype(jnp.float32) * diff_lam_k1[e].astype(jnp.float32)))
               - jnp.exp(jnp.sum(diff_lam_q2[e].astype(jnp.float32) * diff_lam_k2[e].astype(jnp.float32)))
               + lam_init)
        diff_out = diff_attention(q, k_all, v_all, lam, lam_init, diff_subln_g[e])
        y = jnp.concatenate([s5_out, diff_out], axis=-1) @ w_out_even[e]
        return y, (fin_re, fin_im, k, v)

    def odd_mixer(h, o, ctx):
        b, n, _ = h.shape
        proj = h @ w_in_odd[o]
        cq = rmsnorm(proj[..., :MLA_Q_RANK], mla_q_norm_g[o])
        ckv = rmsnorm(proj[..., MLA_Q_RANK:MLA_Q_RANK + MLA_KV_RANK], mla_kv_norm_g[o])
        kr = proj[..., MLA_Q_RANK + MLA_KV_RANK:]
        q = (cq @ mla_w_q_up[o]).reshape(b, n, MLA_HEADS, MLA_NOPE + MLA_ROPE)
        qn, qr = q[..., :MLA_NOPE], q[..., MLA_NOPE:]
        if ctx is None:
            ckv_all, kr_all = ckv, kr
        else:
            c_ckv, c_kr = ctx
            qr = apply_rope(qr, cos_m, sin_m)
            kr_rot = apply_rope(kr, cos_m, sin_m)
            ckv_all = jnp.concatenate([c_ckv.astype(ckv.dtype), ckv], axis=1)
            kr_all = jnp.concatenate([c_kr.astype(kr.dtype), kr_rot], axis=1)
        s_len = ckv_all.shape[1]
        kv = (ckv_all @ mla_w_kv_up[o]).reshape(b, s_len, MLA_HEADS, MLA_NOPE + MLA_V)
        kn, v = kv[..., :MLA_NOPE], kv[..., MLA_NOPE:]
        y = mla_attention(qn, qr, kn, kr_all, v) @ w_out_odd[o]
        return y, (ckv, kr)

    xp, xs = x_prompt, x_sample
    s5_re_list, s5_im_list, dk_list, dv_list, ckv_list, kr_list = [], [], [], [], [], []
    for l in range(DEPTH):
        common = (w_mod[l], b_mod[l], g_pre_mix[l], g_post_mix[l], g_pre_ffn[l], g_post_ffn[l],
                  w_ffn_gate[l], w_ffn_up[l], w_ffn_down[l])
        if l % 2 == 0:
            e = l // 2
            lam_init = 0.8 - 0.6 * math.exp(-0.3 * l)
            xp, (fr, fi, ck, cv) = block(xp, cond_ctx, *common,
                                         mixer=lambda h: even_mixer(h, e, lam_init, None))
            ctx = (state_s5_re[:, e], state_s5_im[:, e], cache_diff_k[:, e], cache_diff_v[:, e])
            xs, _ = block(xs, c, *common, mixer=lambda h: even_mixer(h, e, lam_init, ctx))
            s5_re_list.append(fr)
            s5_im_list.append(fi)
            dk_list.append(ck)
            dv_list.append(cv)
        else:
            o = l // 2
            xp, (ckv, kr) = block(xp, cond_ctx, *common, mixer=lambda h: odd_mixer(h, o, None))
            ctx = (cache_mla_ckv[:, o], cache_mla_krope[:, o])
            xs, _ = block(xs, c, *common, mixer=lambda h: odd_mixer(h, o, ctx))
            ckv_list.append(ckv)
            kr_list.append(kr)

    new_s5_re = jnp.stack(s5_re_list, axis=1).astype(state_s5_re.dtype)
    new_s5_im = jnp.stack(s5_im_list, axis=1).astype(state_s5_im.dtype)
    new_dk = jnp.stack(dk_list, axis=1)
    new_dv = jnp.stack(dv_list, axis=1)
    new_ckv = jnp.stack(ckv_list, axis=1)
    new_kr = jnp.stack(kr_list, axis=1)
    return (xp, xs, new_s5_re, new_s5_im, new_dk, new_dv, new_ckv, new_kr)
```

```python
import math
import numpy as np
import concourse.bass as bass
import concourse.mybir as mybir
from concourse.bass_utils import run_bass_kernel_spmd

F32 = mybir.dt.float32
BF16 = mybir.dt.bfloat16
AF = mybir.ActivationFunctionType
ALU = mybir.AluOpType
AX = mybir.AxisListType

ENGS = ("pe", "act", "dve", "pool", "sp")


class _Op:
    __slots__ = ("fn", "deps", "dma_key", "sig")

    def __init__(self, fn, deps, dma_key):
        self.fn = fn
        self.deps = deps
        self.dma_key = dma_key
        self.sig = None


class Sched:
    def __init__(self, nc):
        self.nc = nc
        self.ops = {e: [] for e in ENGS}
        self.ncomp = {e: 0 for e in ENGS}
        self.last_w = {}
        self.readers = {}
        self.dma_cnt = {}
        self.dma_keys = []
        self.bar_toks = set()

    def _mk(self, eng, fn, reads, writes, dma_key):
        writes = tuple(writes) + tuple(r for r in reads if r.startswith("PS") and r not in writes)
        deps = set()
        for r in reads:
            w = self.last_w.get(r)
            if w is not None:
                deps.add(w)
        for r in writes:
            w = self.last_w.get(r)
            if w is not None:
                deps.add(w)
            for rd in self.readers.get(r, ()):
                deps.add(rd)
        if eng == "pool" and not all(w.startswith("W") for w in writes):
            deps |= self.bar_toks
        op = _Op(fn, deps, dma_key)
        if dma_key is None:
            self.ncomp[eng] += 1
            tok = ("c", eng, self.ncomp[eng])
        else:
            if dma_key not in self.dma_cnt:
                self.dma_cnt[dma_key] = 0
                self.dma_keys.append(dma_key)
            self.dma_cnt[dma_key] += 16
            tok = ("d", dma_key, self.dma_cnt[dma_key])
        op.sig = tok
        op.deps.discard(tok)
        for r in writes:
            self.last_w[r] = tok
            self.readers[r] = []
        for r in reads:
            self.readers.setdefault(r, []).append(tok)
        self.ops[eng].append((eng, op))
        return op

    def op(self, eng, fn, reads=(), writes=()):
        return self._mk(eng, fn, tuple(reads), tuple(writes), None)

    def dma(self, eng, fn, reads=(), writes=(), key=None):
        assert eng in ("sp", "act", "pool")
        return self._mk(eng, fn, tuple(reads), tuple(writes), key)

    def barrier(self):
        toks = set()
        for e in ENGS:
            if self.ncomp[e] > 0:
                toks.add(("c", e, self.ncomp[e]))
        for k, v in self.dma_cnt.items():
            if not (isinstance(k, str) and k.startswith("W")):
                toks.add(("d", k, v))
        self.bar_toks = toks
        for e in ("pe", "act", "dve", "sp"):
            op = _Op(None, set(toks), None)
            op.sig = None
            self.ops[e].append((e, op))

    def emit(self, final_wait=True):
        nc = self.nc
        import contextlib
        with contextlib.ExitStack() as st:
            csem = {e: st.enter_context(nc.semaphore("s_" + e)) for e in ENGS}
            dsem = {}
            for i, k in enumerate(self.dma_keys):
                dsem[k] = st.enter_context(nc.semaphore("d%d" % i))
            block = st.enter_context(nc.Block())
            engobj = {}

            def run(ename, eng):
                seen_c = {e: 0 for e in ENGS}
                seen_d = {}
                issued = 0
                for (_, op) in self.ops[ename]:
                    need_c = {}
                    need_d = {}
                    for d in op.deps:
                        if d[0] == "c":
                            if d[2] > need_c.get(d[1], 0):
                                need_c[d[1]] = d[2]
                        else:
                            if d[2] > need_d.get(d[1], 0):
                                need_d[d[1]] = d[2]
                    for e2, v in need_c.items():
                        if e2 == ename and ename == "pe":
                            continue
                        if v > seen_c[e2]:
                            eng.wait_ge(csem[e2], v)
                            seen_c[e2] = v
                    for k2, v in need_d.items():
                        if v > seen_d.get(k2, 0):
                            eng.wait_ge(dsem[k2], v)
                            seen_d[k2] = v
                    if op.fn is None:
                        continue
                    ins = op.fn(eng)
                    if op.sig[0] == "c":
                        issued += 1
                        ins.then_inc(csem[ename], 1)
                        seen_c[ename] = max(seen_c[ename], 0)
                    else:
                        ins.then_inc(dsem[op.sig[1]], 16)
                if final_wait and ename == "sp":
                    for k2, v in self.dma_cnt.items():
                        if v > seen_d.get(k2, 0):
                            eng.wait_ge(dsem[k2], v)
                    for e2 in ENGS:
                        if self.ncomp[e2] > seen_c[e2]:
                            eng.wait_ge(csem[e2], self.ncomp[e2])

            @block.tensor
            def _(e):
                run("pe", e)

            @block.scalar
            def _(e):
                run("act", e)

            @block.vector
            def _(e):
                run("dve", e)

            @block.gpsimd
            def _(e):
                run("pool", e)

            @block.sync
            def _(e):
                run("sp", e)


D_MODEL = 1024
NT = 1536
DFF = 2816
NJ = 22
TBS = [(0, 512, 0), (512, 512, 1), (1024, 512, 1)]
EPS = 1e-6
EXPS = [1, 2, 3, 4, 5, 6, 7, 8, 16, 32, 64, 128, 256, 512, 1024]
NV = 384

XT_O = 0
VEC_O = 49152
ONES_O = 51200
SCT_O = 51456
MODT_O = 51520
DER_O = 51904
LAMS_O = 52288
CST_O = 53248
WP_O = 54272
SCR_O = 87040
ARENA_B = 204800

IN_SHAPES = {
    "xT": (1024, 1536), "vecT": (128, NV), "cst": (128, 16), "s5p": (2, 128, 5, 32), "dlam": (2, 4, 64),
    "ropeD": (128, 2, 1024), "ropeM": (32, 2, 1024),
    "w_mod": (4, 1024, 6144), "wg": (4, 1024, 2816), "wu": (4, 1024, 2816), "wd": (4, 2816, 1024),
    "w_in_even": (2, 1024, 2048), "w_in_even_sw": (2, 1024, 1024), "w_out_even": (2, 1024, 1024),
    "glu_w": (2, 512, 512), "s5tab": (2, 4, 128, 4096), "cdkT": (2, 512, 512), "cdv": (2, 512, 512),
    "w_in_odd": (2, 1024, 416), "w_kr_sw": (2, 1024, 32), "wq_up": (2, 256, 1536), "wq_up_sw": (2, 256, 512),
    "wkv_kn": (2, 128, 1024), "wkv_v": (2, 128, 1024), "w_out_odd": (2, 1024, 1024),
    "cckvT": (2, 128, 512), "ckrT": (2, 32, 512),
}
OUT_SHAPES = {
    "yT": (1024, 1536), "ns5": (2, 128, 128), "nkT": (2, 512, 512), "nv": (2, 512, 512),
    "nckvT": (2, 128, 512), "nkrT": (2, 32, 512),
}


def vec_cols():
    cols = {}
    cur = [0]

    def add(name, n):
        cols[name] = (cur[0], n)
        cur[0] += n
    add("cond", 16)
    for l in range(4):
        add("bmod%d" % l, 48)
        add("gpm%d" % l, 8)
        add("gqm%d" % l, 8)
        add("gpf%d" % l, 8)
        add("gqf%d" % l, 8)
    for e in range(2):
        add("s5d%d" % e, 4)
        add("glub%d" % e, 4)
        add("subg%d" % e, 1)
    for o in range(2):
        add("qng%d" % o, 2)
        add("kvg%d" % o, 1)
    assert cur[0] <= NV
    return cols


VC = vec_cols()


class _Stop(Exception):
    pass


STOP_AT = [None]
PHASE_MARKS = []
S5_LIMIT = [None]
S5_DIRS = [(0, 1)]


def _call(method, *args, **kwargs):
    return lambda e: getattr(e, method)(*args, **kwargs)


def build(n_layers=4, taps=()):
    nc = bass.Bass("TRN2", target_bir_lowering=False)
    dram = {}
    for name, shape in IN_SHAPES.items():
        dram[name] = nc.dram_tensor(name, list(shape), F32, kind="ExternalInput")
    for name, shape in OUT_SHAPES.items():
        dram[name] = nc.dram_tensor(name, list(shape), F32, kind="ExternalOutput")
    tapd = {}
    for t in taps:
        tapd[t] = nc.dram_tensor("tap_" + t, [1024, 1536], F32, kind="ExternalOutput")
    import contextlib
    with contextlib.ExitStack() as st:
        arena = st.enter_context(nc.sbuf_tensor("arena", [128, ARENA_B // 2], BF16))
        psum = st.enter_context(nc.psum_tensor("psum", [128, 4096], F32))
        S = Sched(nc)
        _build_body(nc, S, dram, tapd, arena, psum, n_layers)
        S.emit()
    return nc


def _build_body(nc, S, dram, tapd, arena, psum, n_layers):
    def chk(name):
        PHASE_MARKS.append((name, {e: sum(1 for (_, o) in S.ops[e] if o.fn is not None and o.dma_key is None) for e in ("pe", "act", "dve")}))
        if STOP_AT[0] == name:
            raise _Stop()

    def V(off, n, dt=BF16, p0=0, p1=128):
        if dt == BF16:
            return arena[p0:p1, off // 2: off // 2 + n]
        return arena[p0:p1, off // 2: off // 2 + 2 * n].bitcast(F32)

    def r3(ap, a):
        return ap.rearrange("p (a b) -> p a b", a=a)

    def bank(i, n=512):
        return psum[:, i * 512: i * 512 + n]

    def DR(name):
        return dram[name].ap()

    XT = r3(V(XT_O, 8 * NT, F32), 8)
    VEC = V(VEC_O, NV, F32)
    ONES = V(ONES_O, 128)
    SCT = r3(V(SCT_O, 16), 8)
    MODT = r3(V(MODT_O, 96, F32), 48)
    DER = V(DER_O, 96, F32).rearrange("p (k c i) -> p k c i", k=6, c=8)
    LAMS = V(LAMS_O, 16, F32)
    CST = V(CST_O, 16, F32)
    WT = [V(WP_O + i * 8192, 4096) for i in range(4)]
    wcnt = [0]

    def wtile():
        i = wcnt[0] % 4
        wcnt[0] += 1
        return WT[i], "W%d" % i

    def SC(off, n, dt=BF16, p0=0, p1=128):
        return V(SCR_O + off, n, dt, p0, p1)

    def vcol(name, j=0, n=1):
        c0 = VC[name][0] + j
        return VEC[:, c0:c0 + n]

    def wdma(dst, src, wres, reads=()):
        S.dma("pool", _call("dma_start", out=dst, in_=src), reads=reads, writes=[wres], key=wres)

    xsrc = DR("xT").rearrange("(c p) t -> p c t", p=128)
    for c in range(8):
        S.dma("sp" if c % 2 == 0 else "act", _call("dma_start", out=XT[:, c, :], in_=xsrc[:, c, :]),
              writes=["XT%d_%d" % (c, tb) for tb in range(3)], key="xin%d" % c)
    S.dma("sp", _call("dma_start", out=VEC, in_=DR("vecT")), writes=["VEC"], key="vec")
    S.dma("sp", _call("dma_start", out=CST, in_=DR("cst")), writes=["CST"], key="cst")
    S.op("pool", _call("memset", ONES, 1.0), writes=["ONES"])
    S.op("act", _call("activation", SCT.rearrange("p a b -> p (a b)"), VEC[:, 0:16], AF.Silu), reads=["VEC"], writes=["SCT"])

    def xres(c, tb):
        return "XT%d_%d" % (c, tb)

    def rstd_from(psn, rstd, reads, wres, scale):
        S.op("act", _call("activation", rstd, psn, AF.Sqrt, bias=EPS, scale=scale), reads=reads, writes=[wres])
        S.op("dve", _call("reciprocal", rstd, rstd), reads=[wres], writes=[wres])

    def pre_norm(l, which, HT, tmp_off, PSN=7):
        ka, kb = (0, 0) if which == 1 else (3, 24)
        TMPN = r3(SC(tmp_off, 8 * 512, F32), 8)
        RSTD = SC(tmp_off + 16384, 512, F32)
        SQ = [SC(tmp_off + 18432 + i * 1024, 512) for i in range(2)]
        for tb, (t0, tl, ci) in enumerate(TBS):
            for c in range(8):
                sq = SQ[c % 2]
                S.op("act", _call("activation", sq, XT[:, c, t0:t0 + 512], AF.Square),
                     reads=[xres(c, tb)], writes=["SQ%d" % (c % 2)])
                S.op("pe", _call("matmul", bank(PSN), ONES, sq, start=(c == 0), stop=(c == 7)),
                     reads=["SQ%d" % (c % 2), "ONES"], writes=["PS%d" % PSN])
            rstd_from(bank(PSN), RSTD, ["PS%d" % PSN], "RSTD", 1.0 / D_MODEL)
            S.op("dve", _call("tensor_tensor", TMPN, XT[:, :, t0:t0 + 512], RSTD.unsqueeze(1).to_broadcast([128, 8, 512]), ALU.mult),
                 reads=["RSTD"] + [xres(c, tb) for c in range(8)], writes=["TMPN"])
            for c in range(8):
                S.op("act", _call("activation", HT[:, c, t0:t0 + 512], TMPN[:, c, :], AF.Identity,
                                                                       bias=MODT[:, kb + c, ci:ci + 1], scale=DER[:, ka, c, ci:ci + 1]),
                     reads=["TMPN", "MODT", "DER"], writes=["HT%d_%d" % (c, tb)])

    def post_norm_res(l, which, tb, ybuf_off, tmp_off, yfn, PSN=7):
        kg = 2 if which == 1 else 5
        t0, tl, ci = TBS[tb]
        YBUF = r3(SC(ybuf_off, 8 * 512, F32), 8)
        TMPN = r3(SC(tmp_off, 8 * 512, F32), 8)
        RSTD = SC(tmp_off + 16384, 512, F32)
        SQ = [SC(tmp_off + 18432 + i * 1024, 512) for i in range(2)]
        for oc in range(8):
            yp, yres = yfn(oc)
            sq = SQ[oc % 2]
            S.op("act", _call("activation", YBUF[:, oc, :], yp, AF.Identity), reads=[yres], writes=["YBUF%d" % oc])
            S.op("act", _call("activation", sq, yp, AF.Square), reads=[yres], writes=["SQ%d" % (oc % 2)])
            S.op("pe", _call("matmul", bank(PSN), ONES, sq, start=(oc == 0), stop=(oc == 7)),
                 reads=["SQ%d" % (oc % 2), "ONES"], writes=["PS%d" % PSN])
        rstd_from(bank(PSN), RSTD, ["PS%d" % PSN], "RSTD", 1.0 / D_MODEL)
        S.op("dve", _call("tensor_tensor", TMPN, YBUF, RSTD.unsqueeze(1).to_broadcast([128, 8, 512]), ALU.mult),
             reads=["RSTD"] + ["YBUF%d" % c for c in range(8)], writes=["TMPN"])
        for c in range(8):
            S.op("dve", _call("scalar_tensor_tensor", XT[:, c, t0:t0 + 512], TMPN[:, c, :], DER[:, kg, c, ci:ci + 1],
                                                               XT[:, c, t0:t0 + 512], ALU.mult, ALU.add),
                 reads=["TMPN", "DER", xres(c, tb)], writes=[xres(c, tb)])

    def compute_mod(l):
        wsrc = DR("w_mod")[l:l + 1].rearrange("o (kc p) n -> p (o kc) n", p=128)
        PSM = r3(bank(6, 96), 48)
        for wt in range(12):
            w, wr = wtile()
            w3 = r3(w, 8)
            wdma(w3, wsrc[:, :, wt * 512:(wt + 1) * 512], wr)
            for fi in range(4):
                f = wt * 4 + fi
                for kc in range(8):
                    S.op("pe", _call("matmul", PSM[:, f, :], w3[:, kc, fi * 128:(fi + 1) * 128], SCT[:, kc, :],
                                                                            start=(kc == 0), stop=(kc == 7)),
                         reads=[wr, "SCT"], writes=["PS6"])
        b0 = VC["bmod%d" % l][0]
        S.op("dve", _call("tensor_tensor", MODT, PSM, VEC[:, b0:b0 + 48].unsqueeze(2).to_broadcast([128, 48, 2]), ALU.add),
             reads=["PS6", "VEC"], writes=["MODT"])
        for k, (sc0, gname) in enumerate([(8, "gpm"), (16, "gqm"), (32, "gpf"), (40, "gqf")]):
            kk = [0, 2, 3, 5][k]
            g0 = VC["%s%d" % (gname, l)][0]
            gb = VEC[:, g0:g0 + 8].unsqueeze(2).to_broadcast([128, 8, 2])
            if k in (0, 2):
                S.op("dve", _call("tensor_scalar", DER[:, kk], MODT[:, sc0:sc0 + 8, :], 1.0, None, ALU.add),
                     reads=["MODT"], writes=["DER"])
                S.op("dve", _call("tensor_tensor", DER[:, kk], DER[:, kk], gb, ALU.mult), reads=["DER", "VEC"], writes=["DER"])
            else:
                S.op("dve", _call("tensor_tensor", DER[:, kk], MODT[:, sc0:sc0 + 8, :], gb, ALU.mult),
                     reads=["MODT", "VEC"], writes=["DER"])

    def tap_f32(name, src3, reads):
        if name in tapd:
            dst = tapd[name].ap().rearrange("(c p) t -> p c t", p=128)
            S.dma("sp", _call("dma_start", out=dst, in_=src3), reads=reads, key="tap")

    def tap_bf16(name, src3, reads, nchunk=8, c0=0):
        if name in tapd:
            dst = tapd[name].ap().rearrange("(c p) t -> p c t", p=128)[:, c0:c0 + nchunk, :]
            S.dma("pool", _call("dma_start", out=dst, in_=src3), reads=reads, key="tap")

    def attn_pipeline(tasks, epilogues, PT, scale, SK=2):
        n = len(tasks)
        deferred = []
        for idx in range(n + SK + 4):
            if idx < n:
                g, ki, nk, N, A, C = tasks[idx][:6]
                if len(tasks[idx]) > 6 and tasks[idx][6] is not None:
                    tasks[idx][6]()
                ps = idx % 3
                pt, ptr = PT[idx % 4], "PT%d" % (idx % 4)
                A(ps)
                S.op("act", _call("activation", pt[:, 0:N], bank(ps, N), AF.Exp, scale=scale), reads=["PS%d" % ps], writes=[ptr])
            jx = idx - SK
            if 0 <= jx < n:
                g, ki, nk, N, A, C = tasks[jx][:6]
                ob, sb = 3 + 2 * (g % 2), 4 + 2 * (g % 2)
                C(PT[jx % 4], "PT%d" % (jx % 4), ob, sb, ki == 0, ki == nk - 1)
                if ki == nk - 1:
                    d2 = epilogues[g](ob, sb, N)
                    if d2 is not None:
                        deferred.append((idx + 3, d2))
            for (due, fn) in [x for x in deferred if x[0] <= idx]:
                fn()
            deferred = [x for x in deferred if x[0] > idx]
        for (due, fn) in deferred:
            fn()

    def ffn(l):
        HT = r3(SC(0, 8 * NT), 8)
        ACTT = r3(SC(24576, NJ * NT), NJ)
        SL = [SC(92160 + i * 2048, 512, F32) for i in range(2)]
        pre_norm(l, 2, HT, 96256)
        gsrc = DR("wg")[l:l + 1].rearrange("o (kc p) n -> p (o kc) n", p=128)
        usrc = DR("wu")[l:l + 1].rearrange("o (kc p) n -> p (o kc) n", p=128)
        it = 0
        for jg in range(6):
            nj = 4 if jg < 5 else 2
            wgt, wgr = wtile()
            wut, wur = wtile()
            wg3 = r3(wgt, 8)[:, :, 0:nj * 128]
            wu3 = r3(wut, 8)[:, :, 0:nj * 128]
            wdma(wg3, gsrc[:, :, jg * 512: jg * 512 + nj * 128], wgr)
            wdma(wu3, usrc[:, :, jg * 512: jg * 512 + nj * 128], wur)
            for ji in range(nj):
                j = jg * 4 + ji
                for tb, (t0, tl, ci) in enumerate(TBS):
                    pg, pu = (it % 2) * 2, (it % 2) * 2 + 1
                    sl = SL[it % 2]
                    it += 1
                    for kc in range(8):
                        S.op("pe", _call("matmul", bank(pg), wg3[:, kc, ji * 128:(ji + 1) * 128], HT[:, kc, t0:t0 + 512],
                                                                                         start=(kc == 0), stop=(kc == 7)),
                             reads=[wgr, "HT%d_%d" % (kc, tb)], writes=["PS%d" % pg])
                    for kc in range(8):
                        S.op("pe", _call("matmul", bank(pu), wu3[:, kc, ji * 128:(ji + 1) * 128], HT[:, kc, t0:t0 + 512],
                                                                                         start=(kc == 0), stop=(kc == 7)),
                             reads=[wur, "HT%d_%d" % (kc, tb)], writes=["PS%d" % pu])
                    S.op("act", _call("activation", sl, bank(pg), AF.Silu), reads=["PS%d" % pg], writes=["SL%d" % (it % 2)])
                    S.op("dve", _call("tensor_tensor", ACTT[:, j, t0:t0 + 512], sl, bank(pu), ALU.mult),
                         reads=["SL%d" % (it % 2), "PS%d" % pu], writes=["ACT%d_%d" % (j, tb)])
        S.barrier()
        chk("ffn_gateup")
        dsrc = DR("wd")[l:l + 1].rearrange("o (j p) n -> p (o j) n", p=128)
        for tb, (t0, tl, ci) in enumerate(TBS):
            state = {}

            def yfn(oc, tb=tb, t0=t0, state=state):
                half, oi = oc // 4, oc % 4
                if oi == 0:
                    for jt in range(3):
                        njj = 8 if jt < 2 else 6
                        w, wr = wtile()
                        w3 = r3(w, 8)[:, 0:njj, :]
                        wdma(w3, dsrc[:, jt * 8: jt * 8 + njj, half * 512:(half + 1) * 512], wr)
                        for jj in range(njj):
                            j = jt * 8 + jj
                            for o2 in range(4):
                                S.op("pe", _call("matmul", bank(o2), w3[:, jj, o2 * 128:(o2 + 1) * 128], ACTT[:, j, t0:t0 + 512],
                                                                                      start=(j == 0), stop=(j == NJ - 1)),
                                     reads=[wr, "ACT%d_%d" % (j, tb)], writes=["PS%d" % o2])
                return bank(oi), "PS%d" % oi
            post_norm_res(l, 2, tb, 0, 96256, yfn)
        S.barrier()
        chk("ffn_down")

    def even_mixer(l):
        e_ = l // 2
        lam_init = 0.8 - 0.6 * math.exp(-0.3 * l)
        HT = r3(SC(0, 8 * NT), 8)
        GT = r3(SC(24576, 4 * NT), 4)
        UT = r3(SC(36864, 4 * NT), 4)
        pre_norm(l, 1, HT, 49152)
        tap_bf16("h1_%d" % l, HT, ["HT%d_%d" % (c, tb) for c in range(8) for tb in range(3)])
        S.barrier()
        chk("prenorm")
        wsrc = DR("w_in_even")[e_:e_ + 1].rearrange("o (kc p) n -> p (o kc) n", p=128)
        wsw = DR("w_in_even_sw")[e_:e_ + 1].rearrange("o (kc p) n -> p (o kc) n", p=128)

        def proj_fm(w3, wr, fc, tb, pb):
            t0 = TBS[tb][0]
            for kc in range(8):
                S.op("pe", _call("matmul", bank(pb), w3[:, kc, fc * 128:(fc + 1) * 128], HT[:, kc, t0:t0 + 512], start=(kc == 0), stop=(kc == 7)),
                     reads=[wr, "HT%d_%d" % (kc, tb)], writes=["PS%d" % pb])

        w, wr = wtile()
        w3 = r3(w, 8)
        wdma(w3, wsrc[:, :, 0:512], wr)
        it = 0
        for fc in range(4):
            for tb in range(3):
                pb = it % 2
                it += 1
                proj_fm(w3, wr, fc, tb, pb)
                t0 = TBS[tb][0]
                S.op("act", _call("activation", UT[:, fc, t0:t0 + 512], bank(pb), AF.Identity),
                     reads=["PS%d" % pb], writes=["UT%d_%d" % (fc, tb)])
        chk("uproj")
        s5(l, e_, UT, GT)
        S.barrier()
        tap_bf16("ut_%d" % l, UT, [], 4, 0)
        tap_bf16("gt_%d" % l, GT, [], 4, 0)
        tap_bf16("hb16_%d" % l, r3(SC(61440, 2 * NT), 2), [], 2, 0)
        chk("s5")
        S5OUT = UT
        w, wr = wtile()
        w3 = r3(w, 8)[:, 0:4, :]
        wdma(w3, DR("glu_w")[e_:e_ + 1].rearrange("o (kc p) n -> p (o kc) n", p=128), wr)
        SG = [SC(49152 + i * 2048, 512, F32) for i in range(2)]
        it = 0
        for fo in range(4):
            for tb in range(3):
                t0 = TBS[tb][0]
                pb = it % 2
                sg = SG[it % 2]
                it += 1
                for kc in range(4):
                    S.op("pe", _call("matmul", bank(pb), w3[:, kc, fo * 128:(fo + 1) * 128], GT[:, kc, t0:t0 + 512], start=(kc == 0), stop=(kc == 3)),
                         reads=[wr] + ["GT%d_%d" % (kc, tb)], writes=["PS%d" % pb])
                S.op("act", _call("activation", sg, bank(pb), AF.Sigmoid, bias=vcol("glub%d" % e_, fo), scale=1.0),
                     reads=["PS%d" % pb, "VEC"], writes=["SG%d" % (it % 2)])
                S.op("dve", _call("tensor_tensor", S5OUT[:, fo, t0:t0 + 512], sg, GT[:, fo, t0:t0 + 512], ALU.mult),
                     reads=["SG%d" % (it % 2), "GT%d_%d" % (fo, tb)], writes=["S5O%d_%d" % (fo, tb)])
        S.barrier()
        tap_bf16("s5out_%d" % l, S5OUT, ["S5O%d_%d" % (c, tb) for c in range(4) for tb in range(3)], 4, 0)
        chk("glu")
        QT = r3(SC(49152, 4 * NT), 4)
        KT = r3(SC(61440, 4 * 2048), 4)
        VT = r3(SC(77824, 16 * 512), 16)
        ROPE = r3(SC(94208, 2 * 1024, F32), 2)
        STG = SC(104448, 512, F32)
        T1 = SC(106496, 512, F32)
        T2 = SC(108544, 512, F32)
        S.dma("sp", _call("dma_start", out=ROPE, in_=DR("ropeD")), writes=["ROPE"], key="rope")
        S.dma("pool", _call("dma_start", out=KT[:, :, 512:1024], in_=DR("cdkT")[e_:e_ + 1].rearrange("o (c p) t -> p (o c) t", p=128)),
              writes=["KTc"], key="KTc")
        S.dma("pool", _call("dma_start", out=VT[:, 4:8, :], in_=DR("cdv")[e_:e_ + 1].rearrange("o (t p) f -> p (o t) f", p=128)),
              writes=["VTc"], key="VTc")
        nk_dst = DR("nkT")[e_:e_ + 1].rearrange("o (c p) t -> p (o c) t", p=128)
        for which in (0, 1):
            w, wr = wtile()
            w3 = r3(w, 8)
            wdma(w3, wsrc[:, :, 512 * (1 + which): 512 * (2 + which)], wr)
            ws_, wsr = wtile()
            ws3 = r3(ws_, 8)
            wdma(ws3, wsw[:, :, 512 * which: 512 * (which + 1)], wsr)
            for fc in range(4):
                for tb in range(3):
                    t0 = TBS[tb][0]
                    proj_fm(w3, wr, fc, tb, 0)
                    if tb == 0:
                        if which == 0:
                            S.op("act", _call("activation", QT[:, fc, 0:512], bank(0), AF.Identity), reads=["PS0"], writes=["QT%d_0" % fc])
                        else:
                            S.op("act", _call("activation", KT[:, fc, 0:512], bank(0), AF.Identity), reads=["PS0"], writes=["KT%d_0" % fc])
                            S.op("dve", _call("tensor_copy", STG, bank(0)), reads=["PS0"], writes=["STG"])
                            S.dma("sp", _call("dma_start", out=nk_dst[:, fc, :], in_=STG), reads=["STG"], key="oSTG")
                    else:
                        proj_fm(ws3, wsr, fc, tb, 1)
                        r0 = t0 - 512
                        S.op("dve", _call("tensor_tensor", T1, bank(0), ROPE[:, 0, r0:r0 + 512], ALU.mult), reads=["PS0", "ROPE"], writes=["T1"])
                        S.op("dve", _call("tensor_tensor", T2, bank(1), ROPE[:, 1, r0:r0 + 512], ALU.mult), reads=["PS1", "ROPE"], writes=["T2"])
                        if which == 0:
                            dst, dres = QT[:, fc, t0:t0 + 512], "QT%d_%d" % (fc, tb)
                        else:
                            dst, dres = KT[:, fc, 512 + t0: 512 + t0 + 512], "KT%d_%d" % (fc, tb)
                        S.op("dve", _call("tensor_tensor", dst, T1, T2, ALU.add), reads=["T1", "T2"], writes=[dres])
        w, wr = wtile()
        w3 = r3(w, 8)
        wdma(w3, wsrc[:, :, 1536:2048], wr)
        nv_dst = DR("nv")[e_:e_ + 1].rearrange("o (t p) f -> p (o t) f", p=128)
        for tt in range(12):
            pb = tt % 2
            vt_i = tt if tt < 4 else tt + 4
            for kc in range(8):
                S.op("pe", _call("matmul", bank(pb), HT[:, kc, tt * 128:(tt + 1) * 128], w3[:, kc, :], start=(kc == 0), stop=(kc == 7)),
                     reads=[wr, "HT%d_%d" % (kc, tt // 4)], writes=["PS%d" % pb])
            S.op("act", _call("activation", VT[:, vt_i, :], bank(pb), AF.Identity), reads=["PS%d" % pb], writes=["VT%d" % vt_i])
            if tt < 4:
                S.op("dve", _call("tensor_copy", STG, bank(pb)), reads=["PS%d" % pb], writes=["STG"])
                S.dma("sp", _call("dma_start", out=nv_dst[:, tt, :], in_=STG), reads=["STG"], key="oSTG")
        chk("qkv")
        DL = r3(SC(110592, 256, F32), 4)
        DP = r3(SC(111616, 128, F32), 2)
        S.dma("sp", _call("dma_start", out=DL, in_=DR("dlam")[e_:e_ + 1].rearrange("o a b -> (o a) b").partition_broadcast(128)), writes=["DL"], key="dl")
        S.op("dve", _call("tensor_tensor", DP[:, 0, :], DL[:, 0, :], DL[:, 1, :], ALU.mult), reads=["DL"], writes=["DP"])
        S.op("dve", _call("tensor_tensor", DP[:, 1, :], DL[:, 2, :], DL[:, 3, :], ALU.mult), reads=["DL", "DP"], writes=["DP"])
        S.op("dve", _call("reduce_sum", LAMS[:, 0:2], DP, AX.X), reads=["DP"], writes=["LAMS"])
        S.op("act", _call("activation", LAMS[:, 2:4], LAMS[:, 0:2], AF.Exp), reads=["LAMS"], writes=["LAMS"])
        S.op("dve", _call("tensor_tensor", LAMS[:, 4:5], LAMS[:, 3:4], LAMS[:, 2:3], ALU.subtract), reads=["LAMS"], writes=["LAMS"])
        S.op("dve", _call("tensor_scalar", LAMS[:, 4:5], LAMS[:, 4:5], -lam_init, None, ALU.add), reads=["LAMS"], writes=["LAMS"])
        S.op("dve", _call("tensor_scalar", LAMS[:, 5:6], vcol("subg%d" % e_), 1.0 - lam_init, None, ALU.mult), reads=["VEC", "LAMS"], writes=["LAMS"])
        QZ = [r3(SC(0, 4 * NT), 4), r3(SC(12288, 4 * NT), 4)]
        allht = ["HT%d_%d" % (c_, t_) for c_ in range(8) for t_ in range(3)]
        allqt = ["QT%d_%d" % (c_, t_) for c_ in range(4) for t_ in range(3)]
        S.op("pool", _call("memset", QZ[0][64:128], 0.0), writes=allht + ["QZ0"])
        S.op("pool", _call("memset", QZ[1][0:64], 0.0), writes=allht + ["QZ1"])
        S.op("act", _call("activation", QZ[0][0:64], QT[0:64], AF.Identity), reads=allqt, writes=allht + ["QZ0"])
        S.op("dve", _call("tensor_copy", QZ[1][64:128], QT[64:128]), reads=allqt, writes=allht + ["QZ1"])
        S.barrier()
        OT = GT
        PT = [SC(98304 + i * 1024, 512) for i in range(4)]
        REC = SC(102400, 512, F32)
        REC2 = SC(110592, 512, F32)
        OC = [T1, T2]
        OO = STG
        SQ = SC(112640, 512)
        seqs = [(0, 256, [0, 1], 0), (256, 256, [2, 3], 256), (512, 1024, list(range(4, 16)), 512)]
        tasks, epis = [], []
        for (q0, qlen, vtiles, k0) in seqs:
            nqb = max(1, qlen // 512)
            N = min(qlen, 512)
            for h in range(4):
                for qb in range(nqb):
                    qs = q0 + qb * 512
                    tbq = 0 if q0 < 512 else 1 + qb
                    for c in range(2):
                        p0, p1 = 64 * c, 64 * c + 64
                        g = len(epis)
                        for ki, vt_i in enumerate(vtiles):
                            kc0 = k0 + ki * 128

                            def A(ps, h=h, kc0=kc0, qs=qs, c=c, N=N):
                                S.op("pe", _call("matmul", bank(ps, N), KT[:, h, kc0:kc0 + 128], QZ[c][:, h, qs:qs + N], start=True, stop=True),
                                     reads=["KTc", "QZ%d" % c] + ["KT%d_%d" % (h, t) for t in range(3)], writes=["PS%d" % ps])

                            def C(pt, ptr, ob, sb, first, last, vt_i=vt_i, h=h, N=N):
                                S.op("pe", _call("matmul", bank(ob, N), VT[:, vt_i, h * 128:(h + 1) * 128], pt[:, 0:N], start=first, stop=last),
                                     reads=[ptr, "VT%d" % vt_i, "VTc"], writes=["PS%d" % ob])
                                S.op("pe", _call("matmul", bank(sb, N), ONES, pt[:, 0:N], start=first, stop=last),
                                     reads=[ptr, "ONES"], writes=["PS%d" % sb])
                            tasks.append((g, ki, len(vtiles), N, A, C))

                        def E(ob, sb, N, c=c, h=h, qs=qs):
                            S.op("dve", _call("reciprocal", REC[:, 0:N], bank(sb, N)), reads=["PS%d" % sb], writes=["REC"])
                            S.op("dve", _call("tensor_tensor", OC[c][:, 0:N], bank(ob, N), REC[:, 0:N], ALU.mult), reads=["PS%d" % ob, "REC"], writes=["OC%d" % c])
                            if c == 0:
                                return None
                            S.op("dve", _call("scalar_tensor_tensor", OO[:, 0:N], OC[1][:, 0:N], LAMS[:, 4:5], OC[0][:, 0:N], ALU.mult, ALU.add),
                                 reads=["OC0", "OC1", "LAMS"], writes=["OO"])
                            S.op("act", _call("activation", SQ[:, 0:N], OO[:, 0:N], AF.Square), reads=["OO"], writes=["SQa"])

                            def E2():
                                S.op("pe", _call("matmul", bank(7, N), ONES, SQ[:, 0:N], start=True, stop=True), reads=["SQa", "ONES"], writes=["PS7"])
                                S.op("act", _call("activation", REC2[:, 0:N], bank(7, N), AF.Sqrt, bias=EPS, scale=1.0 / 128), reads=["PS7"], writes=["REC2"])
                                S.op("dve", _call("reciprocal", REC2[:, 0:N], REC2[:, 0:N]), reads=["REC2"], writes=["REC2"])
                                S.op("dve", _call("tensor_tensor", OO[:, 0:N], OO[:, 0:N], REC2[:, 0:N], ALU.mult), reads=["OO", "REC2"], writes=["OO"])
                                S.op("act", _call("activation", OT[:, h, qs:qs + N], OO[:, 0:N], AF.Identity, scale=LAMS[:, 5:6]),
                                     reads=["OO", "LAMS"], writes=["OT%d" % h])
                            return E2
                        epis.append(E)
        attn_pipeline(tasks, epis, PT, 0.125)
        S.barrier()
        tap_bf16("diffout_%d" % l, OT, ["OT%d" % h for h in range(4)], 4, 4)
        chk("attn")
        osrc = DR("w_out_even")[e_:e_ + 1].rearrange("o (kc p) n -> p (o kc) n", p=128)
        wts = []
        for half in range(2):
            w, wr = wtile()
            w3 = r3(w, 8)
            wdma(w3, osrc[:, :, half * 512:(half + 1) * 512], wr)
            wts.append((w3, wr))
        for tb, (t0, tl, ci) in enumerate(TBS):
            def yfn(oc, tb=tb, t0=t0):
                w3, wr = wts[oc // 4]
                oi = oc % 4
                pb = oc % 2
                for kc in range(8):
                    src = S5OUT[:, kc, t0:t0 + 512] if kc < 4 else OT[:, kc - 4, t0:t0 + 512]
                    S.op("pe", _call("matmul", bank(pb), w3[:, kc, oi * 128:(oi + 1) * 128], src, start=(kc == 0), stop=(kc == 7)),
                         reads=[wr], writes=["PS%d" % pb])
                return bank(pb), "PS%d" % pb
            post_norm_res(l, 1, tb, 49152, 65536, yfn)
        S.barrier()

    def s5(l, e_, UT, GT):
        SLOT = 25856
        ENG = ["dve", "dve"]

        def slot_bufs(s_):
            o = 49152 + s_ * SLOT
            return dict(
                HB=r3(SC(o, 2 * NT, F32), 2), HB16=r3(SC(o + 12288, 2 * NT), 2),
                EA=r3(SC(o + 18432, 2 * 323, F32), 2), EB=r3(SC(o + 21016, 2 * 323, F32), 2),
                XA=r3(SC(o + 23600, 2 * 192, F32), 2),
                TD=SC(o + 12288, 512, F32), TC=SC(o + 14336, 512, F32))
        SB = [slot_bufs(0), slot_bufs(1)]
        sh0 = 49152 + 2 * SLOT
        PW = SC(sh0, 3 * 15 * 32, F32).rearrange("p (a k j) -> p a k j", a=3, k=15)
        S5P = r3(SC(sh0 + 5760, 5 * 32, F32), 5)
        FF = r3(SC(sh0 + 6400, 3 * 32, F32), 3)
        FIN = SC(sh0 + 6784, 128, F32).rearrange("p (d c s q) -> p d c s q", d=2, c=2, s=2)
        GTMP = [SC(sh0 + 7296 + i * 2048, 512, F32) for i in range(3)]
        TM = r3(SC(49152, 6 * 15 * 32, F32), 6)
        S.dma("sp", _call("dma_start", out=S5P, in_=DR("s5p")[e_:e_ + 1].rearrange("o p a j -> p (o a) j")), writes=["S5P"], key="s5p")
        lr, li, ldt = S5P[:, 0, :], S5P[:, 1, :], S5P[:, 2, :]
        tm = [TM[:, i, :].rearrange("p (k j) -> p k j", k=15) for i in range(6)]
        sm = [TM[:, i, 0:32] for i in range(6)]
        D_ = "S5C"

        def dv(fn, eng="dve"):
            S.op(eng, fn, reads=[D_, "S5P", "CST"], writes=[D_])
        dv(_call("activation", sm[0], ldt, AF.Exp), "act")
        dv(_call("tensor_tensor", sm[1], lr, sm[0], ALU.mult))
        dv(_call("tensor_tensor", sm[2], li, sm[0], ALU.mult))
        exb = CST[:, 0:15].unsqueeze(2).to_broadcast([128, 15, 32])
        dv(_call("tensor_tensor", tm[3], sm[1].unsqueeze(1).to_broadcast([128, 15, 32]), exb, ALU.mult))
        dv(_call("activation", tm[3], tm[3], AF.Exp), "act")
        dv(_call("tensor_tensor", tm[4], sm[2].unsqueeze(1).to_broadcast([128, 15, 32]), exb, ALU.mult))

        def sin_of(dst, shift):
            dv(_call("tensor_scalar", tm[5], tm[4], shift, 1.0 / (2 * math.pi), ALU.add, ALU.mult))
            ki = TM[:, 0, :].bitcast(mybir.dt.int32).rearrange("p (k j) -> p k j", k=15)
            dv(_call("tensor_copy", ki, tm[5]))
            dv(_call("tensor_copy", tm[5], ki))
            dv(_call("tensor_scalar", dst, tm[4], shift, None, ALU.add))
            dv(_call("scalar_tensor_tensor", dst, tm[5], -2 * math.pi, dst, ALU.mult, ALU.add))
            dv(_call("tensor_scalar", tm[5], dst, -math.pi, 2 * math.pi, ALU.is_lt, ALU.mult))
            dv(_call("tensor_tensor", dst, dst, tm[5], ALU.add))
            dv(_call("tensor_scalar", tm[5], dst, math.pi, -2 * math.pi, ALU.is_gt, ALU.mult))
            dv(_call("tensor_tensor", dst, dst, tm[5], ALU.add))
            dv(_call("activation", dst, dst, AF.Sin), "act")
        sin_of(tm[1], 0.0)
        sin_of(tm[2], math.pi / 2)
        dv(_call("tensor_tensor", PW[:, 0], tm[3], tm[2], ALU.mult))
        dv(_call("tensor_tensor", PW[:, 1], tm[3], tm[1], ALU.mult))
        dv(_call("tensor_scalar", PW[:, 2], PW[:, 1], -1.0, None, ALU.mult))
        are, aim = PW[:, 0, 0, :], PW[:, 1, 0, :]
        s0, s1, s2, s3, s4 = [TM[:, 0, 32 * i:32 * i + 32] for i in range(5)]
        dv(_call("tensor_scalar", s0, are, -1.0, None, ALU.add))
        dv(_call("tensor_tensor", s1, lr, lr, ALU.mult))
        dv(_call("tensor_tensor", s2, li, li, ALU.mult))
        dv(_call("tensor_tensor", s1, s1, s2, ALU.add))
        dv(_call("reciprocal", s1, s1))
        dv(_call("tensor_tensor", s2, s0, lr, ALU.mult))
        dv(_call("tensor_tensor", s3, aim, li, ALU.mult))
        dv(_call("tensor_tensor", s2, s2, s3, ALU.add))
        dv(_call("tensor_tensor", FF[:, 0, :], s2, s1, ALU.mult))
        dv(_call("tensor_tensor", s2, aim, lr, ALU.mult))
        dv(_call("tensor_tensor", s3, s0, li, ALU.mult))
        dv(_call("tensor_tensor", s2, s2, s3, ALU.subtract))
        dv(_call("tensor_tensor", FF[:, 1, :], s2, s1, ALU.mult))
        dv(_call("tensor_scalar", FF[:, 2, :], FF[:, 1, :], -1.0, None, ALU.mult))
        S.barrier()

        def pw(a, k, j):
            return PW[:, a, k, j:j + 1]

        def rm(ap512):
            return ap512.rearrange("p (r m) -> p r m", r=8)

        for s_i in range(2):
            S.op("dve", _call("memset", SB[s_i]["EA"], 0.0), writes=["EA_%d" % s_i])
            S.op("dve", _call("memset", SB[s_i]["EB"], 0.0), writes=["EB_%d" % s_i])
        tabsrc = DR("s5tab")[e_:e_ + 1]
        tabs = []
        for c in range(4):
            w, wr = wtile()
            wdma(w, tabsrc[:, c].rearrange("o p n -> p (o n)"), wr)
            tabs.append((w.rearrange("p (d q m n) -> p d q m n", d=2, q=4, m=4), wr))
        jobs = [(c, qq, d) for c in range(4) for qq in range(4) for d in range(2)]
        psb = [0]

        def rec_bu(n):
            c, qq, d = jobs[n]
            tab, wr = tabs[c]
            s_ = n % 2
            B_ = SB[s_]
            eng = ENG[s_]
            j = d * 16 + 4 * c + qq
            hbr = "HB_%d" % s_
            for tb, (t0, tl, ci) in enumerate(TBS):
                pr, pi = 4 + (psb[0] % 2) * 2, 5 + (psb[0] % 2) * 2
                psb[0] += 1
                m0 = t0 // 8
                u_rm = UT[:, c, t0:t0 + 512].rearrange("p (m r) -> p r m", r=8)
                HBr = B_["HB"].rearrange("p c (r m) -> p c r m", r=8)
                S.op("pe", _call("matmul", rm(bank(pr)), tab[:, d, qq, 0, :], u_rm, start=True, stop=True),
                     reads=[wr, "UT%d_%d" % (c, tb)], writes=["PS%d" % pr])
                S.op("pe", _call("matmul", rm(bank(pi)), tab[:, d, qq, 1, :], u_rm, start=True, stop=True),
                     reads=[wr, "UT%d_%d" % (c, tb)], writes=["PS%d" % pi])
                S.op("act", _call("activation", HBr[:, 0, :, m0:m0 + 64], rm(bank(pr)), AF.Identity, scale=FF[:, 0, j:j + 1]), reads=["PS%d" % pr, D_], writes=[hbr])
                S.op("act", _call("activation", B_["TD"], bank(pi), AF.Identity, scale=FF[:, 2, j:j + 1]), reads=["PS%d" % pi, D_], writes=["TD_%d" % s_, "HB16_%d" % s_])
                S.op(eng, _call("tensor_tensor", HBr[:, 0, :, m0:m0 + 64], HBr[:, 0, :, m0:m0 + 64], rm(B_["TD"]), ALU.add), reads=[hbr, "TD_%d" % s_], writes=[hbr])
                S.op("act", _call("activation", HBr[:, 1, :, m0:m0 + 64], rm(bank(pi)), AF.Identity, scale=FF[:, 0, j:j + 1]), reads=["PS%d" % pi, D_], writes=[hbr])
                S.op("act", _call("activation", B_["TC"], bank(pr), AF.Identity, scale=FF[:, 1, j:j + 1]), reads=["PS%d" % pr, D_], writes=["TC_%d" % s_, "HB16_%d" % s_])
                S.op(eng, _call("tensor_tensor", HBr[:, 1, :, m0:m0 + 64], HBr[:, 1, :, m0:m0 + 64], rm(B_["TC"]), ALU.add), reads=[hbr, "TC_%d" % s_], writes=[hbr])

        def rec_scan(n):
            c, qq, d = jobs[n]
            q = 4 * c + qq
            s_ = n % 2
            B_ = SB[s_]
            eng = ENG[s_]
            j = d * 16 + q
            HB, HB16, EA, EB = B_["HB"], B_["HB16"], B_["EA"], B_["EB"]
            hbr, h16r, ear, ebr = "HB_%d" % s_, "HB16_%d" % s_, "EA_%d" % s_, "EB_%d" % s_
            ops = []

            def EM(fn, reads=(), writes=()):
                ops.append((fn, reads, writes))
            HBr = HB.rearrange("p c (r m) -> p c r m", r=8)
            H16r = HB16.rearrange("p c (r m) -> p c r m", r=8)
            h16 = HB16.rearrange("p c (m r) -> p c m r", r=8)
            hvP = HB[:, :, 0:512].rearrange("p c (s m r) -> p c s m r", s=2, r=8)
            h16P = HB16[:, :, 0:512].rearrange("p c (s m r) -> p c s m r", s=2, r=8)

            def cm(dst2, src2, k, reads, writes, out2=None, neg_im=False):
                are_, aim_, nim_ = pw(0, k, j), pw(1, k, j), pw(2, k, j)
                o2 = dst2 if out2 is None else out2
                if len(dst2.shape) <= 3:
                    EM(_call("scalar_tensor_tensor", dst2, src2, are_, dst2, ALU.mult, ALU.add), reads=reads, writes=writes)
                else:
                    EM(_call("scalar_tensor_tensor", dst2[:, 0], src2[:, 0], are_, dst2[:, 0], ALU.mult, ALU.add), reads=reads, writes=writes)
                    EM(_call("scalar_tensor_tensor", dst2[:, 1], src2[:, 1], are_, dst2[:, 1], ALU.mult, ALU.add), reads=reads, writes=writes)
                EM(_call("scalar_tensor_tensor", o2[:, 0], src2[:, 1], nim_, dst2[:, 0], ALU.mult, ALU.add), reads=reads, writes=writes)
                if neg_im:
                    EM(_call("scalar_tensor_tensor", o2[:, 1], src2[:, 0], nim_, dst2[:, 1], ALU.mult, ALU.subtract), reads=reads, writes=writes)
                else:
                    EM(_call("scalar_tensor_tensor", o2[:, 1], src2[:, 0], aim_, dst2[:, 1], ALU.mult, ALU.add), reads=reads, writes=writes)

            def cp(dst, src, reads, writes):
                if len(dst.shape) <= 3:
                    EM(_call("tensor_copy", dst, src), reads=reads, writes=writes)
                else:
                    for cc_ in range(2):
                        EM(_call("tensor_copy", dst[:, cc_], src[:, cc_]), reads=reads, writes=writes)
            order = range(1, 8) if d == 0 else range(6, -1, -1)
            for r in order:
                rsrc = r - 1 if d == 0 else r + 1
                cm(HBr[:, :, r, :], HBr[:, :, rsrc, :], 0, [hbr, D_], [hbr])
            rend = 7 if d == 0 else 0

            def PV(buf):
                if d == 0:
                    return buf[:, :, 0:130].rearrange("p c (s n) -> p c s n", s=2), 32
                return buf[:, :, 32:162].rearrange("p c (s n) -> p c s n", s=2), 0
            EAP, n0 = PV(EA)
            EBP, _n = PV(EB)
            S0 = 162
            eofs = 1 if d == 0 else 0
            hidx = 0 if d == 0 else 32
            cp(EAP[:, :, :, n0 + eofs:n0 + eofs + 32], HBr[:, :, rend, 0:64].rearrange("p c (s m) -> p c s m", s=2), [hbr], [ear])
            for cc_ in range(2):
                EM(_call("memset", EAP[:, cc_, :, n0 + hidx:n0 + hidx + 1], 0.0), writes=[ear])
            EM(_call("tensor_copy", EA[:, :, S0 + eofs:S0 + eofs + 128], HBr[:, :, rend, 64:192]), reads=[hbr], writes=[ear])
            hcol = S0 if d == 0 else S0 + 128
            EM(_call("tensor_copy", EA[:, :, hcol:hcol + 1], S5P[:, 3:5, j:j + 1]), reads=["S5P"], writes=[ear])
            bufs = [(EA, EAP, ear), (EB, EBP, ebr)]
            sgn = -1 if d == 0 else 1
            for lev in range(8):
                sh = 1 << lev
                (bi, biP, bir), (bo, boP, bor) = bufs[lev % 2], bufs[(lev + 1) % 2]
                k = 7 + lev
                are_, aim_, nim_ = pw(0, k, j), pw(1, k, j), pw(2, k, j)
                groups = []
                if sh <= 32:
                    so = n0 + sgn * sh
                    groups.append((boP[:, :, :, n0:n0 + 33], biP[:, :, :, n0:n0 + 33], biP[:, :, :, so:so + 33]))
                    so = S0 + sgn * sh
                    groups.append((bo[:, :, S0:S0 + 129], bi[:, :, S0:S0 + 129], bi[:, :, so:so + 129]))
                else:
                    EM(_call("tensor_copy", bo[:, :, 0:S0], bi[:, :, 0:S0]), reads=[bir], writes=[bor])
                    n_ = 129 - sh
                    if d == 0:
                        EM(_call("tensor_copy", bo[:, :, S0:S0 + sh], bi[:, :, S0:S0 + sh]), reads=[bir], writes=[bor])
                        groups.append((bo[:, :, S0 + sh:S0 + 129], bi[:, :, S0 + sh:S0 + 129], bi[:, :, S0:S0 + n_]))
                    else:
                        EM(_call("tensor_copy", bo[:, :, S0 + n_:S0 + 129], bi[:, :, S0 + n_:S0 + 129]), reads=[bir], writes=[bor])
                        groups.append((bo[:, :, S0:S0 + n_], bi[:, :, S0:S0 + n_], bi[:, :, S0 + sh:S0 + 129]))
                for (od, idd, isrc) in groups:
                    if len(od.shape) <= 3:
                        EM(_call("scalar_tensor_tensor", od, isrc, are_, idd, ALU.mult, ALU.add), reads=[bir, D_], writes=[bor])
                    else:
                        for cc_ in range(2):
                            EM(_call("scalar_tensor_tensor", od[:, cc_], isrc[:, cc_], are_, idd[:, cc_], ALU.mult, ALU.add), reads=[bir, D_], writes=[bor])
                    EM(_call("scalar_tensor_tensor", od[:, 0], isrc[:, 1], nim_, od[:, 0], ALU.mult, ALU.add), reads=[bir, bor, D_], writes=[bor])
                    EM(_call("scalar_tensor_tensor", od[:, 1], isrc[:, 0], aim_, od[:, 1], ALU.mult, ALU.add), reads=[bir, bor, D_], writes=[bor])
            fidx = 32 if d == 0 else 0
            cp(FIN[:, d, :, :, q:q + 1], EAP[:, :, :, n0 + fidx:n0 + fidx + 1], [ear], ["FIN"])
            xo = 0 if d == 0 else 1
            XA = B_["XA"]
            xar = "XA_%d" % s_
            EM(_call("tensor_copy", XA[:, :, 0:32], EA[:, :, 32 + xo:32 + xo + 32]), reads=[ear], writes=[xar])
            EM(_call("tensor_copy", XA[:, :, 32:64], EA[:, :, 97 + xo:97 + xo + 32]), reads=[ear], writes=[xar])
            EM(_call("tensor_copy", XA[:, :, 64:192], EA[:, :, 162 + xo:162 + xo + 128]), reads=[ear], writes=[xar])
            for r in range(8):
                k = r if d == 0 else 7 - r
                cm(HBr[:, :, r, :], XA, k, [hbr, xar, D_], [hbr, h16r], out2=H16r[:, :, r, :], neg_im=True)
            return ops

        def rec_y(n):
            c, qq, d = jobs[n]
            tab, wr = tabs[c]
            s_ = n % 2
            HB16 = SB[s_]["HB16"]
            first = (qq == 0 and d == 0)
            last = (qq == 3 and d == 1)
            for tb, (t0, tl, ci) in enumerate(TBS):
                m0 = t0 // 8
                H16r = HB16.rearrange("p c (r m) -> p c r m", r=8)
                S.op("pe", _call("matmul", rm(bank(tb)), tab[:, d, qq, 2, :], H16r[:, 0, :, m0:m0 + 64], start=first, stop=False),
                     reads=[wr, "HB16_%d" % s_], writes=["PS%d" % tb])
                S.op("pe", _call("matmul", rm(bank(tb)), tab[:, d, qq, 3, :], H16r[:, 1, :, m0:m0 + 64], start=False, stop=last),
                     reads=[wr, "HB16_%d" % s_], writes=["PS%d" % tb])

        def rec_gelu(c):
            for tb, (t0, tl, ci) in enumerate(TBS):
                g0, g1, g2 = GTMP
                nat = lambda ap_: ap_.rearrange("p (m r) -> p m r", r=8)
                S.op("dve", _call("scalar_tensor_tensor", nat(g0), nat(UT[:, c, t0:t0 + 512]), vcol("s5d%d" % e_, c),
                                  bank(tb).rearrange("p (r m) -> p m r", r=8), ALU.mult, ALU.add),
                     reads=["PS%d" % tb, "UT%d_%d" % (c, tb), "VEC"], writes=["G0"])
                S.op("act", _call("activation", g1, g0, AF.Square), reads=["G0"], writes=["G1"])
                S.op("dve", _call("tensor_scalar", g1, g1, 0.044715, 1.0, ALU.mult, ALU.add), reads=["G1"], writes=["G1"])
                S.op("dve", _call("tensor_tensor", g1, g1, g0, ALU.mult), reads=["G1", "G0"], writes=["G1"])
                S.op("act", _call("activation", g2, g1, AF.Sigmoid, scale=2.0 * math.sqrt(2.0 / math.pi)), reads=["G1"], writes=["G2"])
                S.op("dve", _call("tensor_tensor", GT[:, c, t0:t0 + 512], g2, g0, ALU.mult), reads=["G2", "G0"], writes=["GT%d_%d" % (c, tb)])

        NJ_ = len(jobs)
        for p_ in range(NJ_ // 2):
            rec_bu(2 * p_)
            rec_bu(2 * p_ + 1)
            A_ = rec_scan(2 * p_)
            B_ = rec_scan(2 * p_ + 1)
            for i_ in range(max(len(A_), len(B_))):
                if i_ < len(A_):
                    S.op("dve", A_[i_][0], reads=A_[i_][1], writes=A_[i_][2])
                if i_ < len(B_):
                    S.op("dve", B_[i_][0], reads=B_[i_][1], writes=B_[i_][2])
            rec_y(2 * p_)
            rec_y(2 * p_ + 1)
            if p_ % 4 == 3 and p_ < NJ_ // 2 - 1:
                rec_gelu(p_ // 4)
        rec_gelu(3)
        S.dma("sp", _call("dma_start", out=DR("ns5")[e_:e_ + 1].rearrange("o p n -> p (o n)"), in_=FIN.rearrange("p d c s q -> p (d c s q)")), reads=["FIN"], key="oFIN")

    def odd_mixer(l):
        o_ = l // 2
        scale = (64 + 32) ** -0.5
        HT = r3(SC(0, 8 * NT), 8)
        CAT = HT
        CQT = r3(SC(24576, 2 * NT), 2)
        CKVT = SC(30720, 2048)
        KRT = SC(34816, 2048)
        VTOK = r3(SC(38912, 16 * 1024), 16)
        QNR = [SC(71680 + i * 3072, NT) for i in range(2)]
        KN2 = [SC(77824 + i * 4096, 2048) for i in range(2)]
        ROPE = r3(SC(86016, 2 * 1024, F32), 2)
        CQF = r3(SC(94208, 2 * 512, F32), 2)
        CKVF = SC(98304, 512, F32)
        KRF = SC(100352, 512, F32)
        SQ = SC(102400, 512)
        PT = [SC(103424 + i * 1024, 512) for i in range(4)]
        REC = SC(107520, 512, F32)
        T1 = SC(109568, 512, F32)
        T2 = SC(111616, 512, F32)
        T3 = SC(113664, 512, F32)
        pre_norm(l, 1, HT, 38912)
        tap_bf16("h1_%d" % l, HT, ["HT%d_%d" % (c, tb) for c in range(8) for tb in range(3)])
        S.barrier()
        S.dma("sp", _call("dma_start", out=ROPE[0:32], in_=DR("ropeM")), writes=["ROPE"], key="rope")
        S.dma("pool", _call("dma_start", out=CKVT[:, 512:1024], in_=DR("cckvT")[o_:o_ + 1].rearrange("o p t -> p (o t)")), writes=["CKVc"], key="CKVc")
        for i_ in range(2):
            S.dma("pool", _call("dma_start", out=KN2[i_][64:96, 512:1024], in_=DR("ckrT")[o_:o_ + 1].rearrange("o p t -> p (o t)")), writes=["KRc"], key="KRc%d" % i_)
            S.op("pool", _call("memset", KN2[i_][96:128, :], 0.0), writes=["KNZ"])
            S.op("pool", _call("memset", QNR[i_][96:128, :], 0.0), writes=["KNZ"])
        w, wr = wtile()
        w3 = r3(w, 8)[:, :, 0:416]
        wdma(w3, DR("w_in_odd")[o_:o_ + 1].rearrange("o (kc p) n -> p (o kc) n", p=128), wr)
        ws_, wsr = wtile()
        ws3 = r3(ws_, 8)[:, :, 0:32]
        wdma(ws3, DR("w_kr_sw")[o_:o_ + 1].rearrange("o (kc p) n -> p (o kc) n", p=128), wsr)
        qg0 = VC["qng%d" % o_][0]
        kvg0 = VC["kvg%d" % o_][0]
        nckv_dst = DR("nckvT")[o_:o_ + 1].rearrange("o p t -> p (o t)")
        nkr_dst = DR("nkrT")[o_:o_ + 1].rearrange("o p t -> p (o t)")
        for tb, (t0, tl, ci) in enumerate(TBS):
            kcol = t0 if tb == 0 else 512 + t0
            for cc in range(2):
                for kc in range(8):
                    S.op("pe", _call("matmul", bank(cc), w3[:, kc, cc * 128:(cc + 1) * 128], HT[:, kc, t0:t0 + 512], start=(kc == 0), stop=(kc == 7)),
                         reads=[wr, "HT%d_%d" % (kc, tb)], writes=["PS%d" % cc])
                S.op("act", _call("activation", CQF[:, cc, :], bank(cc), AF.Identity), reads=["PS%d" % cc], writes=["CQF%d" % cc])
                S.op("act", _call("activation", SQ, bank(cc), AF.Square), reads=["PS%d" % cc], writes=["SQa"])
                S.op("pe", _call("matmul", bank(6), ONES, SQ, start=(cc == 0), stop=(cc == 1)), reads=["SQa", "ONES"], writes=["PS6"])
            rstd_from(bank(6), REC, ["PS6"], "REC", 1.0 / 256)
            for cc in range(2):
                S.op("dve", _call("tensor_tensor", CQF[:, cc, :], CQF[:, cc, :], REC, ALU.mult), reads=["CQF%d" % cc, "REC"], writes=["CQF%d" % cc])
                S.op("act", _call("activation", CQT[:, cc, t0:t0 + 512], CQF[:, cc, :], AF.Identity, scale=VEC[:, qg0 + cc:qg0 + cc + 1]),
                     reads=["CQF%d" % cc, "VEC"], writes=["CQT%d" % tb])
            for kc in range(8):
                S.op("pe", _call("matmul", bank(2), w3[:, kc, 256:384], HT[:, kc, t0:t0 + 512], start=(kc == 0), stop=(kc == 7)),
                     reads=[wr, "HT%d_%d" % (kc, tb)], writes=["PS2"])
            S.op("act", _call("activation", CKVF, bank(2), AF.Identity), reads=["PS2"], writes=["CKVF"])
            S.op("act", _call("activation", SQ, bank(2), AF.Square), reads=["PS2"], writes=["SQa"])
            S.op("pe", _call("matmul", bank(6), ONES, SQ, start=True, stop=True), reads=["SQa", "ONES"], writes=["PS6"])
            rstd_from(bank(6), REC, ["PS6"], "REC", 1.0 / 128)
            S.op("dve", _call("tensor_tensor", CKVF, CKVF, REC, ALU.mult), reads=["CKVF", "REC"], writes=["CKVF"])
            S.op("dve", _call("tensor_scalar", CKVF, CKVF, VEC[:, kvg0:kvg0 + 1], None, ALU.mult), reads=["CKVF", "VEC"], writes=["CKVF"])
            S.op("act", _call("activation", CKVT[:, kcol:kcol + 512], CKVF, AF.Identity), reads=["CKVF"], writes=["CKV%d" % tb])
            if tb == 0:
                S.dma("sp", _call("dma_start", out=nckv_dst, in_=CKVF), reads=["CKVF"], key="oCKV")
            for kc in range(8):
                S.op("pe", _call("matmul", bank(3, 512)[0:32], w3[:, kc, 384:416], HT[:, kc, t0:t0 + 512], start=(kc == 0), stop=(kc == 7)),
                     reads=[wr, "HT%d_%d" % (kc, tb)], writes=["PS3"])
            if tb == 0:
                S.op("act", _call("activation", KRF[0:32], bank(3)[0:32], AF.Identity), reads=["PS3"], writes=["KRF"])
                for i_ in range(2):
                    S.op("act", _call("activation", KN2[i_][64:96, kcol:kcol + 512], bank(3)[0:32], AF.Identity), reads=["PS3"], writes=["KR%d" % tb])
                S.dma("sp", _call("dma_start", out=nkr_dst, in_=KRF[0:32]), reads=["KRF"], key="oKR")
            else:
                for kc in range(8):
                    S.op("pe", _call("matmul", bank(4)[0:32], ws3[:, kc, :], HT[:, kc, t0:t0 + 512], start=(kc == 0), stop=(kc == 7)),
                         reads=[wsr, "HT%d_%d" % (kc, tb)], writes=["PS4"])
                r0 = t0 - 512
                S.op("dve", _call("tensor_tensor", T1[0:32], bank(3)[0:32], ROPE[0:32, 0, r0:r0 + 512], ALU.mult), reads=["PS3", "ROPE"], writes=["T1"])
                S.op("dve", _call("tensor_tensor", T2[0:32], bank(4)[0:32], ROPE[0:32, 1, r0:r0 + 512], ALU.mult), reads=["PS4", "ROPE"], writes=["T2"])
                S.op("dve", _call("tensor_tensor", T3[0:32], T1[0:32], T2[0:32], ALU.add), reads=["T1", "T2"], writes=["T3"])
                for i_ in range(2):
                    S.op("act", _call("activation", KN2[i_][64:96, kcol:kcol + 512], T3[0:32], AF.Identity), reads=["T3"], writes=["KR%d" % tb])
        chk("mla_inproj")
        kv_reads = ["CKVc", "CKV0", "CKV1", "CKV2"]
        kr_reads = ["KRc", "KR0", "KR1", "KR2"]
        w, wr = wtile()
        wv = w[:, 0:1024]
        wdma(wv, DR("wkv_v")[o_:o_ + 1].rearrange("o p n -> p (o n)"), wr)
        for kt in range(16):
            for hf in range(2):
                pb = (kt * 2 + hf) % 2
                S.op("pe", _call("matmul", bank(pb), CKVT[:, kt * 128:(kt + 1) * 128], wv[:, hf * 512:(hf + 1) * 512], start=True, stop=True),
                     reads=[wr] + kv_reads, writes=["PS%d" % pb])
                S.op("act", _call("activation", VTOK[:, kt, hf * 512:(hf + 1) * 512], bank(pb), AF.Identity), reads=["PS%d" % pb], writes=["VTOK"])
        chk("mla_vtok")
        wq, wqr = wtile()
        wq3 = r3(wq, 2)[:, :, 0:1536]
        wdma(wq3, DR("wq_up")[o_:o_ + 1].rearrange("o (kc p) n -> p (o kc) n", p=128), wqr)
        wqs, wqsr = wtile()
        wqs3 = r3(wqs, 2)[:, :, 0:512]
        wdma(wqs3, DR("wq_up_sw")[o_:o_ + 1].rearrange("o (kc p) n -> p (o kc) n", p=128), wqsr)
        wk, wkr = wtile()
        wkn = wk[:, 0:1024]
        wdma(wkn, DR("wkv_kn")[o_:o_ + 1].rearrange("o p n -> p (o n)"), wkr)
        seqs = [(0, 256, [0, 1], 0), (256, 256, [2, 3], 256), (512, 1024, list(range(4, 16)), 512)]

        def prologue_steps(h):
            QNRh, KNh = QNR[h % 2], KN2[h % 2]
            qres, kres = "QNR%d" % (h % 2), "KN%d" % (h % 2)
            steps = []
            for tb, (t0, tl, ci) in enumerate(TBS):
                def st_qn(tb=tb, t0=t0):
                    for kc in range(2):
                        S.op("pe", _call("matmul", bank(7)[0:64], wq3[:, kc, h * 96:h * 96 + 64], CQT[:, kc, t0:t0 + 512], start=(kc == 0), stop=(kc == 1)),
                             reads=[wqr, "CQT%d" % tb], writes=["PS7"])
                    S.op("act", _call("activation", QNRh[0:64, t0:t0 + 512], bank(7)[0:64], AF.Identity), reads=["PS7"], writes=[qres])
                steps.append(st_qn)

                def st_qr(tb=tb, t0=t0):
                    for kc in range(2):
                        S.op("pe", _call("matmul", bank(7)[0:32], wq3[:, kc, h * 96 + 64:h * 96 + 96], CQT[:, kc, t0:t0 + 512], start=(kc == 0), stop=(kc == 1)),
                             reads=[wqr, "CQT%d" % tb], writes=["PS7"])
                    if tb == 0:
                        S.op("act", _call("activation", QNRh[64:96, t0:t0 + 512], bank(7)[0:32], AF.Identity), reads=["PS7"], writes=[qres])
                    else:
                        r0 = t0 - 512
                        S.op("dve", _call("tensor_tensor", T1[0:32], bank(7)[0:32], ROPE[0:32, 0, r0:r0 + 512], ALU.mult), reads=["PS7", "ROPE"], writes=["T1"])
                steps.append(st_qr)
                if tb > 0:
                    def st_qs(tb=tb, t0=t0):
                        for kc in range(2):
                            S.op("pe", _call("matmul", bank(7)[0:32], wqs3[:, kc, h * 32:h * 32 + 32], CQT[:, kc, t0:t0 + 512], start=(kc == 0), stop=(kc == 1)),
                                 reads=[wqsr, "CQT%d" % tb], writes=["PS7"])
                        r0 = t0 - 512
                        S.op("dve", _call("tensor_tensor", T2[0:32], bank(7)[0:32], ROPE[0:32, 1, r0:r0 + 512], ALU.mult), reads=["PS7", "ROPE"], writes=["T2"])
                        S.op("dve", _call("tensor_tensor", T3[0:32], T1[0:32], T2[0:32], ALU.add), reads=["T1", "T2"], writes=["T3"])
                        S.op("act", _call("activation", QNRh[64:96, t0:t0 + 512], T3[0:32], AF.Identity), reads=["T3"], writes=[qres])
                    steps.append(st_qs)
            for kb in range(4):
                def st_kn(kb=kb):
                    S.op("pe", _call("matmul", bank(7)[0:64], wkn[:, h * 64:(h + 1) * 64], CKVT[:, kb * 512:(kb + 1) * 512], start=True, stop=True),
                         reads=[wkr] + kv_reads, writes=["PS7"])
                    S.op("act", _call("activation", KNh[0:64, kb * 512:(kb + 1) * 512], bank(7)[0:64], AF.Identity), reads=["PS7"], writes=[kres])
                steps.append(st_kn)
            return steps

        for st in prologue_steps(0):
            st()
        tasks, epis = [], []
        for h in range(16):
            QNRh, KNh = QNR[h % 2], KN2[h % 2]
            qres, kres = "QNR%d" % (h % 2), "KN%d" % (h % 2)
            nxt = prologue_steps(h + 1) if h + 1 < 16 else []
            hp = h % 2
            tcount = 0
            for (q0, qlen, vtiles, k0) in seqs:
                nqb = max(1, qlen // 512)
                N = min(qlen, 512)
                for qb in range(nqb):
                    qs = q0 + qb * 512
                    g = len(epis)
                    for ki, vt_i in enumerate(vtiles):
                        kc0 = k0 + ki * 128

                        def A(ps, kc0=kc0, qs=qs, N=N, QNRh=QNRh, KNh=KNh, qres=qres, kres=kres):
                            S.op("pe", _call("matmul", bank(ps, N), KNh[:, kc0:kc0 + 128], QNRh[:, qs:qs + N], start=True, stop=True),
                                 reads=kr_reads + ["KNZ", kres, qres], writes=["PS%d" % ps])

                        def C(pt, ptr, ob, sb, first, last, vt_i=vt_i, N=N, hb=(h // 2) * 128):
                            S.op("pe", _call("matmul", bank(ob, N), VTOK[:, vt_i, hb:hb + 128], pt[:, 0:N], start=first, stop=last),
                                 reads=[ptr, "VTOK"], writes=["PS%d" % ob])
                            S.op("pe", _call("matmul", bank(sb, N), ONES, pt[:, 0:N], start=first, stop=last),
                                 reads=[ptr, "ONES"], writes=["PS%d" % sb])
                        pre = nxt[tcount] if tcount < len(nxt) else None
                        tcount += 1
                        tasks.append((g, ki, len(vtiles), N, A, C, pre))

                    def E(ob, sb, N, qs=qs, a0=64 * hp, a1=64 * hp + 64, hc=h // 2):
                        S.op("dve", _call("reciprocal", REC[:, 0:N], bank(sb, N)), reads=["PS%d" % sb], writes=["REC"])
                        S.op("dve", _call("tensor_tensor", CAT[a0:a1, hc, qs:qs + N], bank(ob, N)[a0:a1], REC[a0:a1, 0:N], ALU.mult),
                             reads=["PS%d" % ob, "REC"], writes=["CAT"])
                        return None
                    epis.append(E)
            assert tcount >= len(nxt)
        attn_pipeline(tasks, epis, PT, scale)
        S.barrier()
        chk("mla_attn")
        osrc = DR("w_out_odd")[o_:o_ + 1].rearrange("o (kc p) n -> p (o kc) n", p=128)
        wts = []
        for half in range(2):
            w, wr = wtile()
            w3o = r3(w, 8)
            wdma(w3o, osrc[:, :, half * 512:(half + 1) * 512], wr)
            wts.append((w3o, wr))
        for tb, (t0, tl, ci) in enumerate(TBS):
            def yfn(oc, tb=tb, t0=t0):
                w3o, wr = wts[oc // 4]
                oi = oc % 4
                pb = oc % 2
                for kc in range(8):
                    S.op("pe", _call("matmul", bank(pb), w3o[:, kc, oi * 128:(oi + 1) * 128], CAT[:, kc, t0:t0 + 512], start=(kc == 0), stop=(kc == 7)),
                         reads=[wr, "CAT"], writes=["PS%d" % pb])
                return bank(pb), "PS%d" % pb
            post_norm_res(l, 1, tb, 38912, 55296, yfn)
        S.barrier()

    try:
        chk("load")
        for l in range(n_layers):
            compute_mod(l)
            chk("mod")
            if l % 2 == 0:
                even_mixer(l)
            else:
                odd_mixer(l)
            tap_f32("xmix_%d" % l, XT, [xres(c, tb) for c in range(8) for tb in range(3)])
            chk("mixer")
            ffn(l)
            tap_f32("xffn_%d" % l, XT, [xres(c, tb) for c in range(8) for tb in range(3)])
    except _Stop:
        pass
    ydst = DR("yT").rearrange("(c p) t -> p c t", p=128)
    for c in range(8):
        S.dma("sp" if c % 2 == 0 else "act", _call("dma_start", out=ydst[:, c, :], in_=XT[:, c, :]),
              reads=[xres(c, tb) for tb in range(3)], key="out")


def _rope_tables():
    def ang(n, rot):
        rows = n // 64
        row = np.repeat(np.arange(rows, dtype=np.float32), 64)
        col = np.tile(np.arange(64, dtype=np.float32), rows)
        nf = rot // 4
        inv = (10000.0 ** (-np.arange(nf, dtype=np.float32) / nf)).astype(np.float32)
        return np.concatenate([row[:, None] * inv, col[:, None] * inv], axis=-1).astype(np.float32)
    a = ang(1024, 64)
    cosd, sind = np.cos(a), np.sin(a)
    ropeD = np.zeros((128, 2, 1024), np.float32)
    for p in range(128):
        d = p % 64
        j = d % 32
        ropeD[p, 0] = cosd[:, j]
        ropeD[p, 1] = -sind[:, j] if d < 32 else sind[:, j]
    a = ang(1024, 32)
    cosm, sinm = np.cos(a), np.sin(a)
    ropeM = np.zeros((32, 2, 1024), np.float32)
    for p in range(32):
        j = p % 16
        ropeM[p, 0] = cosm[:, j]
        ropeM[p, 1] = -sinm[:, j] if p < 16 else sinm[:, j]
    return ropeD, ropeM


def _swap_halves(w, half):
    return np.concatenate([w[..., half:], w[..., :half]], axis=-1)


def _prep_shared(inp):
    f = np.float32
    sh = {}
    sh["cst"] = np.tile(np.array(EXPS + [0], f)[None, :], (128, 1))
    sh["ropeD"], sh["ropeM"] = _rope_tables()
    sh["w_mod"] = inp["w_mod"]
    sh["wg"], sh["wu"], sh["wd"] = inp["w_ffn_gate"], inp["w_ffn_up"], inp["w_ffn_down"]
    wie = inp["w_in_even"]
    sh["w_in_even"] = wie
    qk = wie[:, :, 512:1536].reshape(2, 1024, 2, 4, 2, 64)
    sh["w_in_even_sw"] = np.ascontiguousarray(_swap_halves(qk, 32).reshape(2, 1024, 1024))
    sh["w_out_even"] = inp["w_out_even"]
    sh["glu_w"] = inp["s5_glu_w"]
    tab = np.zeros((2, 4, 128, 2, 4, 4, 128), f)
    for e in range(2):
        for d in range(2):
            for c in range(4):
                for qq in range(4):
                    for gi in range(2):
                        g = 8 * c + 2 * qq + gi
                        r0 = (2 * qq + gi) * 16
                        tab[e, c, r0:r0 + 16, d, qq, 0, gi * 64:(gi + 1) * 64] = inp["s5_b_re"][e, d, g].T
                        tab[e, c, r0:r0 + 16, d, qq, 1, gi * 64:(gi + 1) * 64] = inp["s5_b_im"][e, d, g].T
                        tab[e, c, gi * 64:(gi + 1) * 64, d, qq, 2, r0:r0 + 16] = inp["s5_c_re"][e, d, g].T
                        tab[e, c, gi * 64:(gi + 1) * 64, d, qq, 3, r0:r0 + 16] = inp["s5_c_im"][e, d, g].T
    sh["s5tab"] = tab.reshape(2, 4, 128, 4096)
    sh["dlam"] = np.stack([inp["diff_lam_q1"], inp["diff_lam_k1"], inp["diff_lam_q2"], inp["diff_lam_k2"]], axis=1).astype(f)
    wio = inp["w_in_odd"]
    sh["w_in_odd"] = wio
    sh["w_kr_sw"] = np.ascontiguousarray(_swap_halves(wio[:, :, 384:416], 16))
    wq = inp["mla_w_q_up"]
    sh["wq_up"] = wq
    sh["wq_up_sw"] = np.ascontiguousarray(_swap_halves(wq.reshape(2, 256, 16, 96)[..., 64:96], 16).reshape(2, 256, 512))
    wkv = inp["mla_w_kv_up"].reshape(2, 128, 16, 128)
    sh["wkv_kn"] = np.ascontiguousarray(wkv[..., :64].reshape(2, 128, 1024))
    sh["wkv_v"] = np.ascontiguousarray(wkv[..., 64:].reshape(2, 128, 1024))
    sh["w_out_odd"] = inp["w_out_odd"]
    return sh


def _prep_core(inp, c):
    f = np.float32
    m = {}
    x = np.concatenate([inp["x_prompt"][2 * c], inp["x_prompt"][2 * c + 1], inp["x_sample"][c]], axis=0)
    m["xT"] = np.ascontiguousarray(x.T)
    vec = np.zeros((128, NV), f)

    def put(name, arr):
        c0, n = VC[name]
        vec[:, c0:c0 + n] = np.asarray(arr, f).reshape(n, 128).T
    cond = np.zeros((8, 2, 128), f)
    cond[:, 0, :] = inp["c_ctx"].reshape(8, 128)
    cond[:, 1, :] = inp["c"][c].reshape(8, 128)
    put("cond", cond.reshape(16 * 128))
    for l in range(4):
        put("bmod%d" % l, inp["b_mod"][l])
        put("gpm%d" % l, inp["g_pre_mix"][l])
        put("gqm%d" % l, inp["g_post_mix"][l])
        put("gpf%d" % l, inp["g_pre_ffn"][l])
        put("gqf%d" % l, inp["g_post_ffn"][l])
    for e in range(2):
        put("s5d%d" % e, inp["s5_d"][e])
        put("glub%d" % e, inp["s5_glu_b"][e])
        put("subg%d" % e, inp["diff_subln_g"][e])
    for o in range(2):
        put("qng%d" % o, inp["mla_q_norm_g"][o])
        put("kvg%d" % o, inp["mla_kv_norm_g"][o])
    m["vecT"] = vec
    s5p = np.zeros((2, 128, 5, 32), f)
    for e in range(2):
        for d in range(2):
            for q in range(16):
                for gi in range(2):
                    g = 2 * q + gi
                    j = d * 16 + q
                    sl = slice(gi * 64, gi * 64 + 64)
                    s5p[e, sl, 0, j] = inp["s5_lam_re"][e, d, g]
                    s5p[e, sl, 1, j] = inp["s5_lam_im"][e, d, g]
                    s5p[e, sl, 2, j] = inp["s5_log_dt"][e, d, g]
                    s5p[e, sl, 3, j] = inp["state_s5_re"][c, e, d, g]
                    s5p[e, sl, 4, j] = inp["state_s5_im"][c, e, d, g]
    m["s5p"] = s5p
    m["cdkT"] = np.ascontiguousarray(inp["cache_diff_k"][c].reshape(2, 512, 512).transpose(0, 2, 1))
    m["cdv"] = np.ascontiguousarray(inp["cache_diff_v"][c].reshape(2, 512, 512))
    m["cckvT"] = np.ascontiguousarray(inp["cache_mla_ckv"][c].transpose(0, 2, 1))
    m["ckrT"] = np.ascontiguousarray(inp["cache_mla_krope"][c].transpose(0, 2, 1))
    return m


_NC_CACHE = {}


def kernel(**inputs):
    inp = {k: np.asarray(v) for k, v in inputs.items()}
    if "nc" not in _NC_CACHE:
        _NC_CACHE["nc"] = build(4)
    nc = _NC_CACHE["nc"]
    sh = _prep_shared(inp)
    in_maps = []
    for c in range(8):
        m = dict(sh)
        m.update(_prep_core(inp, c))
        in_maps.append({k: np.ascontiguousarray(v, dtype=np.float32) for k, v in m.items()})
    res = run_bass_kernel_spmd(nc, in_maps, core_ids=list(range(8)))
    R = res.results
    f = np.float32
    y_prompt = np.zeros((16, 256, 1024), f)
    y_sample = np.zeros((8, 1024, 1024), f)
    ns_re = np.zeros((16, 2, 2, 32, 64), f)
    ns_im = np.zeros((16, 2, 2, 32, 64), f)
    ndk = np.zeros((16, 2, 256, 4, 2, 64), f)
    ndv = np.zeros((16, 2, 256, 4, 128), f)
    nckv = np.zeros((16, 2, 256, 128), f)
    nkr = np.zeros((16, 2, 256, 32), f)
    for c in range(8):
        r = R[c]
        y = r["yT"].T
        y_prompt[2 * c] = y[0:256]
        y_prompt[2 * c + 1] = y[256:512]
        y_sample[c] = y[512:1536]
        ns5 = r["ns5"].reshape(2, 2, 64, 2, 2, 2, 16)
        for s in range(2):
            t = ns5[:, :, :, :, :, s, :].transpose(0, 3, 4, 5, 1, 2)
            ns_re[2 * c + s] = t[:, :, 0].reshape(2, 2, 32, 64)
            ns_im[2 * c + s] = t[:, :, 1].reshape(2, 2, 32, 64)
            ndk[2 * c + s] = r["nkT"][:, :, s * 256:(s + 1) * 256].transpose(0, 2, 1).reshape(2, 256, 4, 2, 64)
            ndv[2 * c + s] = r["nv"][:, s * 256:(s + 1) * 256, :].reshape(2, 256, 4, 128)
            nckv[2 * c + s] = r["nckvT"][:, :, s * 256:(s + 1) * 256].transpose(0, 2, 1)
            nkr[2 * c + s] = r["nkrT"][:, :, s * 256:(s + 1) * 256].transpose(0, 2, 1)
    return (y_prompt, y_sample, ns_re, ns_im, ndk, ndv, nckv, nkr)
```

```python
import math
import numpy as np
import concourse.bass as bass
import concourse.mybir as mybir
from concourse.bass_utils import run_bass_kernel_spmd

F32 = mybir.dt.float32
BF16 = mybir.dt.bfloat16
AF = mybir.ActivationFunctionType
ALU = mybir.AluOpType
AX = mybir.AxisListType

ENGS = ("pe", "act", "dve", "pool", "sp")


class _Op:
    __slots__ = ("fn", "deps", "dma_key", "sig")

    def __init__(self, fn, deps, dma_key):
        self.fn = fn
        self.deps = deps
        self.dma_key = dma_key
        self.sig = None


class Sched:
    def __init__(self, nc):
        self.nc = nc
        self.ops = {e: [] for e in ENGS}
        self.ncomp = {e: 0 for e in ENGS}
        self.last_w = {}
        self.readers = {}
        self.dma_cnt = {}
        self.dma_keys = []
        self.bar_toks = set()

    def _mk(self, eng, fn, reads, writes, dma_key):
        writes = tuple(writes) + tuple(r for r in reads if r.startswith("PS") and r not in writes)
        deps = set()
        for r in reads:
            w = self.last_w.get(r)
            if w is not None:
                deps.add(w)
        for r in writes:
            w = self.last_w.get(r)
            if w is not None:
                deps.add(w)
            for rd in self.readers.get(r, ()):
                deps.add(rd)
        if eng == "pool" and not all(w.startswith("W") for w in writes):
            deps |= self.bar_toks
        op = _Op(fn, deps, dma_key)
        if dma_key is None:
            self.ncomp[eng] += 1
            tok = ("c", eng, self.ncomp[eng])
        else:
            if dma_key not in self.dma_cnt:
                self.dma_cnt[dma_key] = 0
                self.dma_keys.append(dma_key)
            self.dma_cnt[dma_key] += 16
            tok = ("d", dma_key, self.dma_cnt[dma_key])
        op.sig = tok
        op.deps.discard(tok)
        for r in writes:
            self.last_w[r] = tok
            self.readers[r] = []
        for r in reads:
            self.readers.setdefault(r, []).append(tok)
        self.ops[eng].append((eng, op))
        return op

    def op(self, eng, fn, reads=(), writes=()):
        return self._mk(eng, fn, tuple(reads), tuple(writes), None)

    def dma(self, eng, fn, reads=(), writes=(), key=None):
        assert eng in ("sp", "act", "pool")
        return self._mk(eng, fn, tuple(reads), tuple(writes), key)

    def barrier(self):
        toks = set()
        for e in ENGS:
            if self.ncomp[e] > 0:
                toks.add(("c", e, self.ncomp[e]))
        for k, v in self.dma_cnt.items():
            if not (isinstance(k, str) and k.startswith("W")):
                toks.add(("d", k, v))
        self.bar_toks = toks
        for e in ("pe", "act", "dve", "sp"):
            op = _Op(None, set(toks), None)
            op.sig = None
            self.ops[e].append((e, op))

    def emit(self, final_wait=True):
        nc = self.nc
        import contextlib
        with contextlib.ExitStack() as st:
            csem = {e: st.enter_context(nc.semaphore("s_" + e)) for e in ENGS}
            dsem = {}
            for i, k in enumerate(self.dma_keys):
                dsem[k] = st.enter_context(nc.semaphore("d%d" % i))
            block = st.enter_context(nc.Block())
            engobj = {}

            def run(ename, eng):
                seen_c = {e: 0 for e in ENGS}
                seen_d = {}
                issued = 0
                for (_, op) in self.ops[ename]:
                    need_c = {}
                    need_d = {}
                    for d in op.deps:
                        if d[0] == "c":
                            if d[2] > need_c.get(d[1], 0):
                                need_c[d[1]] = d[2]
                        else:
                            if d[2] > need_d.get(d[1], 0):
                                need_d[d[1]] = d[2]
                    for e2, v in need_c.items():
                        if e2 == ename and ename == "pe":
                            continue
                        if v > seen_c[e2]:
                            eng.wait_ge(csem[e2], v)
                            seen_c[e2] = v
                    for k2, v in need_d.items():
                        if v > seen_d.get(k2, 0):
                            eng.wait_ge(dsem[k2], v)
                            seen_d[k2] = v
                    if op.fn is None:
                        continue
                    ins = op.fn(eng)
                    if op.sig[0] == "c":
                        issued += 1
                        ins.then_inc(csem[ename], 1)
                        seen_c[ename] = max(seen_c[ename], 0)
                    else:
                        ins.then_inc(dsem[op.sig[1]], 16)
                if final_wait and ename == "sp":
                    for k2, v in self.dma_cnt.items():
                        if v > seen_d.get(k2, 0):
                            eng.wait_ge(dsem[k2], v)
                    for e2 in ENGS:
                        if self.ncomp[e2] > seen_c[e2]:
                            eng.wait_ge(csem[e2], self.ncomp[e2])

            @block.tensor
            def _(e):
                run("pe", e)

            @block.scalar
            def _(e):
                run("act", e)

            @block.vector
            def _(e):
                run("dve", e)

            @block.gpsimd
            def _(e):
                run("pool", e)

            @block.sync
            def _(e):
                run("sp", e)


D_MODEL = 1024
NT = 1536
DFF = 2816
NJ = 22
TBS = [(0, 512, 0), (512, 512, 1), (1024, 512, 1)]
EPS = 1e-6
EXPS = [1, 2, 3, 4, 5, 6, 7, 8, 16, 32, 64, 128, 256, 512, 1024]
NV = 384

XT_O = 0
VEC_O = 49152
ONES_O = 51200
SCT_O = 51456
MODT_O = 51520
DER_O = 51904
LAMS_O = 52288
CST_O = 53248
WP_O = 54272
SCR_O = 87040
ARENA_B = 204800

IN_SHAPES = {
    "xT": (1024, 1536), "vecT": (128, NV), "cst": (128, 16), "s5p": (2, 128, 5, 32), "dlam": (2, 4, 64),
    "ropeD": (128, 2, 1024), "ropeM": (32, 2, 1024),
    "w_mod": (4, 1024, 6144), "wg": (4, 1024, 2816), "wu": (4, 1024, 2816), "wd": (4, 2816, 1024),
    "w_in_even": (2, 1024, 2048), "w_in_even_sw": (2, 1024, 1024), "w_out_even": (2, 1024, 1024),
    "glu_w": (2, 512, 512), "s5tab": (2, 4, 128, 4096), "cdkT": (2, 512, 512), "cdv": (2, 512, 512),
    "w_in_odd": (2, 1024, 416), "w_kr_sw": (2, 1024, 32), "wq_up": (2, 256, 1536), "wq_up_sw": (2, 256, 512),
    "wkv_kn": (2, 128, 1024), "wkv_v": (2, 128, 1024), "w_out_odd": (2, 1024, 1024),
    "cckvT": (2, 128, 512), "ckrT": (2, 32, 512),
}
OUT_SHAPES = {
    "yT": (1024, 1536), "ns5": (2, 128, 128), "nkT": (2, 512, 512), "nv": (2, 512, 512),
    "nckvT": (2, 128, 512), "nkrT": (2, 32, 512),
}


def vec_cols():
    cols = {}
    cur = [0]

    def add(name, n):
        cols[name] = (cur[0], n)
        cur[0] += n
    add("cond", 16)
    for l in range(4):
        add("bmod%d" % l, 48)
        add("gpm%d" % l, 8)
        add("gqm%d" % l, 8)
        add("gpf%d" % l, 8)
        add("gqf%d" % l, 8)
    for e in range(2):
        add("s5d%d" % e, 4)
        add("glub%d" % e, 4)
        add("subg%d" % e, 1)
    for o in range(2):
        add("qng%d" % o, 2)
        add("kvg%d" % o, 1)
    assert cur[0] <= NV
    return cols


VC = vec_cols()


class _Stop(Exception):
    pass


STOP_AT = [None]
PHASE_MARKS = []
S5_LIMIT = [None]
S5_DIRS = [(0, 1)]


def _call(method, *args, **kwargs):
    return lambda e: getattr(e, method)(*args, **kwargs)


def build(n_layers=4, taps=()):
    nc = bass.Bass("TRN2", target_bir_lowering=False)
    dram = {}
    for name, shape in IN_SHAPES.items():
        dram[name] = nc.dram_tensor(name, list(shape), F32, kind="ExternalInput")
    for name, shape in OUT_SHAPES.items():
        dram[name] = nc.dram_tensor(name, list(shape), F32, kind="ExternalOutput")
    tapd = {}
    for t in taps:
        tapd[t] = nc.dram_tensor("tap_" + t, [1024, 1536], F32, kind="ExternalOutput")
    import contextlib
    with contextlib.ExitStack() as st:
        arena = st.enter_context(nc.sbuf_tensor("arena", [128, ARENA_B // 2], BF16))
        psum = st.enter_context(nc.psum_tensor("psum", [128, 4096], F32))
        S = Sched(nc)
        _build_body(nc, S, dram, tapd, arena, psum, n_layers)
        S.emit()
    return nc


def _build_body(nc, S, dram, tapd, arena, psum, n_layers):
    def chk(name):
        PHASE_MARKS.append((name, {e: sum(1 for (_, o) in S.ops[e] if o.fn is not None and o.dma_key is None) for e in ("pe", "act", "dve")}))
        if STOP_AT[0] == name:
            raise _Stop()

    def V(off, n, dt=BF16, p0=0, p1=128):
        if dt == BF16:
            return arena[p0:p1, off // 2: off // 2 + n]
        return arena[p0:p1, off // 2: off // 2 + 2 * n].bitcast(F32)

    def r3(ap, a):
        return ap.rearrange("p (a b) -> p a b", a=a)

    def bank(i, n=512):
        return psum[:, i * 512: i * 512 + n]

    def DR(name):
        return dram[name].ap()

    XT = r3(V(XT_O, 8 * NT, F32), 8)
    VEC = V(VEC_O, NV, F32)
    ONES = V(ONES_O, 128)
    SCT = r3(V(SCT_O, 16), 8)
    MODT = r3(V(MODT_O, 96, F32), 48)
    DER = V(DER_O, 96, F32).rearrange("p (k c i) -> p k c i", k=6, c=8)
    LAMS = V(LAMS_O, 16, F32)
    CST = V(CST_O, 16, F32)
    WT = [V(WP_O + i * 8192, 4096) for i in range(4)]
    wcnt = [0]

    def wtile():
        i = wcnt[0] % 4
        wcnt[0] += 1
        return WT[i], "W%d" % i

    def SC(off, n, dt=BF16, p0=0, p1=128):
        return V(SCR_O + off, n, dt, p0, p1)

    def vcol(name, j=0, n=1):
        c0 = VC[name][0] + j
        return VEC[:, c0:c0 + n]

    def wdma(dst, src, wres, reads=()):
        S.dma("pool", _call("dma_start", out=dst, in_=src), reads=reads, writes=[wres], key=wres)

    xsrc = DR("xT").rearrange("(c p) t -> p c t", p=128)
    for c in range(8):
        S.dma("sp" if c % 2 == 0 else "act", _call("dma_start", out=XT[:, c, :], in_=xsrc[:, c, :]),
              writes=["XT%d_%d" % (c, tb) for tb in range(3)], key="xin%d" % c)
    S.dma("sp", _call("dma_start", out=VEC, in_=DR("vecT")), writes=["VEC"], key="vec")
    S.dma("sp", _call("dma_start", out=CST, in_=DR("cst")), writes=["CST"], key="cst")
    S.op("pool", _call("memset", ONES, 1.0), writes=["ONES"])
    S.op("act", _call("activation", SCT.rearrange("p a b -> p (a b)"), VEC[:, 0:16], AF.Silu), reads=["VEC"], writes=["SCT"])

    def xres(c, tb):
        return "XT%d_%d" % (c, tb)

    def rstd_from(psn, rstd, reads, wres, scale):
        S.op("act", _call("activation", rstd, psn, AF.Sqrt, bias=EPS, scale=scale), reads=reads, writes=[wres])
        S.op("dve", _call("reciprocal", rstd, rstd), reads=[wres], writes=[wres])

    def pre_norm(l, which, HT, tmp_off, PSN=7):
        ka, kb = (0, 0) if which == 1 else (3, 24)
        TMPN = r3(SC(tmp_off, 8 * 512, F32), 8)
        RSTD = SC(tmp_off + 16384, 512, F32)
        SQ = [SC(tmp_off + 18432 + i * 1024, 512) for i in range(2)]
        for tb, (t0, tl, ci) in enumerate(TBS):
            for c in range(8):
                sq = SQ[c % 2]
                S.op("act", _call("activation", sq, XT[:, c, t0:t0 + 512], AF.Square),
                     reads=[xres(c, tb)], writes=["SQ%d" % (c % 2)])
                S.op("pe", _call("matmul", bank(PSN), ONES, sq, start=(c == 0), stop=(c == 7)),
                     reads=["SQ%d" % (c % 2), "ONES"], writes=["PS%d" % PSN])
            rstd_from(bank(PSN), RSTD, ["PS%d" % PSN], "RSTD", 1.0 / D_MODEL)
            S.op("dve", _call("tensor_tensor", TMPN, XT[:, :, t0:t0 + 512], RSTD.unsqueeze(1).to_broadcast([128, 8, 512]), ALU.mult),
                 reads=["RSTD"] + [xres(c, tb) for c in range(8)], writes=["TMPN"])
            for c in range(8):
                S.op("act", _call("activation", HT[:, c, t0:t0 + 512], TMPN[:, c, :], AF.Identity,
                                                                       bias=MODT[:, kb + c, ci:ci + 1], scale=DER[:, ka, c, ci:ci + 1]),
                     reads=["TMPN", "MODT", "DER"], writes=["HT%d_%d" % (c, tb)])

    def post_norm_res(l, which, tb, ybuf_off, tmp_off, yfn, PSN=7):
        kg = 2 if which == 1 else 5
        t0, tl, ci = TBS[tb]
        YBUF = r3(SC(ybuf_off, 8 * 512, F32), 8)
        TMPN = r3(SC(tmp_off, 8 * 512, F32), 8)
        RSTD = SC(tmp_off + 16384, 512, F32)
        SQ = [SC(tmp_off + 18432 + i * 1024, 512) for i in range(2)]
        for oc in range(8):
            yp, yres = yfn(oc)
            sq = SQ[oc % 2]
            S.op("act", _call("activation", YBUF[:, oc, :], yp, AF.Identity), reads=[yres], writes=["YBUF%d" % oc])
            S.op("act", _call("activation", sq, yp, AF.Square), reads=[yres], writes=["SQ%d" % (oc % 2)])
            S.op("pe", _call("matmul", bank(PSN), ONES, sq, start=(oc == 0), stop=(oc == 7)),
                 reads=["SQ%d" % (oc % 2), "ONES"], writes=["PS%d" % PSN])
        rstd_from(bank(PSN), RSTD, ["PS%d" % PSN], "RSTD", 1.0 / D_MODEL)
        S.op("dve", _call("tensor_tensor", TMPN, YBUF, RSTD.unsqueeze(1).to_broadcast([128, 8, 512]), ALU.mult),
             reads=["RSTD"] + ["YBUF%d" % c for c in range(8)], writes=["TMPN"])
        for c in range(8):
            S.op("dve", _call("scalar_tensor_tensor", XT[:, c, t0:t0 + 512], TMPN[:, c, :], DER[:, kg, c, ci:ci + 1],
                                                               XT[:, c, t0:t0 + 512], ALU.mult, ALU.add),
                 reads=["TMPN", "DER", xres(c, tb)], writes=[xres(c, tb)])

    def compute_mod(l):
        wsrc = DR("w_mod")[l:l + 1].rearrange("o (kc p) n -> p (o kc) n", p=128)
        PSM = r3(bank(6, 96), 48)
        for wt in range(12):
            w, wr = wtile()
            w3 = r3(w, 8)
            wdma(w3, wsrc[:, :, wt * 512:(wt + 1) * 512], wr)
            for fi in range(4):
                f = wt * 4 + fi
                for kc in range(8):
                    S.op("pe", _call("matmul", PSM[:, f, :], w3[:, kc, fi * 128:(fi + 1) * 128], SCT[:, kc, :],
                                                                            start=(kc == 0), stop=(kc == 7)),
                         reads=[wr, "SCT"], writes=["PS6"])
        b0 = VC["bmod%d" % l][0]
        S.op("dve", _call("tensor_tensor", MODT, PSM, VEC[:, b0:b0 + 48].unsqueeze(2).to_broadcast([128, 48, 2]), ALU.add),
             reads=["PS6", "VEC"], writes=["MODT"])
        for k, (sc0, gname) in enumerate([(8, "gpm"), (16, "gqm"), (32, "gpf"), (40, "gqf")]):
            kk = [0, 2, 3, 5][k]
            g0 = VC["%s%d" % (gname, l)][0]
            gb = VEC[:, g0:g0 + 8].unsqueeze(2).to_broadcast([128, 8, 2])
            if k in (0, 2):
                S.op("dve", _call("tensor_scalar", DER[:, kk], MODT[:, sc0:sc0 + 8, :], 1.0, None, ALU.add),
                     reads=["MODT"], writes=["DER"])
                S.op("dve", _call("tensor_tensor", DER[:, kk], DER[:, kk], gb, ALU.mult), reads=["DER", "VEC"], writes=["DER"])
            else:
                S.op("dve", _call("tensor_tensor", DER[:, kk], MODT[:, sc0:sc0 + 8, :], gb, ALU.mult),
                     reads=["MODT", "VEC"], writes=["DER"])

    def tap_f32(name, src3, reads):
        if name in tapd:
            dst = tapd[name].ap().rearrange("(c p) t -> p c t", p=128)
            S.dma("sp", _call("dma_start", out=dst, in_=src3), reads=reads, key="tap")

    def tap_bf16(name, src3, reads, nchunk=8, c0=0):
        if name in tapd:
            dst = tapd[name].ap().rearrange("(c p) t -> p c t", p=128)[:, c0:c0 + nchunk, :]
            S.dma("pool", _call("dma_start", out=dst, in_=src3), reads=reads, key="tap")

    def attn_pipeline(tasks, epilogues, PT, scale, SK=2):
        n = len(tasks)
        deferred = []
        for idx in range(n + SK + 4):
            if idx < n:
                g, ki, nk, N, A, C = tasks[idx][:6]
                if len(tasks[idx]) > 6 and tasks[idx][6] is not None:
                    tasks[idx][6]()
                ps = idx % 3
                pt, ptr = PT[idx % 4], "PT%d" % (idx % 4)
                A(ps)
                S.op("act", _call("activation", pt[:, 0:N], bank(ps, N), AF.Exp, scale=scale), reads=["PS%d" % ps], writes=[ptr])
            jx = idx - SK
            if 0 <= jx < n:
                g, ki, nk, N, A, C = tasks[jx][:6]
                ob, sb = 3 + 2 * (g % 2), 4 + 2 * (g % 2)
                C(PT[jx % 4], "PT%d" % (jx % 4), ob, sb, ki == 0, ki == nk - 1)
                if ki == nk - 1:
                    d2 = epilogues[g](ob, sb, N)
                    if d2 is not None:
                        deferred.append((idx + 3, d2))
            for (due, fn) in [x for x in deferred if x[0] <= idx]:
                fn()
            deferred = [x for x in deferred if x[0] > idx]
        for (due, fn) in deferred:
            fn()

    def ffn(l):
        HT = r3(SC(0, 8 * NT), 8)
        ACTT = r3(SC(24576, NJ * NT), NJ)
        SL = [SC(92160 + i * 2048, 512, F32) for i in range(2)]
        pre_norm(l, 2, HT, 96256)
        gsrc = DR("wg")[l:l + 1].rearrange("o (kc p) n -> p (o kc) n", p=128)
        usrc = DR("wu")[l:l + 1].rearrange("o (kc p) n -> p (o kc) n", p=128)
        it = 0
        for jg in range(6):
            nj = 4 if jg < 5 else 2
            wgt, wgr = wtile()
            wut, wur = wtile()
            wg3 = r3(wgt, 8)[:, :, 0:nj * 128]
            wu3 = r3(wut, 8)[:, :, 0:nj * 128]
            wdma(wg3, gsrc[:, :, jg * 512: jg * 512 + nj * 128], wgr)
            wdma(wu3, usrc[:, :, jg * 512: jg * 512 + nj * 128], wur)
            for ji in range(nj):
                j = jg * 4 + ji
                for tb, (t0, tl, ci) in enumerate(TBS):
                    pg, pu = (it % 2) * 2, (it % 2) * 2 + 1
                    sl = SL[it % 2]
                    it += 1
                    for kc in range(8):
                        S.op("pe", _call("matmul", bank(pg), wg3[:, kc, ji * 128:(ji + 1) * 128], HT[:, kc, t0:t0 + 512],
                                                                                         start=(kc == 0), stop=(kc == 7)),
                             reads=[wgr, "HT%d_%d" % (kc, tb)], writes=["PS%d" % pg])
                    for kc in range(8):
                        S.op("pe", _call("matmul", bank(pu), wu3[:, kc, ji * 128:(ji + 1) * 128], HT[:, kc, t0:t0 + 512],
                                                                                         start=(kc == 0), stop=(kc == 7)),
                             reads=[wur, "HT%d_%d" % (kc, tb)], writes=["PS%d" % pu])
                    S.op("act", _call("activation", sl, bank(pg), AF.Silu), reads=["PS%d" % pg], writes=["SL%d" % (it % 2)])
                    S.op("dve", _call("tensor_tensor", ACTT[:, j, t0:t0 + 512], sl, bank(pu), ALU.mult),
                         reads=["SL%d" % (it % 2), "PS%d" % pu], writes=["ACT%d_%d" % (j, tb)])
        S.barrier()
        chk("ffn_gateup")
        dsrc = DR("wd")[l:l + 1].rearrange("o (j p) n -> p (o j) n", p=128)
        for tb, (t0, tl, ci) in enumerate(TBS):
            state = {}

            def yfn(oc, tb=tb, t0=t0, state=state):
                half, oi = oc // 4, oc % 4
                if oi == 0:
                    for jt in range(3):
                        njj = 8 if jt < 2 else 6
                        w, wr = wtile()
                        w3 = r3(w, 8)[:, 0:njj, :]
                        wdma(w3, dsrc[:, jt * 8: jt * 8 + njj, half * 512:(half + 1) * 512], wr)
                        for jj in range(njj):
                            j = jt * 8 + jj
                            for o2 in range(4):
                                S.op("pe", _call("matmul", bank(o2), w3[:, jj, o2 * 128:(o2 + 1) * 128], ACTT[:, j, t0:t0 + 512],
                                                                                      start=(j == 0), stop=(j == NJ - 1)),
                                     reads=[wr, "ACT%d_%d" % (j, tb)], writes=["PS%d" % o2])
                return bank(oi), "PS%d" % oi
            post_norm_res(l, 2, tb, 0, 96256, yfn)
        S.barrier()
        chk("ffn_down")

    def even_mixer(l):
        e_ = l // 2
        lam_init = 0.8 - 0.6 * math.exp(-0.3 * l)
        HT = r3(SC(0, 8 * NT), 8)
        GT = r3(SC(24576, 4 * NT), 4)
        UT = r3(SC(36864, 4 * NT), 4)
        pre_norm(l, 1, HT, 49152)
        tap_bf16("h1_%d" % l, HT, ["HT%d_%d" % (c, tb) for c in range(8) for tb in range(3)])
        S.barrier()
        chk("prenorm")
        wsrc = DR("w_in_even")[e_:e_ + 1].rearrange("o (kc p) n -> p (o kc) n", p=128)
        wsw = DR("w_in_even_sw")[e_:e_ + 1].rearrange("o (kc p) n -> p (o kc) n", p=128)

        def proj_fm(w3, wr, fc, tb, pb):
            t0 = TBS[tb][0]
            for kc in range(8):
                S.op("pe", _call("matmul", bank(pb), w3[:, kc, fc * 128:(fc + 1) * 128], HT[:, kc, t0:t0 + 512], start=(kc == 0), stop=(kc == 7)),
                     reads=[wr, "HT%d_%d" % (kc, tb)], writes=["PS%d" % pb])

        w, wr = wtile()
        w3 = r3(w, 8)
        wdma(w3, wsrc[:, :, 0:512], wr)
        it = 0
        for fc in range(4):
            for tb in range(3):
                pb = it % 2
                it += 1
                proj_fm(w3, wr, fc, tb, pb)
                t0 = TBS[tb][0]
                S.op("act", _call("activation", UT[:, fc, t0:t0 + 512], bank(pb), AF.Identity),
                     reads=["PS%d" % pb], writes=["UT%d_%d" % (fc, tb)])
        chk("uproj")
        s5(l, e_, UT, GT)
        S.barrier()
        tap_bf16("ut_%d" % l, UT, [], 4, 0)
        tap_bf16("gt_%d" % l, GT, [], 4, 0)
        tap_bf16("hb16_%d" % l, r3(SC(61440, 2 * NT), 2), [], 2, 0)
        chk("s5")
        S5OUT = UT
        w, wr = wtile()
        w3 = r3(w, 8)[:, 0:4, :]
        wdma(w3, DR("glu_w")[e_:e_ + 1].rearrange("o (kc p) n -> p (o kc) n", p=128), wr)
        SG = [SC(49152 + i * 2048, 512, F32) for i in range(2)]
        it = 0
        for fo in range(4):
            for tb in range(3):
                t0 = TBS[tb][0]
                pb = it % 2
                sg = SG[it % 2]
                it += 1
                for kc in range(4):
                    S.op("pe", _call("matmul", bank(pb), w3[:, kc, fo * 128:(fo + 1) * 128], GT[:, kc, t0:t0 + 512], start=(kc == 0), stop=(kc == 3)),
                         reads=[wr] + ["GT%d_%d" % (kc, tb)], writes=["PS%d" % pb])
                S.op("act", _call("activation", sg, bank(pb), AF.Sigmoid, bias=vcol("glub%d" % e_, fo), scale=1.0),
                     reads=["PS%d" % pb, "VEC"], writes=["SG%d" % (it % 2)])
                S.op("dve", _call("tensor_tensor", S5OUT[:, fo, t0:t0 + 512], sg, GT[:, fo, t0:t0 + 512], ALU.mult),
                     reads=["SG%d" % (it % 2), "GT%d_%d" % (fo, tb)], writes=["S5O%d_%d" % (fo, tb)])
        S.barrier()
        tap_bf16("s5out_%d" % l, S5OUT, ["S5O%d_%d" % (c, tb) for c in range(4) for tb in range(3)], 4, 0)
        chk("glu")
        QT = r3(SC(49152, 4 * NT), 4)
        KT = r3(SC(61440, 4 * 2048), 4)
        VT = r3(SC(77824, 16 * 512), 16)
        ROPE = r3(SC(94208, 2 * 1024, F32), 2)
        STG = SC(104448, 512, F32)
        T1 = SC(106496, 512, F32)
        T2 = SC(108544, 512, F32)
        S.dma("sp", _call("dma_start", out=ROPE, in_=DR("ropeD")), writes=["ROPE"], key="rope")
        S.dma("pool", _call("dma_start", out=KT[:, :, 512:1024], in_=DR("cdkT")[e_:e_ + 1].rearrange("o (c p) t -> p (o c) t", p=128)),
              writes=["KTc"], key="KTc")
        S.dma("pool", _call("dma_start", out=VT[:, 4:8, :], in_=DR("cdv")[e_:e_ + 1].rearrange("o (t p) f -> p (o t) f", p=128)),
              writes=["VTc"], key="VTc")
        nk_dst = DR("nkT")[e_:e_ + 1].rearrange("o (c p) t -> p (o c) t", p=128)
        for which in (0, 1):
            w, wr = wtile()
            w3 = r3(w, 8)
            wdma(w3, wsrc[:, :, 512 * (1 + which): 512 * (2 + which)], wr)
            ws_, wsr = wtile()
            ws3 = r3(ws_, 8)
            wdma(ws3, wsw[:, :, 512 * which: 512 * (which + 1)], wsr)
            for fc in range(4):
                for tb in range(3):
                    t0 = TBS[tb][0]
                    proj_fm(w3, wr, fc, tb, 0)
                    if tb == 0:
                        if which == 0:
                            S.op("act", _call("activation", QT[:, fc, 0:512], bank(0), AF.Identity), reads=["PS0"], writes=["QT%d_0" % fc])
                        else:
                            S.op("act", _call("activation", KT[:, fc, 0:512], bank(0), AF.Identity), reads=["PS0"], writes=["KT%d_0" % fc])
                            S.op("dve", _call("tensor_copy", STG, bank(0)), reads=["PS0"], writes=["STG"])
                            S.dma("sp", _call("dma_start", out=nk_dst[:, fc, :], in_=STG), reads=["STG"], key="oSTG")
                    else:
                        proj_fm(ws3, wsr, fc, tb, 1)
                        r0 = t0 - 512
                        S.op("dve", _call("tensor_tensor", T1, bank(0), ROPE[:, 0, r0:r0 + 512], ALU.mult), reads=["PS0", "ROPE"], writes=["T1"])
                        S.op("dve", _call("tensor_tensor", T2, bank(1), ROPE[:, 1, r0:r0 + 512], ALU.mult), reads=["PS1", "ROPE"], writes=["T2"])
                        if which == 0:
                            dst, dres = QT[:, fc, t0:t0 + 512], "QT%d_%d" % (fc, tb)
                        else:
                            dst, dres = KT[:, fc, 512 + t0: 512 + t0 + 512], "KT%d_%d" % (fc, tb)
                        S.op("dve", _call("tensor_tensor", dst, T1, T2, ALU.add), reads=["T1", "T2"], writes=[dres])
        w, wr = wtile()
        w3 = r3(w, 8)
        wdma(w3, wsrc[:, :, 1536:2048], wr)
        nv_dst = DR("nv")[e_:e_ + 1].rearrange("o (t p) f -> p (o t) f", p=128)
        for tt in range(12):
            pb = tt % 2
            vt_i = tt if tt < 4 else tt + 4
            for kc in range(8):
                S.op("pe", _call("matmul", bank(pb), HT[:, kc, tt * 128:(tt + 1) * 128], w3[:, kc, :], start=(kc == 0), stop=(kc == 7)),
                     reads=[wr, "HT%d_%d" % (kc, tt // 4)], writes=["PS%d" % pb])
            S.op("act", _call("activation", VT[:, vt_i, :], bank(pb), AF.Identity), reads=["PS%d" % pb], writes=["VT%d" % vt_i])
            if tt < 4:
                S.op("dve", _call("tensor_copy", STG, bank(pb)), reads=["PS%d" % pb], writes=["STG"])
                S.dma("sp", _call("dma_start", out=nv_dst[:, tt, :], in_=STG), reads=["STG"], key="oSTG")
        chk("qkv")
        DL = r3(SC(110592, 256, F32), 4)
        DP = r3(SC(111616, 128, F32), 2)
        S.dma("sp", _call("dma_start", out=DL, in_=DR("dlam")[e_:e_ + 1].rearrange("o a b -> (o a) b").partition_broadcast(128)), writes=["DL"], key="dl")
        S.op("dve", _call("tensor_tensor", DP[:, 0, :], DL[:, 0, :], DL[:, 1, :], ALU.mult), reads=["DL"], writes=["DP"])
        S.op("dve", _call("tensor_tensor", DP[:, 1, :], DL[:, 2, :], DL[:, 3, :], ALU.mult), reads=["DL", "DP"], writes=["DP"])
        S.op("dve", _call("reduce_sum", LAMS[:, 0:2], DP, AX.X), reads=["DP"], writes=["LAMS"])
        S.op("act", _call("activation", LAMS[:, 2:4], LAMS[:, 0:2], AF.Exp), reads=["LAMS"], writes=["LAMS"])
        S.op("dve", _call("tensor_tensor", LAMS[:, 4:5], LAMS[:, 3:4], LAMS[:, 2:3], ALU.subtract), reads=["LAMS"], writes=["LAMS"])
        S.op("dve", _call("tensor_scalar", LAMS[:, 4:5], LAMS[:, 4:5], -lam_init, None, ALU.add), reads=["LAMS"], writes=["LAMS"])
        S.op("dve", _call("tensor_scalar", LAMS[:, 5:6], vcol("subg%d" % e_), 1.0 - lam_init, None, ALU.mult), reads=["VEC", "LAMS"], writes=["LAMS"])
        QZ = [r3(SC(0, 4 * NT), 4), r3(SC(12288, 4 * NT), 4)]
        allht = ["HT%d_%d" % (c_, t_) for c_ in range(8) for t_ in range(3)]
        allqt = ["QT%d_%d" % (c_, t_) for c_ in range(4) for t_ in range(3)]
        S.op("pool", _call("memset", QZ[0][64:128], 0.0), writes=allht + ["QZ0"])
        S.op("pool", _call("memset", QZ[1][0:64], 0.0), writes=allht + ["QZ1"])
        S.op("act", _call("activation", QZ[0][0:64], QT[0:64], AF.Identity), reads=allqt, writes=allht + ["QZ0"])
        S.op("dve", _call("tensor_copy", QZ[1][64:128], QT[64:128]), reads=allqt, writes=allht + ["QZ1"])
        S.barrier()
        OT = GT
        PT = [SC(98304 + i * 1024, 512) for i in range(4)]
        REC = SC(102400, 512, F32)
        REC2 = SC(110592, 512, F32)
        OC = [T1, T2]
        OO = STG
        SQ = SC(112640, 512)
        seqs = [(0, 256, [0, 1], 0), (256, 256, [2, 3], 256), (512, 1024, list(range(4, 16)), 512)]
        tasks, epis = [], []
        for (q0, qlen, vtiles, k0) in seqs:
            nqb = max(1, qlen // 512)
            N = min(qlen, 512)
            for h in range(4):
                for qb in range(nqb):
                    qs = q0 + qb * 512
                    tbq = 0 if q0 < 512 else 1 + qb
                    for c in range(2):
                        p0, p1 = 64 * c, 64 * c + 64
                        g = len(epis)
                        for ki, vt_i in enumerate(vtiles):
                            kc0 = k0 + ki * 128

                            def A(ps, h=h, kc0=kc0, qs=qs, c=c, N=N):
                                S.op("pe", _call("matmul", bank(ps, N), KT[:, h, kc0:kc0 + 128], QZ[c][:, h, qs:qs + N], start=True, stop=True),
                                     reads=["KTc", "QZ%d" % c] + ["KT%d_%d" % (h, t) for t in range(3)], writes=["PS%d" % ps])

                            def C(pt, ptr, ob, sb, first, last, vt_i=vt_i, h=h, N=N):
                                S.op("pe", _call("matmul", bank(ob, N), VT[:, vt_i, h * 128:(h + 1) * 128], pt[:, 0:N], start=first, stop=last),
                                     reads=[ptr, "VT%d" % vt_i, "VTc"], writes=["PS%d" % ob])
                                S.op("pe", _call("matmul", bank(sb, N), ONES, pt[:, 0:N], start=first, stop=last),
                                     reads=[ptr, "ONES"], writes=["PS%d" % sb])
                            tasks.append((g, ki, len(vtiles), N, A, C))

                        def E(ob, sb, N, c=c, h=h, qs=qs):
                            S.op("dve", _call("reciprocal", REC[:, 0:N], bank(sb, N)), reads=["PS%d" % sb], writes=["REC"])
                            S.op("dve", _call("tensor_tensor", OC[c][:, 0:N], bank(ob, N), REC[:, 0:N], ALU.mult), reads=["PS%d" % ob, "REC"], writes=["OC%d" % c])
                            if c == 0:
                                return None
                            S.op("dve", _call("scalar_tensor_tensor", OO[:, 0:N], OC[1][:, 0:N], LAMS[:, 4:5], OC[0][:, 0:N], ALU.mult, ALU.add),
                                 reads=["OC0", "OC1", "LAMS"], writes=["OO"])
                            S.op("act", _call("activation", SQ[:, 0:N], OO[:, 0:N], AF.Square), reads=["OO"], writes=["SQa"])

                            def E2():
                                S.op("pe", _call("matmul", bank(7, N), ONES, SQ[:, 0:N], start=True, stop=True), reads=["SQa", "ONES"], writes=["PS7"])
                                S.op("act", _call("activation", REC2[:, 0:N], bank(7, N), AF.Sqrt, bias=EPS, scale=1.0 / 128), reads=["PS7"], writes=["REC2"])
                                S.op("dve", _call("reciprocal", REC2[:, 0:N], REC2[:, 0:N]), reads=["REC2"], writes=["REC2"])
                                S.op("dve", _call("tensor_tensor", OO[:, 0:N], OO[:, 0:N], REC2[:, 0:N], ALU.mult), reads=["OO", "REC2"], writes=["OO"])
                                S.op("act", _call("activation", OT[:, h, qs:qs + N], OO[:, 0:N], AF.Identity, scale=LAMS[:, 5:6]),
                                     reads=["OO", "LAMS"], writes=["OT%d" % h])
                            return E2
                        epis.append(E)
        attn_pipeline(tasks, epis, PT, 0.125)
        S.barrier()
        tap_bf16("diffout_%d" % l, OT, ["OT%d" % h for h in range(4)], 4, 4)
        chk("attn")
        osrc = DR("w_out_even")[e_:e_ + 1].rearrange("o (kc p) n -> p (o kc) n", p=128)
        wts = []
        for half in range(2):
            w, wr = wtile()
            w3 = r3(w, 8)
            wdma(w3, osrc[:, :, half * 512:(half + 1) * 512], wr)
            wts.append((w3, wr))
        for tb, (t0, tl, ci) in enumerate(TBS):
            def yfn(oc, tb=tb, t0=t0):
                w3, wr = wts[oc // 4]
                oi = oc % 4
                pb = oc % 2
                for kc in range(8):
                    src = S5OUT[:, kc, t0:t0 + 512] if kc < 4 else OT[:, kc - 4, t0:t0 + 512]
                    S.op("pe", _call("matmul", bank(pb), w3[:, kc, oi * 128:(oi + 1) * 128], src, start=(kc == 0), stop=(kc == 7)),
                         reads=[wr], writes=["PS%d" % pb])
                return bank(pb), "PS%d" % pb
            post_norm_res(l, 1, tb, 49152, 65536, yfn)
        S.barrier()

    def s5(l, e_, UT, GT):
        SLOT = 25856
        ENG = ["dve", "dve"]

        def slot_bufs(s_):
            o = 49152 + s_ * SLOT
            return dict(
                HB=r3(SC(o, 2 * NT, F32), 2), HB16=r3(SC(o + 12288, 2 * NT), 2),
                EA=r3(SC(o + 18432, 2 * 323, F32), 2), EB=r3(SC(o + 21016, 2 * 323, F32), 2),
                XA=r3(SC(o + 23600, 2 * 192, F32), 2),
                TD=SC(o + 12288, 512, F32), TC=SC(o + 14336, 512, F32))
        SB = [slot_bufs(0), slot_bufs(1)]
        sh0 = 49152 + 2 * SLOT
        PW = SC(sh0, 3 * 15 * 32, F32).rearrange("p (a k j) -> p a k j", a=3, k=15)
        S5P = r3(SC(sh0 + 5760, 5 * 32, F32), 5)
        FF = r3(SC(sh0 + 6400, 3 * 32, F32), 3)
        FIN = SC(sh0 + 6784, 128, F32).rearrange("p (d c s q) -> p d c s q", d=2, c=2, s=2)
        GTMP = [SC(sh0 + 7296 + i * 2048, 512, F32) for i in range(3)]
        TM = r3(SC(49152, 6 * 15 * 32, F32), 6)
        S.dma("sp", _call("dma_start", out=S5P, in_=DR("s5p")[e_:e_ + 1].rearrange("o p a j -> p (o a) j")), writes=["S5P"], key="s5p")
        lr, li, ldt = S5P[:, 0, :], S5P[:, 1, :], S5P[:, 2, :]
        tm = [TM[:, i, :].rearrange("p (k j) -> p k j", k=15) for i in range(6)]
        sm = [TM[:, i, 0:32] for i in range(6)]
        D_ = "S5C"

        def dv(fn, eng="dve"):
            S.op(eng, fn, reads=[D_, "S5P", "CST"], writes=[D_])
        dv(_call("activation", sm[0], ldt, AF.Exp), "act")
        dv(_call("tensor_tensor", sm[1], lr, sm[0], ALU.mult))
        dv(_call("tensor_tensor", sm[2], li, sm[0], ALU.mult))
        exb = CST[:, 0:15].unsqueeze(2).to_broadcast([128, 15, 32])
        dv(_call("tensor_tensor", tm[3], sm[1].unsqueeze(1).to_broadcast([128, 15, 32]), exb, ALU.mult))
        dv(_call("activation", tm[3], tm[3], AF.Exp), "act")
        dv(_call("tensor_tensor", tm[4], sm[2].unsqueeze(1).to_broadcast([128, 15, 32]), exb, ALU.mult))

        def sin_of(dst, shift):
            dv(_call("tensor_scalar", tm[5], tm[4], shift, 1.0 / (2 * math.pi), ALU.add, ALU.mult))
            ki = TM[:, 0, :].bitcast(mybir.dt.int32).rearrange("p (k j) -> p k j", k=15)
            dv(_call("tensor_copy", ki, tm[5]))
            dv(_call("tensor_copy", tm[5], ki))
            dv(_call("tensor_scalar", dst, tm[4], shift, None, ALU.add))
            dv(_call("scalar_tensor_tensor", dst, tm[5], -2 * math.pi, dst, ALU.mult, ALU.add))
            dv(_call("tensor_scalar", tm[5], dst, -math.pi, 2 * math.pi, ALU.is_lt, ALU.mult))
            dv(_call("tensor_tensor", dst, dst, tm[5], ALU.add))
            dv(_call("tensor_scalar", tm[5], dst, math.pi, -2 * math.pi, ALU.is_gt, ALU.mult))
            dv(_call("tensor_tensor", dst, dst, tm[5], ALU.add))
            dv(_call("activation", dst, dst, AF.Sin), "act")
        sin_of(tm[1], 0.0)
        sin_of(tm[2], math.pi / 2)
        dv(_call("tensor_tensor", PW[:, 0], tm[3], tm[2], ALU.mult))
        dv(_call("tensor_tensor", PW[:, 1], tm[3], tm[1], ALU.mult))
        dv(_call("tensor_scalar", PW[:, 2], PW[:, 1], -1.0, None, ALU.mult))
        are, aim = PW[:, 0, 0, :], PW[:, 1, 0, :]
        s0, s1, s2, s3, s4 = [TM[:, 0, 32 * i:32 * i + 32] for i in range(5)]
        dv(_call("tensor_scalar", s0, are, -1.0, None, ALU.add))
        dv(_call("tensor_tensor", s1, lr, lr, ALU.mult))
        dv(_call("tensor_tensor", s2, li, li, ALU.mult))
        dv(_call("tensor_tensor", s1, s1, s2, ALU.add))
        dv(_call("reciprocal", s1, s1))
        dv(_call("tensor_tensor", s2, s0, lr, ALU.mult))
        dv(_call("tensor_tensor", s3, aim, li, ALU.mult))
        dv(_call("tensor_tensor", s2, s2, s3, ALU.add))
        dv(_call("tensor_tensor", FF[:, 0, :], s2, s1, ALU.mult))
        dv(_call("tensor_tensor", s2, aim, lr, ALU.mult))
        dv(_call("tensor_tensor", s3, s0, li, ALU.mult))
        dv(_call("tensor_tensor", s2, s2, s3, ALU.subtract))
        dv(_call("tensor_tensor", FF[:, 1, :], s2, s1, ALU.mult))
        dv(_call("tensor_scalar", FF[:, 2, :], FF[:, 1, :], -1.0, None, ALU.mult))
        S.barrier()

        def pw(a, k, j):
            return PW[:, a, k, j:j + 1]

        def rm(ap512):
            return ap512.rearrange("p (r m) -> p r m", r=8)

        for s_i in range(2):
            S.op("dve", _call("memset", SB[s_i]["EA"], 0.0), writes=["EA_%d" % s_i])
            S.op("dve", _call("memset", SB[s_i]["EB"], 0.0), writes=["EB_%d" % s_i])
        tabsrc = DR("s5tab")[e_:e_ + 1]
        tabs = []
        for c in range(4):
            w, wr = wtile()
            wdma(w, tabsrc[:, c].rearrange("o p n -> p (o n)"), wr)
            tabs.append((w.rearrange("p (d q m n) -> p d q m n", d=2, q=4, m=4), wr))
        jobs = [(c, qq, d) for c in range(4) for qq in range(4) for d in range(2)]
        psb = [0]

        def rec_bu(n):
            c, qq, d = jobs[n]
            tab, wr = tabs[c]
            s_ = n % 2
            B_ = SB[s_]
            eng = ENG[s_]
            j = d * 16 + 4 * c + qq
            hbr = "HB_%d" % s_
            for tb, (t0, tl, ci) in enumerate(TBS):
                pr, pi = 4 + (psb[0] % 2) * 2, 5 + (psb[0] % 2) * 2
                psb[0] += 1
                m0 = t0 // 8
                u_rm = UT[:, c, t0:t0 + 512].rearrange("p (m r) -> p r m", r=8)
                HBr = B_["HB"].rearrange("p c (r m) -> p c r m", r=8)
                S.op("pe", _call("matmul", rm(bank(pr)), tab[:, d, qq, 0, :], u_rm, start=True, stop=True),
                     reads=[wr, "UT%d_%d" % (c, tb)], writes=["PS%d" % pr])
                S.op("pe", _call("matmul", rm(bank(pi)), tab[:, d, qq, 1, :], u_rm, start=True, stop=True),
                     reads=[wr, "UT%d_%d" % (c, tb)], writes=["PS%d" % pi])
                S.op("act", _call("activation", HBr[:, 0, :, m0:m0 + 64], rm(bank(pr)), AF.Identity, scale=FF[:, 0, j:j + 1]), reads=["PS%d" % pr, D_], writes=[hbr])
                S.op("act", _call("activation", HBr[:, 1, :, m0:m0 + 64], rm(bank(pi)), AF.Identity, scale=FF[:, 0, j:j + 1]), reads=["PS%d" % pi, D_], writes=[hbr])
                S.op(eng, _call("scalar_tensor_tensor", HBr[:, 0, :, m0:m0 + 64], rm(bank(pi)), FF[:, 2, j:j + 1], HBr[:, 0, :, m0:m0 + 64], ALU.mult, ALU.add),
                     reads=["PS%d" % pi, hbr, D_], writes=[hbr])
                S.op(eng, _call("scalar_tensor_tensor", HBr[:, 1, :, m0:m0 + 64], rm(bank(pr)), FF[:, 1, j:j + 1], HBr[:, 1, :, m0:m0 + 64], ALU.mult, ALU.add),
                     reads=["PS%d" % pr, hbr, D_], writes=[hbr])

        def rec_scan(n):
            c, qq, d = jobs[n]
            q = 4 * c + qq
            s_ = n % 2
            B_ = SB[s_]
            eng = ENG[s_]
            j = d * 16 + q
            HB, HB16, EA, EB = B_["HB"], B_["HB16"], B_["EA"], B_["EB"]
            hbr, h16r, ear, ebr = "HB_%d" % s_, "HB16_%d" % s_, "EA_%d" % s_, "EB_%d" % s_
            ops = []

            def EM(fn, reads=(), writes=()):
                ops.append((fn, reads, writes))
            HBr = HB.rearrange("p c (r m) -> p c r m", r=8)
            H16r = HB16.rearrange("p c (r m) -> p c r m", r=8)
            h16 = HB16.rearrange("p c (m r) -> p c m r", r=8)
            hvP = HB[:, :, 0:512].rearrange("p c (s m r) -> p c s m r", s=2, r=8)
            h16P = HB16[:, :, 0:512].rearrange("p c (s m r) -> p c s m r", s=2, r=8)

            def cm(dst2, src2, k, reads, writes, out2=None, neg_im=False):
                are_, aim_, nim_ = pw(0, k, j), pw(1, k, j), pw(2, k, j)
                o2 = dst2 if out2 is None else out2
                if len(dst2.shape) <= 3:
                    EM(_call("scalar_tensor_tensor", dst2, src2, are_, dst2, ALU.mult, ALU.add), reads=reads, writes=writes)
                else:
                    EM(_call("scalar_tensor_tensor", dst2[:, 0], src2[:, 0], are_, dst2[:, 0], ALU.mult, ALU.add), reads=reads, writes=writes)
                    EM(_call("scalar_tensor_tensor", dst2[:, 1], src2[:, 1], are_, dst2[:, 1], ALU.mult, ALU.add), reads=reads, writes=writes)
                EM(_call("scalar_tensor_tensor", o2[:, 0], src2[:, 1], nim_, dst2[:, 0], ALU.mult, ALU.add), reads=reads, writes=writes)
                if neg_im:
                    EM(_call("scalar_tensor_tensor", o2[:, 1], src2[:, 0], nim_, dst2[:, 1], ALU.mult, ALU.subtract), reads=reads, writes=writes)
                else:
                    EM(_call("scalar_tensor_tensor", o2[:, 1], src2[:, 0], aim_, dst2[:, 1], ALU.mult, ALU.add), reads=reads, writes=writes)

            def cp(dst, src, reads, writes):
                if len(dst.shape) <= 3:
                    EM(_call("tensor_copy", dst, src), reads=reads, writes=writes)
                else:
                    for cc_ in range(2):
                        EM(_call("tensor_copy", dst[:, cc_], src[:, cc_]), reads=reads, writes=writes)
            order = range(1, 8) if d == 0 else range(6, -1, -1)
            for r in order:
                rsrc = r - 1 if d == 0 else r + 1
                cm(HBr[:, :, r, :], HBr[:, :, rsrc, :], 0, [hbr, D_], [hbr])
            rend = 7 if d == 0 else 0

            def PV(buf):
                if d == 0:
                    return buf[:, :, 0:130].rearrange("p c (s n) -> p c s n", s=2), 32
                return buf[:, :, 32:162].rearrange("p c (s n) -> p c s n", s=2), 0
            EAP, n0 = PV(EA)
            EBP, _n = PV(EB)
            S0 = 162
            eofs = 1 if d == 0 else 0
            hidx = 0 if d == 0 else 32
            cp(EAP[:, :, :, n0 + eofs:n0 + eofs + 32], HBr[:, :, rend, 0:64].rearrange("p c (s m) -> p c s m", s=2), [hbr], [ear])
            for cc_ in range(2):
                EM(_call("memset", EAP[:, cc_, :, n0 + hidx:n0 + hidx + 1], 0.0), writes=[ear])
            EM(_call("tensor_copy", EA[:, :, S0 + eofs:S0 + eofs + 128], HBr[:, :, rend, 64:192]), reads=[hbr], writes=[ear])
            hcol = S0 if d == 0 else S0 + 128
            EM(_call("tensor_copy", EA[:, :, hcol:hcol + 1], S5P[:, 3:5, j:j + 1]), reads=["S5P"], writes=[ear])
            bufs = [(EA, EAP, ear), (EB, EBP, ebr)]
            sgn = -1 if d == 0 else 1
            for lev in range(8):
                sh = 1 << lev
                (bi, biP, bir), (bo, boP, bor) = bufs[lev % 2], bufs[(lev + 1) % 2]
                k = 7 + lev
                are_, aim_, nim_ = pw(0, k, j), pw(1, k, j), pw(2, k, j)
                groups = []
                if sh <= 32:
                    so = n0 + sgn * sh
                    groups.append((boP[:, :, :, n0:n0 + 33], biP[:, :, :, n0:n0 + 33], biP[:, :, :, so:so + 33]))
                    so = S0 + sgn * sh
                    groups.append((bo[:, :, S0:S0 + 129], bi[:, :, S0:S0 + 129], bi[:, :, so:so + 129]))
                else:
                    EM(_call("tensor_copy", bo[:, :, 0:S0], bi[:, :, 0:S0]), reads=[bir], writes=[bor])
                    n_ = 129 - sh
                    if d == 0:
                        EM(_call("tensor_copy", bo[:, :, S0:S0 + sh], bi[:, :, S0:S0 + sh]), reads=[bir], writes=[bor])
                        groups.append((bo[:, :, S0 + sh:S0 + 129], bi[:, :, S0 + sh:S0 + 129], bi[:, :, S0:S0 + n_]))
                    else:
                        EM(_call("tensor_copy", bo[:, :, S0 + n_:S0 + 129], bi[:, :, S0 + n_:S0 + 129]), reads=[bir], writes=[bor])
                        groups.append((bo[:, :, S0:S0 + n_], bi[:, :, S0:S0 + n_], bi[:, :, S0 + sh:S0 + 129]))
                for (od, idd, isrc) in groups:
                    if len(od.shape) <= 3:
                        EM(_call("scalar_tensor_tensor", od, isrc, are_, idd, ALU.mult, ALU.add), reads=[bir, D_], writes=[bor])
                    else:
                        for cc_ in range(2):
                            EM(_call("scalar_tensor_tensor", od[:, cc_], isrc[:, cc_], are_, idd[:, cc_], ALU.mult, ALU.add), reads=[bir, D_], writes=[bor])
                    EM(_call("scalar_tensor_tensor", od[:, 0], isrc[:, 1], nim_, od[:, 0], ALU.mult, ALU.add), reads=[bir, bor, D_], writes=[bor])
                    EM(_call("scalar_tensor_tensor", od[:, 1], isrc[:, 0], aim_, od[:, 1], ALU.mult, ALU.add), reads=[bir, bor, D_], writes=[bor])
            fidx = 32 if d == 0 else 0
            cp(FIN[:, d, :, :, q:q + 1], EAP[:, :, :, n0 + fidx:n0 + fidx + 1], [ear], ["FIN"])
            xo = 0 if d == 0 else 1
            XA = B_["XA"]
            xar = "XA_%d" % s_
            EM(_call("tensor_copy", XA[:, :, 0:32], EA[:, :, 32 + xo:32 + xo + 32]), reads=[ear], writes=[xar])
            EM(_call("tensor_copy", XA[:, :, 32:64], EA[:, :, 97 + xo:97 + xo + 32]), reads=[ear], writes=[xar])
            EM(_call("tensor_copy", XA[:, :, 64:192], EA[:, :, 162 + xo:162 + xo + 128]), reads=[ear], writes=[xar])
            for r in range(8):
                k = r if d == 0 else 7 - r
                cm(HBr[:, :, r, :], XA, k, [hbr, xar, D_], [hbr, h16r], out2=H16r[:, :, r, :], neg_im=True)
            return ops

        def rec_y(n):
            c, qq, d = jobs[n]
            tab, wr = tabs[c]
            s_ = n % 2
            HB16 = SB[s_]["HB16"]
            first = (qq == 0 and d == 0)
            last = (qq == 3 and d == 1)
            for tb, (t0, tl, ci) in enumerate(TBS):
                m0 = t0 // 8
                H16r = HB16.rearrange("p c (r m) -> p c r m", r=8)
                S.op("pe", _call("matmul", rm(bank(tb)), tab[:, d, qq, 2, :], H16r[:, 0, :, m0:m0 + 64], start=first, stop=False),
                     reads=[wr, "HB16_%d" % s_], writes=["PS%d" % tb])
                S.op("pe", _call("matmul", rm(bank(tb)), tab[:, d, qq, 3, :], H16r[:, 1, :, m0:m0 + 64], start=False, stop=last),
                     reads=[wr, "HB16_%d" % s_], writes=["PS%d" % tb])

        def rec_gelu(c):
            for tb, (t0, tl, ci) in enumerate(TBS):
                g0, g1, g2 = GTMP
                nat = lambda ap_: ap_.rearrange("p (m r) -> p m r", r=8)
                S.op("dve", _call("scalar_tensor_tensor", nat(g0), nat(UT[:, c, t0:t0 + 512]), vcol("s5d%d" % e_, c),
                                  bank(tb).rearrange("p (r m) -> p m r", r=8), ALU.mult, ALU.add),
                     reads=["PS%d" % tb, "UT%d_%d" % (c, tb), "VEC"], writes=["G0"])
                S.op("act", _call("activation", g1, g0, AF.Square), reads=["G0"], writes=["G1"])
                S.op("dve", _call("tensor_scalar", g1, g1, 0.044715, 1.0, ALU.mult, ALU.add), reads=["G1"], writes=["G1"])
                S.op("dve", _call("tensor_tensor", g1, g1, g0, ALU.mult), reads=["G1", "G0"], writes=["G1"])
                S.op("act", _call("activation", g2, g1, AF.Sigmoid, scale=2.0 * math.sqrt(2.0 / math.pi)), reads=["G1"], writes=["G2"])
                S.op("dve", _call("tensor_tensor", GT[:, c, t0:t0 + 512], g2, g0, ALU.mult), reads=["G2", "G0"], writes=["GT%d_%d" % (c, tb)])

        NJ_ = len(jobs)
        for p_ in range(NJ_ // 2):
            rec_bu(2 * p_)
            rec_bu(2 * p_ + 1)
            A_ = rec_scan(2 * p_)
            B_ = rec_scan(2 * p_ + 1)
            for i_ in range(max(len(A_), len(B_))):
                if i_ < len(A_):
                    S.op("dve", A_[i_][0], reads=A_[i_][1], writes=A_[i_][2])
                if i_ < len(B_):
                    S.op("dve", B_[i_][0], reads=B_[i_][1], writes=B_[i_][2])
            rec_y(2 * p_)
            rec_y(2 * p_ + 1)
            if p_ % 4 == 3 and p_ < NJ_ // 2 - 1:
                rec_gelu(p_ // 4)
        rec_gelu(3)
        S.dma("sp", _call("dma_start", out=DR("ns5")[e_:e_ + 1].rearrange("o p n -> p (o n)"), in_=FIN.rearrange("p d c s q -> p (d c s q)")), reads=["FIN"], key="oFIN")

    def odd_mixer(l):
        o_ = l // 2
        scale = (64 + 32) ** -0.5
        HT = r3(SC(0, 8 * NT), 8)
        CAT = HT
        CQT = r3(SC(24576, 2 * NT), 2)
        CKVT = SC(30720, 2048)
        KRT = SC(34816, 2048)
        VTOK = r3(SC(38912, 16 * 1024), 16)
        QNR = [SC(71680 + i * 3072, NT) for i in range(2)]
        KN2 = [SC(77824 + i * 4096, 2048) for i in range(2)]
        ROPE = r3(SC(86016, 2 * 1024, F32), 2)
        CQF = r3(SC(94208, 2 * 512, F32), 2)
        CKVF = SC(98304, 512, F32)
        KRF = SC(100352, 512, F32)
        SQ = SC(102400, 512)
        PT = [SC(103424 + i * 1024, 512) for i in range(4)]
        REC = SC(107520, 512, F32)
        T1 = SC(109568, 512, F32)
        T2 = SC(111616, 512, F32)
        T3 = SC(113664, 512, F32)
        pre_norm(l, 1, HT, 38912)
        tap_bf16("h1_%d" % l, HT, ["HT%d_%d" % (c, tb) for c in range(8) for tb in range(3)])
        S.barrier()
        S.dma("sp", _call("dma_start", out=ROPE[0:32], in_=DR("ropeM")), writes=["ROPE"], key="rope")
        S.dma("pool", _call("dma_start", out=CKVT[:, 512:1024], in_=DR("cckvT")[o_:o_ + 1].rearrange("o p t -> p (o t)")), writes=["CKVc"], key="CKVc")
        for i_ in range(2):
            S.dma("pool", _call("dma_start", out=KN2[i_][64:96, 512:1024], in_=DR("ckrT")[o_:o_ + 1].rearrange("o p t -> p (o t)")), writes=["KRc"], key="KRc%d" % i_)
            S.op("pool", _call("memset", KN2[i_][96:128, :], 0.0), writes=["KNZ"])
            S.op("pool", _call("memset", QNR[i_][96:128, :], 0.0), writes=["KNZ"])
        w, wr = wtile()
        w3 = r3(w, 8)[:, :, 0:416]
        wdma(w3, DR("w_in_odd")[o_:o_ + 1].rearrange("o (kc p) n -> p (o kc) n", p=128), wr)
        ws_, wsr = wtile()
        ws3 = r3(ws_, 8)[:, :, 0:32]
        wdma(ws3, DR("w_kr_sw")[o_:o_ + 1].rearrange("o (kc p) n -> p (o kc) n", p=128), wsr)
        qg0 = VC["qng%d" % o_][0]
        kvg0 = VC["kvg%d" % o_][0]
        nckv_dst = DR("nckvT")[o_:o_ + 1].rearrange("o p t -> p (o t)")
        nkr_dst = DR("nkrT")[o_:o_ + 1].rearrange("o p t -> p (o t)")
        for tb, (t0, tl, ci) in enumerate(TBS):
            kcol = t0 if tb == 0 else 512 + t0
            for cc in range(2):
                for kc in range(8):
                    S.op("pe", _call("matmul", bank(cc), w3[:, kc, cc * 128:(cc + 1) * 128], HT[:, kc, t0:t0 + 512], start=(kc == 0), stop=(kc == 7)),
                         reads=[wr, "HT%d_%d" % (kc, tb)], writes=["PS%d" % cc])
                S.op("act", _call("activation", CQF[:, cc, :], bank(cc), AF.Identity), reads=["PS%d" % cc], writes=["CQF%d" % cc])
                S.op("act", _call("activation", SQ, bank(cc), AF.Square), reads=["PS%d" % cc], writes=["SQa"])
                S.op("pe", _call("matmul", bank(6), ONES, SQ, start=(cc == 0), stop=(cc == 1)), reads=["SQa", "ONES"], writes=["PS6"])
            rstd_from(bank(6), REC, ["PS6"], "REC", 1.0 / 256)
            for cc in range(2):
                S.op("dve", _call("tensor_tensor", CQF[:, cc, :], CQF[:, cc, :], REC, ALU.mult), reads=["CQF%d" % cc, "REC"], writes=["CQF%d" % cc])
                S.op("act", _call("activation", CQT[:, cc, t0:t0 + 512], CQF[:, cc, :], AF.Identity, scale=VEC[:, qg0 + cc:qg0 + cc + 1]),
                     reads=["CQF%d" % cc, "VEC"], writes=["CQT%d" % tb])
            for kc in range(8):
                S.op("pe", _call("matmul", bank(2), w3[:, kc, 256:384], HT[:, kc, t0:t0 + 512], start=(kc == 0), stop=(kc == 7)),
                     reads=[wr, "HT%d_%d" % (kc, tb)], writes=["PS2"])
            S.op("act", _call("activation", CKVF, bank(2), AF.Identity), reads=["PS2"], writes=["CKVF"])
            S.op("act", _call("activation", SQ, bank(2), AF.Square), reads=["PS2"], writes=["SQa"])
            S.op("pe", _call("matmul", bank(6), ONES, SQ, start=True, stop=True), reads=["SQa", "ONES"], writes=["PS6"])
            rstd_from(bank(6), REC, ["PS6"], "REC", 1.0 / 128)
            S.op("dve", _call("tensor_tensor", CKVF, CKVF, REC, ALU.mult), reads=["CKVF", "REC"], writes=["CKVF"])
            S.op("dve", _call("tensor_scalar", CKVF, CKVF, VEC[:, kvg0:kvg0 + 1], None, ALU.mult), reads=["CKVF", "VEC"], writes=["CKVF"])
            S.op("act", _call("activation", CKVT[:, kcol:kcol + 512], CKVF, AF.Identity), reads=["CKVF"], writes=["CKV%d" % tb])
            if tb == 0:
                S.dma("sp", _call("dma_start", out=nckv_dst, in_=CKVF), reads=["CKVF"], key="oCKV")
            for kc in range(8):
                S.op("pe", _call("matmul", bank(3, 512)[0:32], w3[:, kc, 384:416], HT[:, kc, t0:t0 + 512], start=(kc == 0), stop=(kc == 7)),
                     reads=[wr, "HT%d_%d" % (kc, tb)], writes=["PS3"])
            if tb == 0:
                S.op("act", _call("activation", KRF[0:32], bank(3)[0:32], AF.Identity), reads=["PS3"], writes=["KRF"])
                for i_ in range(2):
                    S.op("act", _call("activation", KN2[i_][64:96, kcol:kcol + 512], bank(3)[0:32], AF.Identity), reads=["PS3"], writes=["KR%d" % tb])
                S.dma("sp", _call("dma_start", out=nkr_dst, in_=KRF[0:32]), reads=["KRF"], key="oKR")
            else:
                for kc in range(8):
                    S.op("pe", _call("matmul", bank(4)[0:32], ws3[:, kc, :], HT[:, kc, t0:t0 + 512], start=(kc == 0), stop=(kc == 7)),
                         reads=[wsr, "HT%d_%d" % (kc, tb)], writes=["PS4"])
                r0 = t0 - 512
                S.op("dve", _call("tensor_tensor", T1[0:32], bank(3)[0:32], ROPE[0:32, 0, r0:r0 + 512], ALU.mult), reads=["PS3", "ROPE"], writes=["T1"])
                S.op("dve", _call("tensor_tensor", T2[0:32], bank(4)[0:32], ROPE[0:32, 1, r0:r0 + 512], ALU.mult), reads=["PS4", "ROPE"], writes=["T2"])
                S.op("dve", _call("tensor_tensor", T3[0:32], T1[0:32], T2[0:32], ALU.add), reads=["T1", "T2"], writes=["T3"])
                for i_ in range(2):
                    S.op("act", _call("activation", KN2[i_][64:96, kcol:kcol + 512], T3[0:32], AF.Identity), reads=["T3"], writes=["KR%d" % tb])
        chk("mla_inproj")
        kv_reads = ["CKVc", "CKV0", "CKV1", "CKV2"]
        kr_reads = ["KRc", "KR0", "KR1", "KR2"]
        w, wr = wtile()
        wv = w[:, 0:1024]
        wdma(wv, DR("wkv_v")[o_:o_ + 1].rearrange("o p n -> p (o n)"), wr)
        for kt in range(16):
            for hf in range(2):
                pb = (kt * 2 + hf) % 2
                S.op("pe", _call("matmul", bank(pb), CKVT[:, kt * 128:(kt + 1) * 128], wv[:, hf * 512:(hf + 1) * 512], start=True, stop=True),
                     reads=[wr] + kv_reads, writes=["PS%d" % pb])
                S.op("act", _call("activation", VTOK[:, kt, hf * 512:(hf + 1) * 512], bank(pb), AF.Identity), reads=["PS%d" % pb], writes=["VTOK"])
        chk("mla_vtok")
        wq, wqr = wtile()
        wq3 = r3(wq, 2)[:, :, 0:1536]
        wdma(wq3, DR("wq_up")[o_:o_ + 1].rearrange("o (kc p) n -> p (o kc) n", p=128), wqr)
        wqs, wqsr = wtile()
        wqs3 = r3(wqs, 2)[:, :, 0:512]
        wdma(wqs3, DR("wq_up_sw")[o_:o_ + 1].rearrange("o (kc p) n -> p (o kc) n", p=128), wqsr)
        wk, wkr = wtile()
        wkn = wk[:, 0:1024]
        wdma(wkn, DR("wkv_kn")[o_:o_ + 1].rearrange("o p n -> p (o n)"), wkr)
        seqs = [(0, 256, [0, 1], 0), (256, 256, [2, 3], 256), (512, 1024, list(range(4, 16)), 512)]

        def prologue_steps(h):
            QNRh, KNh = QNR[h % 2], KN2[h % 2]
            qres, kres = "QNR%d" % (h % 2), "KN%d" % (h % 2)
            steps = []
            for tb, (t0, tl, ci) in enumerate(TBS):
                def st_qn(tb=tb, t0=t0):
                    for kc in range(2):
                        S.op("pe", _call("matmul", bank(7)[0:64], wq3[:, kc, h * 96:h * 96 + 64], CQT[:, kc, t0:t0 + 512], start=(kc == 0), stop=(kc == 1)),
                             reads=[wqr, "CQT%d" % tb], writes=["PS7"])
                    S.op("act", _call("activation", QNRh[0:64, t0:t0 + 512], bank(7)[0:64], AF.Identity), reads=["PS7"], writes=[qres])
                steps.append(st_qn)

                def st_qr(tb=tb, t0=t0):
                    for kc in range(2):
                        S.op("pe", _call("matmul", bank(7)[0:32], wq3[:, kc, h * 96 + 64:h * 96 + 96], CQT[:, kc, t0:t0 + 512], start=(kc == 0), stop=(kc == 1)),
                             reads=[wqr, "CQT%d" % tb], writes=["PS7"])
                    if tb == 0:
                        S.op("act", _call("activation", QNRh[64:96, t0:t0 + 512], bank(7)[0:32], AF.Identity), reads=["PS7"], writes=[qres])
                    else:
                        r0 = t0 - 512
                        S.op("dve", _call("tensor_tensor", T1[0:32], bank(7)[0:32], ROPE[0:32, 0, r0:r0 + 512], ALU.mult), reads=["PS7", "ROPE"], writes=["T1"])
                steps.append(st_qr)
                if tb > 0:
                    def st_qs(tb=tb, t0=t0):
                        for kc in range(2):
                            S.op("pe", _call("matmul", bank(7)[0:32], wqs3[:, kc, h * 32:h * 32 + 32], CQT[:, kc, t0:t0 + 512], start=(kc == 0), stop=(kc == 1)),
                                 reads=[wqsr, "CQT%d" % tb], writes=["PS7"])
                        r0 = t0 - 512
                        S.op("dve", _call("tensor_tensor", T2[0:32], bank(7)[0:32], ROPE[0:32, 1, r0:r0 + 512], ALU.mult), reads=["PS7", "ROPE"], writes=["T2"])
                        S.op("dve", _call("tensor_tensor", T3[0:32], T1[0:32], T2[0:32], ALU.add), reads=["T1", "T2"], writes=["T3"])
                        S.op("act", _call("activation", QNRh[64:96, t0:t0 + 512], T3[0:32], AF.Identity), reads=["T3"], writes=[qres])
                    steps.append(st_qs)
            for kb in range(4):
                def st_kn(kb=kb):
                    S.op("pe", _call("matmul", bank(7)[0:64], wkn[:, h * 64:(h + 1) * 64], CKVT[:, kb * 512:(kb + 1) * 512], start=True, stop=True),
                         reads=[wkr] + kv_reads, writes=["PS7"])
                    S.op("act", _call("activation", KNh[0:64, kb * 512:(kb + 1) * 512], bank(7)[0:64], AF.Identity), reads=["PS7"], writes=[kres])
                steps.append(st_kn)
            return steps

        for st in prologue_steps(0):
            st()
        tasks, epis = [], []
        for h in range(16):
            QNRh, KNh = QNR[h % 2], KN2[h % 2]
            qres, kres = "QNR%d" % (h % 2), "KN%d" % (h % 2)
            nxt = prologue_steps(h + 1) if h + 1 < 16 else []
            hp = h % 2
            tcount = 0
            for (q0, qlen, vtiles, k0) in seqs:
                nqb = max(1, qlen // 512)
                N = min(qlen, 512)
                for qb in range(nqb):
                    qs = q0 + qb * 512
                    g = len(epis)
                    for ki, vt_i in enumerate(vtiles):
                        kc0 = k0 + ki * 128

                        def A(ps, kc0=kc0, qs=qs, N=N, QNRh=QNRh, KNh=KNh, qres=qres, kres=kres):
                            S.op("pe", _call("matmul", bank(ps, N), KNh[:, kc0:kc0 + 128], QNRh[:, qs:qs + N], start=True, stop=True),
                                 reads=kr_reads + ["KNZ", kres, qres], writes=["PS%d" % ps])

                        def C(pt, ptr, ob, sb, first, last, vt_i=vt_i, N=N, hb=(h // 2) * 128):
                            S.op("pe", _call("matmul", bank(ob, N), VTOK[:, vt_i, hb:hb + 128], pt[:, 0:N], start=first, stop=last),
                                 reads=[ptr, "VTOK"], writes=["PS%d" % ob])
                            S.op("pe", _call("matmul", bank(sb, N), ONES, pt[:, 0:N], start=first, stop=last),
                                 reads=[ptr, "ONES"], writes=["PS%d" % sb])
                        pre = nxt[tcount] if tcount < len(nxt) else None
                        tcount += 1
                        tasks.append((g, ki, len(vtiles), N, A, C, pre))

                    def E(ob, sb, N, qs=qs, a0=64 * hp, a1=64 * hp + 64, hc=h // 2):
                        S.op("dve", _call("reciprocal", REC[:, 0:N], bank(sb, N)), reads=["PS%d" % sb], writes=["REC"])
                        S.op("dve", _call("tensor_tensor", CAT[a0:a1, hc, qs:qs + N], bank(ob, N)[a0:a1], REC[a0:a1, 0:N], ALU.mult),
                             reads=["PS%d" % ob, "REC"], writes=["CAT"])
                        return None
                    epis.append(E)
            assert tcount >= len(nxt)
        attn_pipeline(tasks, epis, PT, scale)
        S.barrier()
        chk("mla_attn")
        osrc = DR("w_out_odd")[o_:o_ + 1].rearrange("o (kc p) n -> p (o kc) n", p=128)
        wts = []
        for half in range(2):
            w, wr = wtile()
            w3o = r3(w, 8)
            wdma(w3o, osrc[:, :, half * 512:(half + 1) * 512], wr)
            wts.append((w3o, wr))
        for tb, (t0, tl, ci) in enumerate(TBS):
            def yfn(oc, tb=tb, t0=t0):
                w3o, wr = wts[oc // 4]
                oi = oc % 4
                pb = oc % 2
                for kc in range(8):
                    S.op("pe", _call("matmul", bank(pb), w3o[:, kc, oi * 128:(oi + 1) * 128], CAT[:, kc, t0:t0 + 512], start=(kc == 0), stop=(kc == 7)),
                         reads=[wr, "CAT"], writes=["PS%d" % pb])
                return bank(pb), "PS%d" % pb
            post_norm_res(l, 1, tb, 38912, 55296, yfn)
        S.barrier()

    try:
        chk("load")
        for l in range(n_layers):
            compute_mod(l)
            chk("mod")
            if l % 2 == 0:
                even_mixer(l)
            else:
                odd_mixer(l)
            tap_f32("xmix_%d" % l, XT, [xres(c, tb) for c in range(8) for tb in range(3)])
            chk("mixer")
            ffn(l)
            tap_f32("xffn_%d" % l, XT, [xres(c, tb) for c in range(8) for tb in range(3)])
    except _Stop:
        pass
    ydst = DR("yT").rearrange("(c p) t -> p c t", p=128)
    for c in range(8):
        S.dma("sp" if c % 2 == 0 else "act", _call("dma_start", out=ydst[:, c, :], in_=XT[:, c, :]),
              reads=[xres(c, tb) for tb in range(3)], key="out")


def _rope_tables():
    def ang(n, rot):
        rows = n // 64
        row = np.repeat(np.arange(rows, dtype=np.float32), 64)
        col = np.tile(np.arange(64, dtype=np.float32), rows)
        nf = rot // 4
        inv = (10000.0 ** (-np.arange(nf, dtype=np.float32) / nf)).astype(np.float32)
        return np.concatenate([row[:, None] * inv, col[:, None] * inv], axis=-1).astype(np.float32)
    a = ang(1024, 64)
    cosd, sind = np.cos(a), np.sin(a)
    ropeD = np.zeros((128, 2, 1024), np.float32)
    for p in range(128):
        d = p % 64
        j = d % 32
        ropeD[p, 0] = cosd[:, j]
        ropeD[p, 1] = -sind[:, j] if d < 32 else sind[:, j]
    a = ang(1024, 32)
    cosm, sinm = np.cos(a), np.sin(a)
    ropeM = np.zeros((32, 2, 1024), np.float32)
    for p in range(32):
        j = p % 16
        ropeM[p, 0] = cosm[:, j]
        ropeM[p, 1] = -sinm[:, j] if p < 16 else sinm[:, j]
    return ropeD, ropeM


def _swap_halves(w, half):
    return np.concatenate([w[..., half:], w[..., :half]], axis=-1)


def _prep_shared(inp):
    f = np.float32
    sh = {}
    sh["cst"] = np.tile(np.array(EXPS + [0], f)[None, :], (128, 1))
    sh["ropeD"], sh["ropeM"] = _rope_tables()
    sh["w_mod"] = inp["w_mod"]
    sh["wg"], sh["wu"], sh["wd"] = inp["w_ffn_gate"], inp["w_ffn_up"], inp["w_ffn_down"]
    wie = inp["w_in_even"]
    sh["w_in_even"] = wie
    qk = wie[:, :, 512:1536].reshape(2, 1024, 2, 4, 2, 64)
    sh["w_in_even_sw"] = np.ascontiguousarray(_swap_halves(qk, 32).reshape(2, 1024, 1024))
    sh["w_out_even"] = inp["w_out_even"]
    sh["glu_w"] = inp["s5_glu_w"]
    tab = np.zeros((2, 4, 128, 2, 4, 4, 128), f)
    for e in range(2):
        for d in range(2):
            for c in range(4):
                for qq in range(4):
                    for gi in range(2):
                        g = 8 * c + 2 * qq + gi
                        r0 = (2 * qq + gi) * 16
                        tab[e, c, r0:r0 + 16, d, qq, 0, gi * 64:(gi + 1) * 64] = inp["s5_b_re"][e, d, g].T
                        tab[e, c, r0:r0 + 16, d, qq, 1, gi * 64:(gi + 1) * 64] = inp["s5_b_im"][e, d, g].T
                        tab[e, c, gi * 64:(gi + 1) * 64, d, qq, 2, r0:r0 + 16] = inp["s5_c_re"][e, d, g].T
                        tab[e, c, gi * 64:(gi + 1) * 64, d, qq, 3, r0:r0 + 16] = inp["s5_c_im"][e, d, g].T
    sh["s5tab"] = tab.reshape(2, 4, 128, 4096)
    sh["dlam"] = np.stack([inp["diff_lam_q1"], inp["diff_lam_k1"], inp["diff_lam_q2"], inp["diff_lam_k2"]], axis=1).astype(f)
    wio = inp["w_in_odd"]
    sh["w_in_odd"] = wio
    sh["w_kr_sw"] = np.ascontiguousarray(_swap_halves(wio[:, :, 384:416], 16))
    wq = inp["mla_w_q_up"]
    sh["wq_up"] = wq
    sh["wq_up_sw"] = np.ascontiguousarray(_swap_halves(wq.reshape(2, 256, 16, 96)[..., 64:96], 16).reshape(2, 256, 512))
    wkv = inp["mla_w_kv_up"].reshape(2, 128, 16, 128)
    sh["wkv_kn"] = np.ascontiguousarray(wkv[..., :64].reshape(2, 128, 1024))
    sh["wkv_v"] = np.ascontiguousarray(wkv[..., 64:].reshape(2, 128, 1024))
    sh["w_out_odd"] = inp["w_out_odd"]
    return sh


def _prep_core(inp, c):
    f = np.float32
    m = {}
    x = np.concatenate([inp["x_prompt"][2 * c], inp["x_prompt"][2 * c + 1], inp["x_sample"][c]], axis=0)
    m["xT"] = np.ascontiguousarray(x.T)
    vec = np.zeros((128, NV), f)

    def put(name, arr):
        c0, n = VC[name]
        vec[:, c0:c0 + n] = np.asarray(arr, f).reshape(n, 128).T
    cond = np.zeros((8, 2, 128), f)
    cond[:, 0, :] = inp["c_ctx"].reshape(8, 128)
    cond[:, 1, :] = inp["c"][c].reshape(8, 128)
    put("cond", cond.reshape(16 * 128))
    for l in range(4):
        put("bmod%d" % l, inp["b_mod"][l])
        put("gpm%d" % l, inp["g_pre_mix"][l])
        put("gqm%d" % l, inp["g_post_mix"][l])
        put("gpf%d" % l, inp["g_pre_ffn"][l])
        put("gqf%d" % l, inp["g_post_ffn"][l])
    for e in range(2):
        put("s5d%d" % e, inp["s5_d"][e])
        put("glub%d" % e, inp["s5_glu_b"][e])
        put("subg%d" % e, inp["diff_subln_g"][e])
    for o in range(2):
        put("qng%d" % o, inp["mla_q_norm_g"][o])
        put("kvg%d" % o, inp["mla_kv_norm_g"][o])
    m["vecT"] = vec
    s5p = np.zeros((2, 128, 5, 32), f)
    for e in range(2):
        for d in range(2):
            for q in range(16):
                for gi in range(2):
                    g = 2 * q + gi
                    j = d * 16 + q
                    sl = slice(gi * 64, gi * 64 + 64)
                    s5p[e, sl, 0, j] = inp["s5_lam_re"][e, d, g]
                    s5p[e, sl, 1, j] = inp["s5_lam_im"][e, d, g]
                    s5p[e, sl, 2, j] = inp["s5_log_dt"][e, d, g]
                    s5p[e, sl, 3, j] = inp["state_s5_re"][c, e, d, g]
                    s5p[e, sl, 4, j] = inp["state_s5_im"][c, e, d, g]
    m["s5p"] = s5p
    m["cdkT"] = np.ascontiguousarray(inp["cache_diff_k"][c].reshape(2, 512, 512).transpose(0, 2, 1))
    m["cdv"] = np.ascontiguousarray(inp["cache_diff_v"][c].reshape(2, 512, 512))
    m["cckvT"] = np.ascontiguousarray(inp["cache_mla_ckv"][c].transpose(0, 2, 1))
    m["ckrT"] = np.ascontiguousarray(inp["cache_mla_krope"][c].transpose(0, 2, 1))
    return m


_NC_CACHE = {}


def kernel(**inputs):
    inp = {k: np.asarray(v) for k, v in inputs.items()}
    if "nc" not in _NC_CACHE:
        _NC_CACHE["nc"] = build(4)
    nc = _NC_CACHE["nc"]
    sh = _prep_shared(inp)
    in_maps = []
    for c in range(8):
        m = dict(sh)
        m.update(_prep_core(inp, c))
        in_maps.append({k: np.ascontiguousarray(v, dtype=np.float32) for k, v in m.items()})
    res = run_bass_kernel_spmd(nc, in_maps, core_ids=list(range(8)))
    R = res.results
    f = np.float32
    y_prompt = np.zeros((16, 256, 1024), f)
    y_sample = np.zeros((8, 1024, 1024), f)
    ns_re = np.zeros((16, 2, 2, 32, 64), f)
    ns_im = np.zeros((16, 2, 2, 32, 64), f)
    ndk = np.zeros((16, 2, 256, 4, 2, 64), f)
    ndv = np.zeros((16, 2, 256, 4, 128), f)
    nckv = np.zeros((16, 2, 256, 128), f)
    nkr = np.zeros((16, 2, 256, 32), f)
    for c in range(8):
        r = R[c]
        y = r["yT"].T
        y_prompt[2 * c] = y[0:256]
        y_prompt[2 * c + 1] = y[256:512]
        y_sample[c] = y[512:1536]
        ns5 = r["ns5"].reshape(2, 2, 64, 2, 2, 2, 16)
        for s in range(2):
            t = ns5[:, :, :, :, :, s, :].transpose(0, 3, 4, 5, 1, 2)
            ns_re[2 * c + s] = t[:, :, 0].reshape(2, 2, 32, 64)
            ns_im[2 * c + s] = t[:, :, 1].reshape(2, 2, 32, 64)
            ndk[2 * c + s] = r["nkT"][:, :, s * 256:(s + 1) * 256].transpose(0, 2, 1).reshape(2, 256, 4, 2, 64)
            ndv[2 * c + s] = r["nv"][:, s * 256:(s + 1) * 256, :].reshape(2, 256, 4, 128)
            nckv[2 * c + s] = r["nckvT"][:, :, s * 256:(s + 1) * 256].transpose(0, 2, 1)
            nkr[2 * c + s] = r["nkrT"][:, :, s * 256:(s + 1) * 256].transpose(0, 2, 1)
    return (y_prompt, y_sample, ns_re, ns_im, ndk, ndv, nckv, nkr)
```

```python
import math
import numpy as np
import concourse.bass as bass
import concourse.mybir as mybir
from concourse.bass_utils import run_bass_kernel_spmd

F32 = mybir.dt.float32
BF16 = mybir.dt.bfloat16
AF = mybir.ActivationFunctionType
ALU = mybir.AluOpType
AX = mybir.AxisListType

ENGS = ("pe", "act", "dve", "pool", "sp")


class _Op:
    __slots__ = ("fn", "deps", "dma_key", "sig")

    def __init__(self, fn, deps, dma_key):
        self.fn = fn
        self.deps = deps
        self.dma_key = dma_key
        self.sig = None


class Sched:
    def __init__(self, nc):
        self.nc = nc
        self.ops = {e: [] for e in ENGS}
        self.ncomp = {e: 0 for e in ENGS}
        self.last_w = {}
        self.readers = {}
        self.dma_cnt = {}
        self.dma_keys = []
        self.bar_toks = set()

    def _mk(self, eng, fn, reads, writes, dma_key):
        writes = tuple(writes) + tuple(r for r in reads if r.startswith("PS") and r not in writes)
        deps = set()
        for r in reads:
            w = self.last_w.get(r)
            if w is not None:
                deps.add(w)
        for r in writes:
            w = self.last_w.get(r)
            if w is not None:
                deps.add(w)
            for rd in self.readers.get(r, ()):
                deps.add(rd)
        if eng == "pool" and not all(w.startswith("W") for w in writes):
            deps |= self.bar_toks
        op = _Op(fn, deps, dma_key)
        if dma_key is None:
            self.ncomp[eng] += 1
            tok = ("c", eng, self.ncomp[eng])
        else:
            if dma_key not in self.dma_cnt:
                self.dma_cnt[dma_key] = 0
                self.dma_keys.append(dma_key)
            self.dma_cnt[dma_key] += 16
            tok = ("d", dma_key, self.dma_cnt[dma_key])
        op.sig = tok
        op.deps.discard(tok)
        for r in writes:
            self.last_w[r] = tok
            self.readers[r] = []
        for r in reads:
            self.readers.setdefault(r, []).append(tok)
        self.ops[eng].append((eng, op))
        return op

    def op(self, eng, fn, reads=(), writes=()):
        return self._mk(eng, fn, tuple(reads), tuple(writes), None)

    def dma(self, eng, fn, reads=(), writes=(), key=None):
        assert eng in ("sp", "act", "pool")
        return self._mk(eng, fn, tuple(reads), tuple(writes), key)

    def barrier(self):
        toks = set()
        for e in ENGS:
            if self.ncomp[e] > 0:
                toks.add(("c", e, self.ncomp[e]))
        for k, v in self.dma_cnt.items():
            if not (isinstance(k, str) and k.startswith("W")):
                toks.add(("d", k, v))
        self.bar_toks = toks
        for e in ("pe", "act", "dve", "sp"):
            op = _Op(None, set(toks), None)
            op.sig = None
            self.ops[e].append((e, op))

    def emit(self, final_wait=True):
        nc = self.nc
        import contextlib
        with contextlib.ExitStack() as st:
            csem = {e: st.enter_context(nc.semaphore("s_" + e)) for e in ENGS}
            dsem = {}
            for i, k in enumerate(self.dma_keys):
                dsem[k] = st.enter_context(nc.semaphore("d%d" % i))
            block = st.enter_context(nc.Block())
            engobj = {}

            def run(ename, eng):
                seen_c = {e: 0 for e in ENGS}
                seen_d = {}
                issued = 0
                for (_, op) in self.ops[ename]:
                    need_c = {}
                    need_d = {}
                    for d in op.deps:
                        if d[0] == "c":
                            if d[2] > need_c.get(d[1], 0):
                                need_c[d[1]] = d[2]
                        else:
                            if d[2] > need_d.get(d[1], 0):
                                need_d[d[1]] = d[2]
                    for e2, v in need_c.items():
                        if e2 == ename and ename == "pe":
                            continue
                        if v > seen_c[e2]:
                            eng.wait_ge(csem[e2], v)
                            seen_c[e2] = v
                    for k2, v in need_d.items():
                        if v > seen_d.get(k2, 0):
                            eng.wait_ge(dsem[k2], v)
                            seen_d[k2] = v
                    if op.fn is None:
                        continue
                    ins = op.fn(eng)
                    if op.sig[0] == "c":
                        issued += 1
                        ins.then_inc(csem[ename], 1)
                        seen_c[ename] = max(seen_c[ename], 0)
                    else:
                        ins.then_inc(dsem[op.sig[1]], 16)
                if final_wait and ename == "sp":
                    for k2, v in self.dma_cnt.items():
                        if v > seen_d.get(k2, 0):
                            eng.wait_ge(dsem[k2], v)
                    for e2 in ENGS:
                        if self.ncomp[e2] > seen_c[e2]:
                            eng.wait_ge(csem[e2], self.ncomp[e2])

            @block.tensor
            def _(e):
                run("pe", e)

            @block.scalar
            def _(e):
                run("act", e)

            @block.vector
            def _(e):
                run("dve", e)

            @block.gpsimd
            def _(e):
                run("pool", e)

            @block.sync
            def _(e):
                run("sp", e)


D_MODEL = 1024
NT = 1536
DFF = 2816
NJ = 22
TBS = [(0, 512, 0), (512, 512, 1), (1024, 512, 1)]
EPS = 1e-6
EXPS = [1, 2, 3, 4, 5, 6, 7, 8, 16, 32, 64, 128, 256, 512, 1024]
NV = 384

XT_O = 0
VEC_O = 49152
ONES_O = 51200
SCT_O = 51456
MODT_O = 51520
DER_O = 51904
LAMS_O = 52288
CST_O = 53248
WP_O = 54272
SCR_O = 87040
ARENA_B = 204800

IN_SHAPES = {
    "xT": (1024, 1536), "vecT": (128, NV), "cst": (128, 16), "s5p": (2, 128, 5, 32), "dlam": (2, 4, 64),
    "ropeD": (128, 2, 1024), "ropeM": (32, 2, 1024),
    "w_mod": (4, 1024, 6144), "wg": (4, 1024, 2816), "wu": (4, 1024, 2816), "wd": (4, 2816, 1024),
    "w_in_even": (2, 1024, 2048), "w_in_even_sw": (2, 1024, 1024), "w_out_even": (2, 1024, 1024),
    "glu_w": (2, 512, 512), "s5tab": (2, 4, 128, 4096), "cdkT": (2, 512, 512), "cdv": (2, 512, 512),
    "w_in_odd": (2, 1024, 416), "w_kr_sw": (2, 1024, 32), "wq_up": (2, 256, 1536), "wq_up_sw": (2, 256, 512),
    "wkv_kn": (2, 128, 1024), "wkv_v": (2, 128, 1024), "w_out_odd": (2, 1024, 1024),
    "cckvT": (2, 128, 512), "ckrT": (2, 32, 512),
}
OUT_SHAPES = {
    "yT": (1024, 1536), "ns5": (2, 128, 128), "nkT": (2, 512, 512), "nv": (2, 512, 512),
    "nckvT": (2, 128, 512), "nkrT": (2, 32, 512),
}


def vec_cols():
    cols = {}
    cur = [0]

    def add(name, n):
        cols[name] = (cur[0], n)
        cur[0] += n
    add("cond", 16)
    for l in range(4):
        add("bmod%d" % l, 48)
        add("gpm%d" % l, 8)
        add("gqm%d" % l, 8)
        add("gpf%d" % l, 8)
        add("gqf%d" % l, 8)
    for e in range(2):
        add("s5d%d" % e, 4)
        add("glub%d" % e, 4)
        add("subg%d" % e, 1)
    for o in range(2):
        add("qng%d" % o, 2)
        add("kvg%d" % o, 1)
    assert cur[0] <= NV
    return cols


VC = vec_cols()


class _Stop(Exception):
    pass


STOP_AT = [None]
PHASE_MARKS = []
S5_LIMIT = [None]
S5_DIRS = [(0, 1)]


def _call(method, *args, **kwargs):
    return lambda e: getattr(e, method)(*args, **kwargs)


def build(n_layers=4, taps=()):
    nc = bass.Bass("TRN2", target_bir_lowering=False)
    dram = {}
    for name, shape in IN_SHAPES.items():
        dram[name] = nc.dram_tensor(name, list(shape), F32, kind="ExternalInput")
    for name, shape in OUT_SHAPES.items():
        dram[name] = nc.dram_tensor(name, list(shape), F32, kind="ExternalOutput")
    tapd = {}
    for t in taps:
        tapd[t] = nc.dram_tensor("tap_" + t, [1024, 1536], F32, kind="ExternalOutput")
    import contextlib
    with contextlib.ExitStack() as st:
        arena = st.enter_context(nc.sbuf_tensor("arena", [128, ARENA_B // 2], BF16))
        psum = st.enter_context(nc.psum_tensor("psum", [128, 4096], F32))
        S = Sched(nc)
        _build_body(nc, S, dram, tapd, arena, psum, n_layers)
        S.emit()
    return nc


def _build_body(nc, S, dram, tapd, arena, psum, n_layers):
    def chk(name):
        PHASE_MARKS.append((name, {e: sum(1 for (_, o) in S.ops[e] if o.fn is not None and o.dma_key is None) for e in ("pe", "act", "dve")}))
        if STOP_AT[0] == name:
            raise _Stop()

    def V(off, n, dt=BF16, p0=0, p1=128):
        if dt == BF16:
            return arena[p0:p1, off // 2: off // 2 + n]
        return arena[p0:p1, off // 2: off // 2 + 2 * n].bitcast(F32)

    def r3(ap, a):
        return ap.rearrange("p (a b) -> p a b", a=a)

    def bank(i, n=512):
        return psum[:, i * 512: i * 512 + n]

    def DR(name):
        return dram[name].ap()

    XT = r3(V(XT_O, 8 * NT, F32), 8)
    VEC = V(VEC_O, NV, F32)
    ONES = V(ONES_O, 128)
    SCT = r3(V(SCT_O, 16), 8)
    MODT = r3(V(MODT_O, 96, F32), 48)
    DER = V(DER_O, 96, F32).rearrange("p (k c i) -> p k c i", k=6, c=8)
    LAMS = V(LAMS_O, 16, F32)
    CST = V(CST_O, 16, F32)
    WT = [V(WP_O + i * 8192, 4096) for i in range(4)]
    wcnt = [0]

    def wtile():
        i = wcnt[0] % 4
        wcnt[0] += 1
        return WT[i], "W%d" % i

    def SC(off, n, dt=BF16, p0=0, p1=128):
        return V(SCR_O + off, n, dt, p0, p1)

    def vcol(name, j=0, n=1):
        c0 = VC[name][0] + j
        return VEC[:, c0:c0 + n]

    def wdma(dst, src, wres, reads=()):
        S.dma("pool", _call("dma_start", out=dst, in_=src), reads=reads, writes=[wres], key=wres)

    xsrc = DR("xT").rearrange("(c p) t -> p c t", p=128)
    for c in range(8):
        S.dma("sp" if c % 2 == 0 else "act", _call("dma_start", out=XT[:, c, :], in_=xsrc[:, c, :]),
              writes=["XT%d_%d" % (c, tb) for tb in range(3)], key="xin%d" % c)
    S.dma("sp", _call("dma_start", out=VEC, in_=DR("vecT")), writes=["VEC"], key="vec")
    S.dma("sp", _call("dma_start", out=CST, in_=DR("cst")), writes=["CST"], key="cst")
    S.op("pool", _call("memset", ONES, 1.0), writes=["ONES"])
    S.op("act", _call("activation", SCT.rearrange("p a b -> p (a b)"), VEC[:, 0:16], AF.Silu), reads=["VEC"], writes=["SCT"])

    def xres(c, tb):
        return "XT%d_%d" % (c, tb)

    def rstd_from(psn, rstd, reads, wres, scale):
        S.op("act", _call("activation", rstd, psn, AF.Sqrt, bias=EPS, scale=scale), reads=reads, writes=[wres])
        S.op("dve", _call("reciprocal", rstd, rstd), reads=[wres], writes=[wres])

    def pre_norm(l, which, HT, tmp_off, PSN=7):
        ka, kb = (0, 0) if which == 1 else (3, 24)
        TMPN = r3(SC(tmp_off, 8 * 512, F32), 8)
        RSTD = SC(tmp_off + 16384, 512, F32)
        SQ = [SC(tmp_off + 18432 + i * 1024, 512) for i in range(2)]
        for tb, (t0, tl, ci) in enumerate(TBS):
            for c in range(8):
                sq = SQ[c % 2]
                S.op("act", _call("activation", sq, XT[:, c, t0:t0 + 512], AF.Square),
                     reads=[xres(c, tb)], writes=["SQ%d" % (c % 2)])
                S.op("pe", _call("matmul", bank(PSN), ONES, sq, start=(c == 0), stop=(c == 7)),
                     reads=["SQ%d" % (c % 2), "ONES"], writes=["PS%d" % PSN])
            rstd_from(bank(PSN), RSTD, ["PS%d" % PSN], "RSTD", 1.0 / D_MODEL)
            S.op("dve", _call("tensor_tensor", TMPN, XT[:, :, t0:t0 + 512], RSTD.unsqueeze(1).to_broadcast([128, 8, 512]), ALU.mult),
                 reads=["RSTD"] + [xres(c, tb) for c in range(8)], writes=["TMPN"])
            for c in range(8):
                S.op("act", _call("activation", HT[:, c, t0:t0 + 512], TMPN[:, c, :], AF.Identity,
                                                                       bias=MODT[:, kb + c, ci:ci + 1], scale=DER[:, ka, c, ci:ci + 1]),
                     reads=["TMPN", "MODT", "DER"], writes=["HT%d_%d" % (c, tb)])

    def post_norm_res(l, which, tb, ybuf_off, tmp_off, yfn, PSN=7):
        kg = 2 if which == 1 else 5
        t0, tl, ci = TBS[tb]
        YBUF = r3(SC(ybuf_off, 8 * 512, F32), 8)
        TMPN = r3(SC(tmp_off, 8 * 512, F32), 8)
        RSTD = SC(tmp_off + 16384, 512, F32)
        SQ = [SC(tmp_off + 18432 + i * 1024, 512) for i in range(2)]
        for oc in range(8):
            yp, yres = yfn(oc)
            sq = SQ[oc % 2]
            S.op("act", _call("activation", YBUF[:, oc, :], yp, AF.Identity), reads=[yres], writes=["YBUF%d" % oc])
            S.op("act", _call("activation", sq, yp, AF.Square), reads=[yres], writes=["SQ%d" % (oc % 2)])
            S.op("pe", _call("matmul", bank(PSN), ONES, sq, start=(oc == 0), stop=(oc == 7)),
                 reads=["SQ%d" % (oc % 2), "ONES"], writes=["PS%d" % PSN])
        rstd_from(bank(PSN), RSTD, ["PS%d" % PSN], "RSTD", 1.0 / D_MODEL)
        S.op("dve", _call("tensor_tensor", TMPN, YBUF, RSTD.unsqueeze(1).to_broadcast([128, 8, 512]), ALU.mult),
             reads=["RSTD"] + ["YBUF%d" % c for c in range(8)], writes=["TMPN"])
        for c in range(8):
            S.op("dve", _call("scalar_tensor_tensor", XT[:, c, t0:t0 + 512], TMPN[:, c, :], DER[:, kg, c, ci:ci + 1],
                                                               XT[:, c, t0:t0 + 512], ALU.mult, ALU.add),
                 reads=["TMPN", "DER", xres(c, tb)], writes=[xres(c, tb)])

    def compute_mod(l):
        wsrc = DR("w_mod")[l:l + 1].rearrange("o (kc p) n -> p (o kc) n", p=128)
        PSM = r3(bank(6, 96), 48)
        for wt in range(12):
            w, wr = wtile()
            w3 = r3(w, 8)
            wdma(w3, wsrc[:, :, wt * 512:(wt + 1) * 512], wr)
            for fi in range(4):
                f = wt * 4 + fi
                for kc in range(8):
                    S.op("pe", _call("matmul", PSM[:, f, :], w3[:, kc, fi * 128:(fi + 1) * 128], SCT[:, kc, :],
                                                                            start=(kc == 0), stop=(kc == 7)),
                         reads=[wr, "SCT"], writes=["PS6"])
        b0 = VC["bmod%d" % l][0]
        S.op("dve", _call("tensor_tensor", MODT, PSM, VEC[:, b0:b0 + 48].unsqueeze(2).to_broadcast([128, 48, 2]), ALU.add),
             reads=["PS6", "VEC"], writes=["MODT"])
        for k, (sc0, gname) in enumerate([(8, "gpm"), (16, "gqm"), (32, "gpf"), (40, "gqf")]):
            kk = [0, 2, 3, 5][k]
            g0 = VC["%s%d" % (gname, l)][0]
            gb = VEC[:, g0:g0 + 8].unsqueeze(2).to_broadcast([128, 8, 2])
            if k in (0, 2):
                S.op("dve", _call("tensor_scalar", DER[:, kk], MODT[:, sc0:sc0 + 8, :], 1.0, None, ALU.add),
                     reads=["MODT"], writes=["DER"])
                S.op("dve", _call("tensor_tensor", DER[:, kk], DER[:, kk], gb, ALU.mult), reads=["DER", "VEC"], writes=["DER"])
            else:
                S.op("dve", _call("tensor_tensor", DER[:, kk], MODT[:, sc0:sc0 + 8, :], gb, ALU.mult),
                     reads=["MODT", "VEC"], writes=["DER"])

    def tap_f32(name, src3, reads):
        if name in tapd:
            dst = tapd[name].ap().rearrange("(c p) t -> p c t", p=128)
            S.dma("sp", _call("dma_start", out=dst, in_=src3), reads=reads, key="tap")

    def tap_bf16(name, src3, reads, nchunk=8, c0=0):
        if name in tapd:
            dst = tapd[name].ap().rearrange("(c p) t -> p c t", p=128)[:, c0:c0 + nchunk, :]
            S.dma("pool", _call("dma_start", out=dst, in_=src3), reads=reads, key="tap")

    def attn_pipeline(tasks, epilogues, PT, scale, SK=2):
        n = len(tasks)
        deferred = []
        for idx in range(n + SK + 4):
            if idx < n:
                g, ki, nk, N, A, C = tasks[idx][:6]
                if len(tasks[idx]) > 6 and tasks[idx][6] is not None:
                    tasks[idx][6]()
                ps = idx % 3
                pt, ptr = PT[idx % 4], "PT%d" % (idx % 4)
                A(ps)
                S.op("act", _call("activation", pt[:, 0:N], bank(ps, N), AF.Exp, scale=scale), reads=["PS%d" % ps], writes=[ptr])
            jx = idx - SK
            if 0 <= jx < n:
                g, ki, nk, N, A, C = tasks[jx][:6]
                ob, sb = 3 + 2 * (g % 2), 4 + 2 * (g % 2)
                C(PT[jx % 4], "PT%d" % (jx % 4), ob, sb, ki == 0, ki == nk - 1)
                if ki == nk - 1:
                    d2 = epilogues[g](ob, sb, N)
                    if d2 is not None:
                        deferred.append((idx + 3, d2))
            for (due, fn) in [x for x in deferred if x[0] <= idx]:
                fn()
            deferred = [x for x in deferred if x[0] > idx]
        for (due, fn) in deferred:
            fn()

    def ffn(l):
        HT = r3(SC(0, 8 * NT), 8)
        ACTT = r3(SC(24576, NJ * NT), NJ)
        SL = [SC(92160 + i * 2048, 512, F32) for i in range(2)]
        pre_norm(l, 2, HT, 96256)
        gsrc = DR("wg")[l:l + 1].rearrange("o (kc p) n -> p (o kc) n", p=128)
        usrc = DR("wu")[l:l + 1].rearrange("o (kc p) n -> p (o kc) n", p=128)
        it = 0
        for jg in range(6):
            nj = 4 if jg < 5 else 2
            wgt, wgr = wtile()
            wut, wur = wtile()
            wg3 = r3(wgt, 8)[:, :, 0:nj * 128]
            wu3 = r3(wut, 8)[:, :, 0:nj * 128]
            wdma(wg3, gsrc[:, :, jg * 512: jg * 512 + nj * 128], wgr)
            wdma(wu3, usrc[:, :, jg * 512: jg * 512 + nj * 128], wur)
            for ji in range(nj):
                j = jg * 4 + ji
                for tb, (t0, tl, ci) in enumerate(TBS):
                    pg, pu = (it % 2) * 2, (it % 2) * 2 + 1
                    sl = SL[it % 2]
                    it += 1
                    for kc in range(8):
                        S.op("pe", _call("matmul", bank(pg), wg3[:, kc, ji * 128:(ji + 1) * 128], HT[:, kc, t0:t0 + 512],
                                                                                         start=(kc == 0), stop=(kc == 7)),
                             reads=[wgr, "HT%d_%d" % (kc, tb)], writes=["PS%d" % pg])
                    for kc in range(8):
                        S.op("pe", _call("matmul", bank(pu), wu3[:, kc, ji * 128:(ji + 1) * 128], HT[:, kc, t0:t0 + 512],
                                                                                         start=(kc == 0), stop=(kc == 7)),
                             reads=[wur, "HT%d_%d" % (kc, tb)], writes=["PS%d" % pu])
                    S.op("act", _call("activation", sl, bank(pg), AF.Silu), reads=["PS%d" % pg], writes=["SL%d" % (it % 2)])
                    S.op("dve", _call("tensor_tensor", ACTT[:, j, t0:t0 + 512], sl, bank(pu), ALU.mult),
                         reads=["SL%d" % (it % 2), "PS%d" % pu], writes=["ACT%d_%d" % (j, tb)])
        S.barrier()
        chk("ffn_gateup")
        dsrc = DR("wd")[l:l + 1].rearrange("o (j p) n -> p (o j) n", p=128)
        for tb, (t0, tl, ci) in enumerate(TBS):
            state = {}

            def yfn(oc, tb=tb, t0=t0, state=state):
                half, oi = oc // 4, oc % 4
                if oi == 0:
                    for jt in range(3):
                        njj = 8 if jt < 2 else 6
                        w, wr = wtile()
                        w3 = r3(w, 8)[:, 0:njj, :]
                        wdma(w3, dsrc[:, jt * 8: jt * 8 + njj, half * 512:(half + 1) * 512], wr)
                        for jj in range(njj):
                            j = jt * 8 + jj
                            for o2 in range(4):
                                S.op("pe", _call("matmul", bank(o2), w3[:, jj, o2 * 128:(o2 + 1) * 128], ACTT[:, j, t0:t0 + 512],
                                                                                      start=(j == 0), stop=(j == NJ - 1)),
                                     reads=[wr, "ACT%d_%d" % (j, tb)], writes=["PS%d" % o2])
                return bank(oi), "PS%d" % oi
            post_norm_res(l, 2, tb, 0, 96256, yfn)
        S.barrier()
        chk("ffn_down")

    def even_mixer(l):
        e_ = l // 2
        lam_init = 0.8 - 0.6 * math.exp(-0.3 * l)
        HT = r3(SC(0, 8 * NT), 8)
        GT = r3(SC(24576, 4 * NT), 4)
        UT = r3(SC(36864, 4 * NT), 4)
        pre_norm(l, 1, HT, 49152)
        tap_bf16("h1_%d" % l, HT, ["HT%d_%d" % (c, tb) for c in range(8) for tb in range(3)])
        S.barrier()
        chk("prenorm")
        wsrc = DR("w_in_even")[e_:e_ + 1].rearrange("o (kc p) n -> p (o kc) n", p=128)
        wsw = DR("w_in_even_sw")[e_:e_ + 1].rearrange("o (kc p) n -> p (o kc) n", p=128)

        def proj_fm(w3, wr, fc, tb, pb):
            t0 = TBS[tb][0]
            for kc in range(8):
                S.op("pe", _call("matmul", bank(pb), w3[:, kc, fc * 128:(fc + 1) * 128], HT[:, kc, t0:t0 + 512], start=(kc == 0), stop=(kc == 7)),
                     reads=[wr, "HT%d_%d" % (kc, tb)], writes=["PS%d" % pb])

        w, wr = wtile()
        w3 = r3(w, 8)
        wdma(w3, wsrc[:, :, 0:512], wr)
        it = 0
        for fc in range(4):
            for tb in range(3):
                pb = it % 2
                it += 1
                proj_fm(w3, wr, fc, tb, pb)
                t0 = TBS[tb][0]
                S.op("act", _call("activation", UT[:, fc, t0:t0 + 512], bank(pb), AF.Identity),
                     reads=["PS%d" % pb], writes=["UT%d_%d" % (fc, tb)])
        chk("uproj")
        s5(l, e_, UT, GT)
        S.barrier()
        tap_bf16("ut_%d" % l, UT, [], 4, 0)
        tap_bf16("gt_%d" % l, GT, [], 4, 0)
        tap_bf16("hb16_%d" % l, r3(SC(61440, 2 * NT), 2), [], 2, 0)
        chk("s5")
        S5OUT = UT
        w, wr = wtile()
        w3 = r3(w, 8)[:, 0:4, :]
        wdma(w3, DR("glu_w")[e_:e_ + 1].rearrange("o (kc p) n -> p (o kc) n", p=128), wr)
        SG = [SC(49152 + i * 2048, 512, F32) for i in range(2)]
        it = 0
        for fo in range(4):
            for tb in range(3):
                t0 = TBS[tb][0]
                pb = it % 2
                sg = SG[it % 2]
                it += 1
                for kc in range(4):
                    S.op("pe", _call("matmul", bank(pb), w3[:, kc, fo * 128:(fo + 1) * 128], GT[:, kc, t0:t0 + 512], start=(kc == 0), stop=(kc == 3)),
                         reads=[wr] + ["GT%d_%d" % (kc, tb)], writes=["PS%d" % pb])
                S.op("act", _call("activation", sg, bank(pb), AF.Sigmoid, bias=vcol("glub%d" % e_, fo), scale=1.0),
                     reads=["PS%d" % pb, "VEC"], writes=["SG%d" % (it % 2)])
                S.op("dve", _call("tensor_tensor", S5OUT[:, fo, t0:t0 + 512], sg, GT[:, fo, t0:t0 + 512], ALU.mult),
                     reads=["SG%d" % (it % 2), "GT%d_%d" % (fo, tb)], writes=["S5O%d_%d" % (fo, tb)])
        S.barrier()
        tap_bf16("s5out_%d" % l, S5OUT, ["S5O%d_%d" % (c, tb) for c in range(4) for tb in range(3)], 4, 0)
        chk("glu")
        QT = r3(SC(49152, 4 * NT), 4)
        KT = r3(SC(61440, 4 * 2048), 4)
        VT = r3(SC(77824, 16 * 512), 16)
        ROPE = r3(SC(94208, 2 * 1024, F32), 2)
        STG = SC(104448, 512, F32)
        T1 = SC(106496, 512, F32)
        T2 = SC(108544, 512, F32)
        S.dma("sp", _call("dma_start", out=ROPE, in_=DR("ropeD")), writes=["ROPE"], key="rope")
        S.dma("pool", _call("dma_start", out=KT[:, :, 512:1024], in_=DR("cdkT")[e_:e_ + 1].rearrange("o (c p) t -> p (o c) t", p=128)),
              writes=["KTc"], key="KTc")
        S.dma("pool", _call("dma_start", out=VT[:, 4:8, :], in_=DR("cdv")[e_:e_ + 1].rearrange("o (t p) f -> p (o t) f", p=128)),
              writes=["VTc"], key="VTc")
        nk_dst = DR("nkT")[e_:e_ + 1].rearrange("o (c p) t -> p (o c) t", p=128)
        for which in (0, 1):
            w, wr = wtile()
            w3 = r3(w, 8)
            wdma(w3, wsrc[:, :, 512 * (1 + which): 512 * (2 + which)], wr)
            ws_, wsr = wtile()
            ws3 = r3(ws_, 8)
            wdma(ws3, wsw[:, :, 512 * which: 512 * (which + 1)], wsr)
            for fc in range(4):
                for tb in range(3):
                    t0 = TBS[tb][0]
                    proj_fm(w3, wr, fc, tb, 0)
                    if tb == 0:
                        if which == 0:
                            S.op("act", _call("activation", QT[:, fc, 0:512], bank(0), AF.Identity), reads=["PS0"], writes=["QT%d_0" % fc])
                        else:
                            S.op("act", _call("activation", KT[:, fc, 0:512], bank(0), AF.Identity), reads=["PS0"], writes=["KT%d_0" % fc])
                            S.op("dve", _call("tensor_copy", STG, bank(0)), reads=["PS0"], writes=["STG"])
                            S.dma("sp", _call("dma_start", out=nk_dst[:, fc, :], in_=STG), reads=["STG"], key="oSTG")
                    else:
                        proj_fm(ws3, wsr, fc, tb, 1)
                        r0 = t0 - 512
                        S.op("dve", _call("tensor_tensor", T1, bank(0), ROPE[:, 0, r0:r0 + 512], ALU.mult), reads=["PS0", "ROPE"], writes=["T1"])
                        S.op("dve", _call("tensor_tensor", T2, bank(1), ROPE[:, 1, r0:r0 + 512], ALU.mult), reads=["PS1", "ROPE"], writes=["T2"])
                        if which == 0:
                            dst, dres = QT[:, fc, t0:t0 + 512], "QT%d_%d" % (fc, tb)
                        else:
                            dst, dres = KT[:, fc, 512 + t0: 512 + t0 + 512], "KT%d_%d" % (fc, tb)
                        S.op("dve", _call("tensor_tensor", dst, T1, T2, ALU.add), reads=["T1", "T2"], writes=[dres])
        w, wr = wtile()
        w3 = r3(w, 8)
        wdma(w3, wsrc[:, :, 1536:2048], wr)
        nv_dst = DR("nv")[e_:e_ + 1].rearrange("o (t p) f -> p (o t) f", p=128)
        for tt in range(12):
            pb = tt % 2
            vt_i = tt if tt < 4 else tt + 4
            for kc in range(8):
                S.op("pe", _call("matmul", bank(pb), HT[:, kc, tt * 128:(tt + 1) * 128], w3[:, kc, :], start=(kc == 0), stop=(kc == 7)),
                     reads=[wr, "HT%d_%d" % (kc, tt // 4)], writes=["PS%d" % pb])
            S.op("act", _call("activation", VT[:, vt_i, :], bank(pb), AF.Identity), reads=["PS%d" % pb], writes=["VT%d" % vt_i])
            if tt < 4:
                S.op("dve", _call("tensor_copy", STG, bank(pb)), reads=["PS%d" % pb], writes=["STG"])
                S.dma("sp", _call("dma_start", out=nv_dst[:, tt, :], in_=STG), reads=["STG"], key="oSTG")
        chk("qkv")
        DL = r3(SC(110592, 256, F32), 4)
        DP = r3(SC(111616, 128, F32), 2)
        S.dma("sp", _call("dma_start", out=DL, in_=DR("dlam")[e_:e_ + 1].rearrange("o a b -> (o a) b").partition_broadcast(128)), writes=["DL"], key="dl")
        S.op("dve", _call("tensor_tensor", DP[:, 0, :], DL[:, 0, :], DL[:, 1, :], ALU.mult), reads=["DL"], writes=["DP"])
        S.op("dve", _call("tensor_tensor", DP[:, 1, :], DL[:, 2, :], DL[:, 3, :], ALU.mult), reads=["DL", "DP"], writes=["DP"])
        S.op("dve", _call("reduce_sum", LAMS[:, 0:2], DP, AX.X), reads=["DP"], writes=["LAMS"])
        S.op("act", _call("activation", LAMS[:, 2:4], LAMS[:, 0:2], AF.Exp), reads=["LAMS"], writes=["LAMS"])
        S.op("dve", _call("tensor_tensor", LAMS[:, 4:5], LAMS[:, 3:4], LAMS[:, 2:3], ALU.subtract), reads=["LAMS"], writes=["LAMS"])
        S.op("dve", _call("tensor_scalar", LAMS[:, 4:5], LAMS[:, 4:5], -lam_init, None, ALU.add), reads=["LAMS"], writes=["LAMS"])
        S.op("dve", _call("tensor_scalar", LAMS[:, 5:6], vcol("subg%d" % e_), 1.0 - lam_init, None, ALU.mult), reads=["VEC", "LAMS"], writes=["LAMS"])
        QZ = [r3(SC(0, 4 * NT), 4), r3(SC(12288, 4 * NT), 4)]
        allht = ["HT%d_%d" % (c_, t_) for c_ in range(8) for t_ in range(3)]
        allqt = ["QT%d_%d" % (c_, t_) for c_ in range(4) for t_ in range(3)]
        S.op("pool", _call("memset", QZ[0][64:128], 0.0), writes=allht + ["QZ0"])
        S.op("pool", _call("memset", QZ[1][0:64], 0.0), writes=allht + ["QZ1"])
        S.op("act", _call("activation", QZ[0][0:64], QT[0:64], AF.Identity), reads=allqt, writes=allht + ["QZ0"])
        S.op("dve", _call("tensor_copy", QZ[1][64:128], QT[64:128]), reads=allqt, writes=allht + ["QZ1"])
        S.barrier()
        OT = GT
        PT = [SC(98304 + i * 1024, 512) for i in range(4)]
        REC = SC(102400, 512, F32)
        REC2 = SC(110592, 512, F32)
        OC = [T1, T2]
        OO = STG
        SQ = SC(112640, 512)
        seqs = [(0, 256, [0, 1], 0), (256, 256, [2, 3], 256), (512, 1024, list(range(4, 16)), 512)]
        tasks, epis = [], []
        for (q0, qlen, vtiles, k0) in seqs:
            nqb = max(1, qlen // 512)
            N = min(qlen, 512)
            for h in range(4):
                for qb in range(nqb):
                    qs = q0 + qb * 512
                    tbq = 0 if q0 < 512 else 1 + qb
                    for c in range(2):
                        p0, p1 = 64 * c, 64 * c + 64
                        g = len(epis)
                        for ki, vt_i in enumerate(vtiles):
                            kc0 = k0 + ki * 128

                            def A(ps, h=h, kc0=kc0, qs=qs, c=c, N=N):
                                S.op("pe", _call("matmul", bank(ps, N), KT[:, h, kc0:kc0 + 128], QZ[c][:, h, qs:qs + N], start=True, stop=True),
                                     reads=["KTc", "QZ%d" % c] + ["KT%d_%d" % (h, t) for t in range(3)], writes=["PS%d" % ps])

                            def C(pt, ptr, ob, sb, first, last, vt_i=vt_i, h=h, N=N):
                                S.op("pe", _call("matmul", bank(ob, N), VT[:, vt_i, h * 128:(h + 1) * 128], pt[:, 0:N], start=first, stop=last),
                                     reads=[ptr, "VT%d" % vt_i, "VTc"], writes=["PS%d" % ob])
                                S.op("pe", _call("matmul", bank(sb, N), ONES, pt[:, 0:N], start=first, stop=last),
                                     reads=[ptr, "ONES"], writes=["PS%d" % sb])
                            tasks.append((g, ki, len(vtiles), N, A, C))

                        def E(ob, sb, N, c=c, h=h, qs=qs):
                            S.op("dve", _call("reciprocal", REC[:, 0:N], bank(sb, N)), reads=["PS%d" % sb], writes=["REC"])
                            S.op("dve", _call("tensor_tensor", OC[c][:, 0:N], bank(ob, N), REC[:, 0:N], ALU.mult), reads=["PS%d" % ob, "REC"], writes=["OC%d" % c])
                            if c == 0:
                                return None
                            S.op("dve", _call("scalar_tensor_tensor", OO[:, 0:N], OC[1][:, 0:N], LAMS[:, 4:5], OC[0][:, 0:N], ALU.mult, ALU.add),
                                 reads=["OC0", "OC1", "LAMS"], writes=["OO"])
                            S.op("act", _call("activation", SQ[:, 0:N], OO[:, 0:N], AF.Square), reads=["OO"], writes=["SQa"])

                            def E2():
                                S.op("pe", _call("matmul", bank(7, N), ONES, SQ[:, 0:N], start=True, stop=True), reads=["SQa", "ONES"], writes=["PS7"])
                                S.op("act", _call("activation", REC2[:, 0:N], bank(7, N), AF.Sqrt, bias=EPS, scale=1.0 / 128), reads=["PS7"], writes=["REC2"])
                                S.op("dve", _call("reciprocal", REC2[:, 0:N], REC2[:, 0:N]), reads=["REC2"], writes=["REC2"])
                                S.op("dve", _call("tensor_tensor", OO[:, 0:N], OO[:, 0:N], REC2[:, 0:N], ALU.mult), reads=["OO", "REC2"], writes=["OO"])
                                S.op("act", _call("activation", OT[:, h, qs:qs + N], OO[:, 0:N], AF.Identity, scale=LAMS[:, 5:6]),
                                     reads=["OO", "LAMS"], writes=["OT%d" % h])
                            return E2
                        epis.append(E)
        attn_pipeline(tasks, epis, PT, 0.125)
        S.barrier()
        tap_bf16("diffout_%d" % l, OT, ["OT%d" % h for h in range(4)], 4, 4)
        chk("attn")
        osrc = DR("w_out_even")[e_:e_ + 1].rearrange("o (kc p) n -> p (o kc) n", p=128)
        wts = []
        for half in range(2):
            w, wr = wtile()
            w3 = r3(w, 8)
            wdma(w3, osrc[:, :, half * 512:(half + 1) * 512], wr)
            wts.append((w3, wr))
        for tb, (t0, tl, ci) in enumerate(TBS):
            def yfn(oc, tb=tb, t0=t0):
                w3, wr = wts[oc // 4]
                oi = oc % 4
                pb = oc % 2
                for kc in range(8):
                    src = S5OUT[:, kc, t0:t0 + 512] if kc < 4 else OT[:, kc - 4, t0:t0 + 512]
                    S.op("pe", _call("matmul", bank(pb), w3[:, kc, oi * 128:(oi + 1) * 128], src, start=(kc == 0), stop=(kc == 7)),
                         reads=[wr], writes=["PS%d" % pb])
                return bank(pb), "PS%d" % pb
            post_norm_res(l, 1, tb, 49152, 65536, yfn)
        S.barrier()

    def s5(l, e_, UT, GT):
        SLOT = 25856
        ENG = ["dve", "dve"]

        def slot_bufs(s_):
            o = 49152 + s_ * SLOT
            return dict(
                HB=r3(SC(o, 2 * NT, F32), 2), HB16=r3(SC(o + 12288, 2 * NT), 2),
                EA=r3(SC(o + 18432, 2 * 323, F32), 2), EB=r3(SC(o + 21016, 2 * 323, F32), 2),
                XA=r3(SC(o + 23600, 2 * 192, F32), 2),
                TD=SC(o + 12288, 512, F32), TC=SC(o + 14336, 512, F32))
        SB = [slot_bufs(0), slot_bufs(1)]
        sh0 = 49152 + 2 * SLOT
        PW = SC(sh0, 3 * 15 * 32, F32).rearrange("p (a k j) -> p a k j", a=3, k=15)
        S5P = r3(SC(sh0 + 5760, 5 * 32, F32), 5)
        FF = r3(SC(sh0 + 6400, 3 * 32, F32), 3)
        FIN = SC(sh0 + 6784, 128, F32).rearrange("p (d c s q) -> p d c s q", d=2, c=2, s=2)
        GTMP = [SC(sh0 + 7296 + i * 2048, 512, F32) for i in range(3)]
        TM = r3(SC(49152, 6 * 15 * 32, F32), 6)
        S.dma("sp", _call("dma_start", out=S5P, in_=DR("s5p")[e_:e_ + 1].rearrange("o p a j -> p (o a) j")), writes=["S5P"], key="s5p")
        lr, li, ldt = S5P[:, 0, :], S5P[:, 1, :], S5P[:, 2, :]
        tm = [TM[:, i, :].rearrange("p (k j) -> p k j", k=15) for i in range(6)]
        sm = [TM[:, i, 0:32] for i in range(6)]
        D_ = "S5C"

        def dv(fn, eng="dve"):
            S.op(eng, fn, reads=[D_, "S5P", "CST"], writes=[D_])
        dv(_call("activation", sm[0], ldt, AF.Exp), "act")
        dv(_call("tensor_tensor", sm[1], lr, sm[0], ALU.mult))
        dv(_call("tensor_tensor", sm[2], li, sm[0], ALU.mult))
        exb = CST[:, 0:15].unsqueeze(2).to_broadcast([128, 15, 32])
        dv(_call("tensor_tensor", tm[3], sm[1].unsqueeze(1).to_broadcast([128, 15, 32]), exb, ALU.mult))
        dv(_call("activation", tm[3], tm[3], AF.Exp), "act")
        dv(_call("tensor_tensor", tm[4], sm[2].unsqueeze(1).to_broadcast([128, 15, 32]), exb, ALU.mult))

        def sin_of(dst, shift):
            dv(_call("tensor_scalar", tm[5], tm[4], shift, 1.0 / (2 * math.pi), ALU.add, ALU.mult))
            ki = TM[:, 0, :].bitcast(mybir.dt.int32).rearrange("p (k j) -> p k j", k=15)
            dv(_call("tensor_copy", ki, tm[5]))
            dv(_call("tensor_copy", tm[5], ki))
            dv(_call("tensor_scalar", dst, tm[4], shift, None, ALU.add))
            dv(_call("scalar_tensor_tensor", dst, tm[5], -2 * math.pi, dst, ALU.mult, ALU.add))
            dv(_call("tensor_scalar", tm[5], dst, -math.pi, 2 * math.pi, ALU.is_lt, ALU.mult))
            dv(_call("tensor_tensor", dst, dst, tm[5], ALU.add))
            dv(_call("tensor_scalar", tm[5], dst, math.pi, -2 * math.pi, ALU.is_gt, ALU.mult))
            dv(_call("tensor_tensor", dst, dst, tm[5], ALU.add))
            dv(_call("activation", dst, dst, AF.Sin), "act")
        sin_of(tm[1], 0.0)
        sin_of(tm[2], math.pi / 2)
        dv(_call("tensor_tensor", PW[:, 0], tm[3], tm[2], ALU.mult))
        dv(_call("tensor_tensor", PW[:, 1], tm[3], tm[1], ALU.mult))
        dv(_call("tensor_scalar", PW[:, 2], PW[:, 1], -1.0, None, ALU.mult))
        are, aim = PW[:, 0, 0, :], PW[:, 1, 0, :]
        s0, s1, s2, s3, s4 = [TM[:, 0, 32 * i:32 * i + 32] for i in range(5)]
        dv(_call("tensor_scalar", s0, are, -1.0, None, ALU.add))
        dv(_call("tensor_tensor", s1, lr, lr, ALU.mult))
        dv(_call("tensor_tensor", s2, li, li, ALU.mult))
        dv(_call("tensor_tensor", s1, s1, s2, ALU.add))
        dv(_call("reciprocal", s1, s1))
        dv(_call("tensor_tensor", s2, s0, lr, ALU.mult))
        dv(_call("tensor_tensor", s3, aim, li, ALU.mult))
        dv(_call("tensor_tensor", s2, s2, s3, ALU.add))
        dv(_call("tensor_tensor", FF[:, 0, :], s2, s1, ALU.mult))
        dv(_call("tensor_tensor", s2, aim, lr, ALU.mult))
        dv(_call("tensor_tensor", s3, s0, li, ALU.mult))
        dv(_call("tensor_tensor", s2, s2, s3, ALU.subtract))
        dv(_call("tensor_tensor", FF[:, 1, :], s2, s1, ALU.mult))
        dv(_call("tensor_scalar", FF[:, 2, :], FF[:, 1, :], -1.0, None, ALU.mult))
        S.barrier()

        def pw(a, k, j):
            return PW[:, a, k, j:j + 1]

        def rm(ap512):
            return ap512.rearrange("p (r m) -> p r m", r=8)

        for s_i in range(2):
            S.op("dve", _call("memset", SB[s_i]["EA"], 0.0), writes=["EA_%d" % s_i])
            S.op("dve", _call("memset", SB[s_i]["EB"], 0.0), writes=["EB_%d" % s_i])
        tabsrc = DR("s5tab")[e_:e_ + 1]
        tabs = []
        for c in range(4):
            w, wr = wtile()
            wdma(w, tabsrc[:, c].rearrange("o p n -> p (o n)"), wr)
            tabs.append((w.rearrange("p (d q m n) -> p d q m n", d=2, q=4, m=4), wr))
        jobs = [(c, qq, d) for c in range(4) for qq in range(4) for d in range(2)]
        psb = [0]

        def rec_bu(n):
            c, qq, d = jobs[n]
            tab, wr = tabs[c]
            s_ = n % 2
            B_ = SB[s_]
            eng = ENG[s_]
            j = d * 16 + 4 * c + qq
            hbr = "HB_%d" % s_
            for tb, (t0, tl, ci) in enumerate(TBS):
                pr, pi = 4 + (psb[0] % 2) * 2, 5 + (psb[0] % 2) * 2
                psb[0] += 1
                m0 = t0 // 8
                u_rm = UT[:, c, t0:t0 + 512].rearrange("p (m r) -> p r m", r=8)
                HBr = B_["HB"].rearrange("p c (r m) -> p c r m", r=8)
                S.op("pe", _call("matmul", rm(bank(pr)), tab[:, d, qq, 0, :], u_rm, start=True, stop=True),
                     reads=[wr, "UT%d_%d" % (c, tb)], writes=["PS%d" % pr])
                S.op("pe", _call("matmul", rm(bank(pi)), tab[:, d, qq, 1, :], u_rm, start=True, stop=True),
                     reads=[wr, "UT%d_%d" % (c, tb)], writes=["PS%d" % pi])
                S.op("act", _call("activation", HBr[:, 0, :, m0:m0 + 64], rm(bank(pr)), AF.Identity, scale=FF[:, 0, j:j + 1]), reads=["PS%d" % pr, D_], writes=[hbr])
                S.op("act", _call("activation", HBr[:, 1, :, m0:m0 + 64], rm(bank(pi)), AF.Identity, scale=FF[:, 0, j:j + 1]), reads=["PS%d" % pi, D_], writes=[hbr])
                S.op(eng, _call("scalar_tensor_tensor", HBr[:, 0, :, m0:m0 + 64], rm(bank(pi)), FF[:, 2, j:j + 1], HBr[:, 0, :, m0:m0 + 64], ALU.mult, ALU.add),
                     reads=["PS%d" % pi, hbr, D_], writes=[hbr])
                S.op(eng, _call("scalar_tensor_tensor", HBr[:, 1, :, m0:m0 + 64], rm(bank(pr)), FF[:, 1, j:j + 1], HBr[:, 1, :, m0:m0 + 64], ALU.mult, ALU.add),
                     reads=["PS%d" % pr, hbr, D_], writes=[hbr])

        def rec_scan(n):
            c, qq, d = jobs[n]
            q = 4 * c + qq
            s_ = n % 2
            B_ = SB[s_]
            eng = ENG[s_]
            j = d * 16 + q
            HB, HB16, EA, EB = B_["HB"], B_["HB16"], B_["EA"], B_["EB"]
            hbr, h16r, ear, ebr = "HB_%d" % s_, "HB16_%d" % s_, "EA_%d" % s_, "EB_%d" % s_
            ops = []

            def EM(fn, reads=(), writes=()):
                ops.append((fn, reads, writes))
            HBr = HB.rearrange("p c (r m) -> p c r m", r=8)
            H16r = HB16.rearrange("p c (r m) -> p c r m", r=8)
            h16 = HB16.rearrange("p c (m r) -> p c m r", r=8)
            hvP = HB[:, :, 0:512].rearrange("p c (s m r) -> p c s m r", s=2, r=8)
            h16P = HB16[:, :, 0:512].rearrange("p c (s m r) -> p c s m r", s=2, r=8)

            def cm(dst2, src2, k, reads, writes, out2=None, neg_im=False):
                are_, aim_, nim_ = pw(0, k, j), pw(1, k, j), pw(2, k, j)
                o2 = dst2 if out2 is None else out2
                if len(dst2.shape) <= 3:
                    EM(_call("scalar_tensor_tensor", dst2, src2, are_, dst2, ALU.mult, ALU.add), reads=reads, writes=writes)
                else:
                    EM(_call("scalar_tensor_tensor", dst2[:, 0], src2[:, 0], are_, dst2[:, 0], ALU.mult, ALU.add), reads=reads, writes=writes)
                    EM(_call("scalar_tensor_tensor", dst2[:, 1], src2[:, 1], are_, dst2[:, 1], ALU.mult, ALU.add), reads=reads, writes=writes)
                EM(_call("scalar_tensor_tensor", o2[:, 0], src2[:, 1], nim_, dst2[:, 0], ALU.mult, ALU.add), reads=reads, writes=writes)
                if neg_im:
                    EM(_call("scalar_tensor_tensor", o2[:, 1], src2[:, 0], nim_, dst2[:, 1], ALU.mult, ALU.subtract), reads=reads, writes=writes)
                else:
                    EM(_call("scalar_tensor_tensor", o2[:, 1], src2[:, 0], aim_, dst2[:, 1], ALU.mult, ALU.add), reads=reads, writes=writes)

            def cp(dst, src, reads, writes):
                if len(dst.shape) <= 3:
                    EM(_call("tensor_copy", dst, src), reads=reads, writes=writes)
                else:
                    for cc_ in range(2):
                        EM(_call("tensor_copy", dst[:, cc_], src[:, cc_]), reads=reads, writes=writes)
            order = range(1, 8) if d == 0 else range(6, -1, -1)
            for r in order:
                rsrc = r - 1 if d == 0 else r + 1
                cm(HBr[:, :, r, :], HBr[:, :, rsrc, :], 0, [hbr, D_], [hbr])
            rend = 7 if d == 0 else 0

            def PV(buf):
                if d == 0:
                    return buf[:, :, 0:130].rearrange("p c (s n) -> p c s n", s=2), 32
                return buf[:, :, 32:162].rearrange("p c (s n) -> p c s n", s=2), 0
            EAP, n0 = PV(EA)
            EBP, _n = PV(EB)
            S0 = 162
            eofs = 1 if d == 0 else 0
            hidx = 0 if d == 0 else 32
            cp(EAP[:, :, :, n0 + eofs:n0 + eofs + 32], HBr[:, :, rend, 0:64].rearrange("p c (s m) -> p c s m", s=2), [hbr], [ear])
            for cc_ in range(2):
                EM(_call("memset", EAP[:, cc_, :, n0 + hidx:n0 + hidx + 1], 0.0), writes=[ear])
            EM(_call("tensor_copy", EA[:, :, S0 + eofs:S0 + eofs + 128], HBr[:, :, rend, 64:192]), reads=[hbr], writes=[ear])
            hcol = S0 if d == 0 else S0 + 128
            EM(_call("tensor_copy", EA[:, :, hcol:hcol + 1], S5P[:, 3:5, j:j + 1]), reads=["S5P"], writes=[ear])
            bufs = [(EA, EAP, ear), (EB, EBP, ebr)]
            sgn = -1 if d == 0 else 1
            for lev in range(8):
                sh = 1 << lev
                (bi, biP, bir), (bo, boP, bor) = bufs[lev % 2], bufs[(lev + 1) % 2]
                k = 7 + lev
                are_, aim_, nim_ = pw(0, k, j), pw(1, k, j), pw(2, k, j)
                groups = []
                if sh <= 32:
                    so = n0 + sgn * sh
                    groups.append((boP[:, :, :, n0:n0 + 33], biP[:, :, :, n0:n0 + 33], biP[:, :, :, so:so + 33]))
                    so = S0 + sgn * sh
                    groups.append((bo[:, :, S0:S0 + 129], bi[:, :, S0:S0 + 129], bi[:, :, so:so + 129]))
                else:
                    EM(_call("tensor_copy", bo[:, :, 0:S0], bi[:, :, 0:S0]), reads=[bir], writes=[bor])
                    n_ = 129 - sh
                    if d == 0:
                        EM(_call("tensor_copy", bo[:, :, S0:S0 + sh], bi[:, :, S0:S0 + sh]), reads=[bir], writes=[bor])
                        groups.append((bo[:, :, S0 + sh:S0 + 129], bi[:, :, S0 + sh:S0 + 129], bi[:, :, S0:S0 + n_]))
                    else:
                        EM(_call("tensor_copy", bo[:, :, S0 + n_:S0 + 129], bi[:, :, S0 + n_:S0 + 129]), reads=[bir], writes=[bor])
                        groups.append((bo[:, :, S0:S0 + n_], bi[:, :, S0:S0 + n_], bi[:, :, S0 + sh:S0 + 129]))
                for (od, idd, isrc) in groups:
                    if len(od.shape) <= 3:
                        EM(_call("scalar_tensor_tensor", od, isrc, are_, idd, ALU.mult, ALU.add), reads=[bir, D_], writes=[bor])
                    else:
                        for cc_ in range(2):
                            EM(_call("scalar_tensor_tensor", od[:, cc_], isrc[:, cc_], are_, idd[:, cc_], ALU.mult, ALU.add), reads=[bir, D_], writes=[bor])
                    EM(_call("scalar_tensor_tensor", od[:, 0], isrc[:, 1], nim_, od[:, 0], ALU.mult, ALU.add), reads=[bir, bor, D_], writes=[bor])
                    EM(_call("scalar_tensor_tensor", od[:, 1], isrc[:, 0], aim_, od[:, 1], ALU.mult, ALU.add), reads=[bir, bor, D_], writes=[bor])
            fidx = 32 if d == 0 else 0
            cp(FIN[:, d, :, :, q:q + 1], EAP[:, :, :, n0 + fidx:n0 + fidx + 1], [ear], ["FIN"])
            xo = 0 if d == 0 else 1
            XA = B_["XA"]
            xar = "XA_%d" % s_
            EM(_call("tensor_copy", XA[:, :, 0:32], EA[:, :, 32 + xo:32 + xo + 32]), reads=[ear], writes=[xar])
            EM(_call("tensor_copy", XA[:, :, 32:64], EA[:, :, 97 + xo:97 + xo + 32]), reads=[ear], writes=[xar])
            EM(_call("tensor_copy", XA[:, :, 64:192], EA[:, :, 162 + xo:162 + xo + 128]), reads=[ear], writes=[xar])
            for r in range(8):
                k = r if d == 0 else 7 - r
                cm(HBr[:, :, r, :], XA, k, [hbr, xar, D_], [hbr, h16r], out2=H16r[:, :, r, :], neg_im=True)
            return ops

        def rec_y(n):
            c, qq, d = jobs[n]
            tab, wr = tabs[c]
            s_ = n % 2
            HB16 = SB[s_]["HB16"]
            first = (qq == 0 and d == 0)
            last = (qq == 3 and d == 1)
            for tb, (t0, tl, ci) in enumerate(TBS):
                m0 = t0 // 8
                H16r = HB16.rearrange("p c (r m) -> p c r m", r=8)
                S.op("pe", _call("matmul", rm(bank(tb)), tab[:, d, qq, 2, :], H16r[:, 0, :, m0:m0 + 64], start=first, stop=False),
                     reads=[wr, "HB16_%d" % s_], writes=["PS%d" % tb])
                S.op("pe", _call("matmul", rm(bank(tb)), tab[:, d, qq, 3, :], H16r[:, 1, :, m0:m0 + 64], start=False, stop=last),
                     reads=[wr, "HB16_%d" % s_], writes=["PS%d" % tb])

        def rec_gelu(c):
            for tb, (t0, tl, ci) in enumerate(TBS):
                g0, g1, g2 = GTMP
                nat = lambda ap_: ap_.rearrange("p (m r) -> p m r", r=8)
                S.op("dve", _call("scalar_tensor_tensor", nat(g0), nat(UT[:, c, t0:t0 + 512]), vcol("s5d%d" % e_, c),
                                  bank(tb).rearrange("p (r m) -> p m r", r=8), ALU.mult, ALU.add),
                     reads=["PS%d" % tb, "UT%d_%d" % (c, tb), "VEC"], writes=["G0"])
                S.op("act", _call("activation", g1, g0, AF.Square), reads=["G0"], writes=["G1"])
                S.op("dve", _call("tensor_scalar", g1, g1, 0.044715, 1.0, ALU.mult, ALU.add), reads=["G1"], writes=["G1"])
                S.op("dve", _call("tensor_tensor", g1, g1, g0, ALU.mult), reads=["G1", "G0"], writes=["G1"])
                S.op("act", _call("activation", g2, g1, AF.Sigmoid, scale=2.0 * math.sqrt(2.0 / math.pi)), reads=["G1"], writes=["G2"])
                S.op("dve", _call("tensor_tensor", GT[:, c, t0:t0 + 512], g2, g0, ALU.mult), reads=["G2", "G0"], writes=["GT%d_%d" % (c, tb)])

        NJ_ = len(jobs)
        for p_ in range(NJ_ // 2):
            rec_bu(2 * p_)
            rec_bu(2 * p_ + 1)
            A_ = rec_scan(2 * p_)
            B_ = rec_scan(2 * p_ + 1)
            for i_ in range(max(len(A_), len(B_))):
                if i_ < len(A_):
                    S.op("dve", A_[i_][0], reads=A_[i_][1], writes=A_[i_][2])
                if i_ < len(B_):
                    S.op("dve", B_[i_][0], reads=B_[i_][1], writes=B_[i_][2])
            rec_y(2 * p_)
            rec_y(2 * p_ + 1)
            if p_ % 4 == 3 and p_ < NJ_ // 2 - 1:
                rec_gelu(p_ // 4)
        rec_gelu(3)
        S.dma("sp", _call("dma_start", out=DR("ns5")[e_:e_ + 1].rearrange("o p n -> p (o n)"), in_=FIN.rearrange("p d c s q -> p (d c s q)")), reads=["FIN"], key="oFIN")

    def odd_mixer(l):
        o_ = l // 2
        scale = (64 + 32) ** -0.5
        HT = r3(SC(0, 8 * NT), 8)
        CAT = HT
        CQT = r3(SC(24576, 2 * NT), 2)
        CKVT = SC(30720, 2048)
        KRT = SC(34816, 2048)
        VTOK = r3(SC(38912, 16 * 1024), 16)
        QNR = [SC(71680 + i * 3072, NT) for i in range(2)]
        KN2 = [SC(77824 + i * 4096, 2048) for i in range(2)]
        ROPE = r3(SC(86016, 2 * 1024, F32), 2)
        CQF = r3(SC(94208, 2 * 512, F32), 2)
        CKVF = SC(98304, 512, F32)
        KRF = SC(100352, 512, F32)
        SQ = SC(102400, 512)
        PT = [SC(103424 + i * 1024, 512) for i in range(4)]
        REC = SC(107520, 512, F32)
        T1 = SC(109568, 512, F32)
        T2 = SC(111616, 512, F32)
        T3 = SC(113664, 512, F32)
        pre_norm(l, 1, HT, 38912)
        tap_bf16("h1_%d" % l, HT, ["HT%d_%d" % (c, tb) for c in range(8) for tb in range(3)])
        S.barrier()
        S.dma("sp", _call("dma_start", out=ROPE[0:32], in_=DR("ropeM")), writes=["ROPE"], key="rope")
        S.dma("pool", _call("dma_start", out=CKVT[:, 512:1024], in_=DR("cckvT")[o_:o_ + 1].rearrange("o p t -> p (o t)")), writes=["CKVc"], key="CKVc")
        for i_ in range(2):
            S.dma("pool", _call("dma_start", out=KN2[i_][64:96, 512:1024], in_=DR("ckrT")[o_:o_ + 1].rearrange("o p t -> p (o t)")), writes=["KRc"], key="KRc%d" % i_)
            S.op("pool", _call("memset", KN2[i_][96:128, :], 0.0), writes=["KNZ"])
            S.op("pool", _call("memset", QNR[i_][96:128, :], 0.0), writes=["KNZ"])
        w, wr = wtile()
        w3 = r3(w, 8)[:, :, 0:416]
        wdma(w3, DR("w_in_odd")[o_:o_ + 1].rearrange("o (kc p) n -> p (o kc) n", p=128), wr)
        ws_, wsr = wtile()
        ws3 = r3(ws_, 8)[:, :, 0:32]
        wdma(ws3, DR("w_kr_sw")[o_:o_ + 1].rearrange("o (kc p) n -> p (o kc) n", p=128), wsr)
        qg0 = VC["qng%d" % o_][0]
        kvg0 = VC["kvg%d" % o_][0]
        nckv_dst = DR("nckvT")[o_:o_ + 1].rearrange("o p t -> p (o t)")
        nkr_dst = DR("nkrT")[o_:o_ + 1].rearrange("o p t -> p (o t)")
        for tb, (t0, tl, ci) in enumerate(TBS):
            kcol = t0 if tb == 0 else 512 + t0
            for cc in range(2):
                for kc in range(8):
                    S.op("pe", _call("matmul", bank(cc), w3[:, kc, cc * 128:(cc + 1) * 128], HT[:, kc, t0:t0 + 512], start=(kc == 0), stop=(kc == 7)),
                         reads=[wr, "HT%d_%d" % (kc, tb)], writes=["PS%d" % cc])
                S.op("act", _call("activation", CQF[:, cc, :], bank(cc), AF.Identity), reads=["PS%d" % cc], writes=["CQF%d" % cc])
                S.op("act", _call("activation", SQ, bank(cc), AF.Square), reads=["PS%d" % cc], writes=["SQa"])
                S.op("pe", _call("matmul", bank(6), ONES, SQ, start=(cc == 0), stop=(cc == 1)), reads=["SQa", "ONES"], writes=["PS6"])
            rstd_from(bank(6), REC, ["PS6"], "REC", 1.0 / 256)
            for cc in range(2):
                S.op("dve", _call("tensor_tensor", CQF[:, cc, :], CQF[:, cc, :], REC, ALU.mult), reads=["CQF%d" % cc, "REC"], writes=["CQF%d" % cc])
                S.op("act", _call("activation", CQT[:, cc, t0:t0 + 512], CQF[:, cc, :], AF.Identity, scale=VEC[:, qg0 + cc:qg0 + cc + 1]),
                     reads=["CQF%d" % cc, "VEC"], writes=["CQT%d" % tb])
            for kc in range(8):
                S.op("pe", _call("matmul", bank(2), w3[:, kc, 256:384], HT[:, kc, t0:t0 + 512], start=(kc == 0), stop=(kc == 7)),
                     reads=[wr, "HT%d_%d" % (kc, tb)], writes=["PS2"])
            S.op("act", _call("activation", CKVF, bank(2), AF.Identity), reads=["PS2"], writes=["CKVF"])
            S.op("act", _call("activation", SQ, bank(2), AF.Square), reads=["PS2"], writes=["SQa"])
            S.op("pe", _call("matmul", bank(6), ONES, SQ, start=True, stop=True), reads=["SQa", "ONES"], writes=["PS6"])
            rstd_from(bank(6), REC, ["PS6"], "REC", 1.0 / 128)
            S.op("dve", _call("tensor_tensor", CKVF, CKVF, REC, ALU.mult), reads=["CKVF", "REC"], writes=["CKVF"])
            S.op("dve", _call("tensor_scalar", CKVF, CKVF, VEC[:, kvg0:kvg0 + 1], None, ALU.mult), reads=["CKVF", "VEC"], writes=["CKVF"])
            S.op("act", _call("activation", CKVT[:, kcol:kcol + 512], CKVF, AF.Identity), reads=["CKVF"], writes=["CKV%d" % tb])
            if tb == 0:
                S.dma("sp", _call("dma_start", out=nckv_dst, in_=CKVF), reads=["CKVF"], key="oCKV")
            for kc in range(8):
                S.op("pe", _call("matmul", bank(3, 512)[0:32], w3[:, kc, 384:416], HT[:, kc, t0:t0 + 512], start=(kc == 0), stop=(kc == 7)),
                     reads=[wr, "HT%d_%d" % (kc, tb)], writes=["PS3"])
            if tb == 0:
                S.op("act", _call("activation", KRF[0:32], bank(3)[0:32], AF.Identity), reads=["PS3"], writes=["KRF"])
                for i_ in range(2):
                    S.op("act", _call("activation", KN2[i_][64:96, kcol:kcol + 512], bank(3)[0:32], AF.Identity), reads=["PS3"], writes=["KR%d" % tb])
                S.dma("sp", _call("dma_start", out=nkr_dst, in_=KRF[0:32]), reads=["KRF"], key="oKR")
            else:
                for kc in range(8):
                    S.op("pe", _call("matmul", bank(4)[0:32], ws3[:, kc, :], HT[:, kc, t0:t0 + 512], start=(kc == 0), stop=(kc == 7)),
                         reads=[wsr, "HT%d_%d" % (kc, tb)], writes=["PS4"])
                r0 = t0 - 512
                S.op("dve", _call("tensor_tensor", T1[0:32], bank(3)[0:32], ROPE[0:32, 0, r0:r0 + 512], ALU.mult), reads=["PS3", "ROPE"], writes=["T1"])
                S.op("dve", _call("tensor_tensor", T2[0:32], bank(4)[0:32], ROPE[0:32, 1, r0:r0 + 512], ALU.mult), reads=["PS4", "ROPE"], writes=["T2"])
                S.op("dve", _call("tensor_tensor", T3[0:32], T1[0:32], T2[0:32], ALU.add), reads=["T1", "T2"], writes=["T3"])
                for i_ in range(2):
                    S.op("act", _call("activation", KN2[i_][64:96, kcol:kcol + 512], T3[0:32], AF.Identity), reads=["T3"], writes=["KR%d" % tb])
        chk("mla_inproj")
        kv_reads = ["CKVc", "CKV0", "CKV1", "CKV2"]
        kr_reads = ["KRc", "KR0", "KR1", "KR2"]
        w, wr = wtile()
        wv = w[:, 0:1024]
        wdma(wv, DR("wkv_v")[o_:o_ + 1].rearrange("o p n -> p (o n)"), wr)
        for kt in range(16):
            for hf in range(2):
                pb = (kt * 2 + hf) % 2
                S.op("pe", _call("matmul", bank(pb), CKVT[:, kt * 128:(kt + 1) * 128], wv[:, hf * 512:(hf + 1) * 512], start=True, stop=True),
                     reads=[wr] + kv_reads, writes=["PS%d" % pb])
                S.op("act", _call("activation", VTOK[:, kt, hf * 512:(hf + 1) * 512], bank(pb), AF.Identity), reads=["PS%d" % pb], writes=["VTOK"])
        chk("mla_vtok")
        wq, wqr = wtile()
        wq3 = r3(wq, 2)[:, :, 0:1536]
        wdma(wq3, DR("wq_up")[o_:o_ + 1].rearrange("o (kc p) n -> p (o kc) n", p=128), wqr)
        wqs, wqsr = wtile()
        wqs3 = r3(wqs, 2)[:, :, 0:512]
        wdma(wqs3, DR("wq_up_sw")[o_:o_ + 1].rearrange("o (kc p) n -> p (o kc) n", p=128), wqsr)
        wk, wkr = wtile()
        wkn = wk[:, 0:1024]
        wdma(wkn, DR("wkv_kn")[o_:o_ + 1].rearrange("o p n -> p (o n)"), wkr)
        seqs = [(0, 256, [0, 1], 0), (256, 256, [2, 3], 256), (512, 1024, list(range(4, 16)), 512)]

        def prologue_steps(h):
            QNRh, KNh = QNR[h % 2], KN2[h % 2]
            qres, kres = "QNR%d" % (h % 2), "KN%d" % (h % 2)
            steps = []
            for tb, (t0, tl, ci) in enumerate(TBS):
                def st_qn(tb=tb, t0=t0):
                    for kc in range(2):
                        S.op("pe", _call("matmul", bank(7)[0:64], wq3[:, kc, h * 96:h * 96 + 64], CQT[:, kc, t0:t0 + 512], start=(kc == 0), stop=(kc == 1)),
                             reads=[wqr, "CQT%d" % tb], writes=["PS7"])
                    S.op("act", _call("activation", QNRh[0:64, t0:t0 + 512], bank(7)[0:64], AF.Identity), reads=["PS7"], writes=[qres])
                steps.append(st_qn)

                def st_qr(tb=tb, t0=t0):
                    for kc in range(2):
                        S.op("pe", _call("matmul", bank(7)[0:32], wq3[:, kc, h * 96 + 64:h * 96 + 96], CQT[:, kc, t0:t0 + 512], start=(kc == 0), stop=(kc == 1)),
                             reads=[wqr, "CQT%d" % tb], writes=["PS7"])
                    if tb == 0:
                        S.op("act", _call("activation", QNRh[64:96, t0:t0 + 512], bank(7)[0:32], AF.Identity), reads=["PS7"], writes=[qres])
                    else:
                        r0 = t0 - 512
                        S.op("dve", _call("tensor_tensor", T1[0:32], bank(7)[0:32], ROPE[0:32, 0, r0:r0 + 512], ALU.mult), reads=["PS7", "ROPE"], writes=["T1"])
                steps.append(st_qr)
                if tb > 0:
                    def st_qs(tb=tb, t0=t0):
                        for kc in range(2):
                            S.op("pe", _call("matmul", bank(7)[0:32], wqs3[:, kc, h * 32:h * 32 + 32], CQT[:, kc, t0:t0 + 512], start=(kc == 0), stop=(kc == 1)),
                                 reads=[wqsr, "CQT%d" % tb], writes=["PS7"])
                        r0 = t0 - 512
                        S.op("dve", _call("tensor_tensor", T2[0:32], bank(7)[0:32], ROPE[0:32, 1, r0:r0 + 512], ALU.mult), reads=["PS7", "ROPE"], writes=["T2"])
                        S.op("dve", _call("tensor_tensor", T3[0:32], T1[0:32], T2[0:32], ALU.add), reads=["T1", "T2"], writes=["T3"])
                        S.op("act", _call("activation", QNRh[64:96, t0:t0 + 512], T3[0:32], AF.Identity), reads=["T3"], writes=[qres])
                    steps.append(st_qs)
            for kb in range(4):
                def st_kn(kb=kb):
                    S.op("pe", _call("matmul", bank(7)[0:64], wkn[:, h * 64:(h + 1) * 64], CKVT[:, kb * 512:(kb + 1) * 512], start=True, stop=True),
                         reads=[wkr] + kv_reads, writes=["PS7"])
                    S.op("act", _call("activation", KNh[0:64, kb * 512:(kb + 1) * 512], bank(7)[0:64], AF.Identity), reads=["PS7"], writes=[kres])
                steps.append(st_kn)
            return steps

        for st in prologue_steps(0):
            st()
        tasks, epis = [], []
        for h in range(16):
            QNRh, KNh = QNR[h % 2], KN2[h % 2]
            qres, kres = "QNR%d" % (h % 2), "KN%d" % (h % 2)
            nxt = prologue_steps(h + 1) if h + 1 < 16 else []
            hp = h % 2
            tcount = 0
            for (q0, qlen, vtiles, k0) in seqs:
                nqb = max(1, qlen // 512)
                N = min(qlen, 512)
                for qb in range(nqb):
                    qs = q0 + qb * 512
                    g = len(epis)
                    for ki, vt_i in enumerate(vtiles):
                        kc0 = k0 + ki * 128

                        def A(ps, kc0=kc0, qs=qs, N=N, QNRh=QNRh, KNh=KNh, qres=qres, kres=kres):
                            S.op("pe", _call("matmul", bank(ps, N), KNh[:, kc0:kc0 + 128], QNRh[:, qs:qs + N], start=True, stop=True),
                                 reads=kr_reads + ["KNZ", kres, qres], writes=["PS%d" % ps])

                        def C(pt, ptr, ob, sb, first, last, vt_i=vt_i, N=N, hb=(h // 2) * 128):
                            S.op("pe", _call("matmul", bank(ob, N), VTOK[:, vt_i, hb:hb + 128], pt[:, 0:N], start=first, stop=last),
                                 reads=[ptr, "VTOK"], writes=["PS%d" % ob])
                            S.op("pe", _call("matmul", bank(sb, N), ONES, pt[:, 0:N], start=first, stop=last),
                                 reads=[ptr, "ONES"], writes=["PS%d" % sb])
                        pre = nxt[tcount // 2] if (tcount % 2 == 0 and tcount // 2 < len(nxt)) else None
                        tcount += 1
                        tasks.append((g, ki, len(vtiles), N, A, C, pre))

                    def E(ob, sb, N, qs=qs, a0=64 * hp, a1=64 * hp + 64, hc=h // 2):
                        S.op("dve", _call("reciprocal", REC[:, 0:N], bank(sb, N)), reads=["PS%d" % sb], writes=["REC"])
                        S.op("dve", _call("tensor_tensor", CAT[a0:a1, hc, qs:qs + N], bank(ob, N)[a0:a1], REC[a0:a1, 0:N], ALU.mult),
                             reads=["PS%d" % ob, "REC"], writes=["CAT"])
                        return None
                    epis.append(E)
            assert (tcount + 1) // 2 >= len(nxt)
        attn_pipeline(tasks, epis, PT, scale)
        S.barrier()
        chk("mla_attn")
        osrc = DR("w_out_odd")[o_:o_ + 1].rearrange("o (kc p) n -> p (o kc) n", p=128)
        wts = []
        for half in range(2):
            w, wr = wtile()
            w3o = r3(w, 8)
            wdma(w3o, osrc[:, :, half * 512:(half + 1) * 512], wr)
            wts.append((w3o, wr))
        for tb, (t0, tl, ci) in enumerate(TBS):
            def yfn(oc, tb=tb, t0=t0):
                w3o, wr = wts[oc // 4]
                oi = oc % 4
                pb = oc % 2
                for kc in range(8):
                    S.op("pe", _call("matmul", bank(pb), w3o[:, kc, oi * 128:(oi + 1) * 128], CAT[:, kc, t0:t0 + 512], start=(kc == 0), stop=(kc == 7)),
                         reads=[wr, "CAT"], writes=["PS%d" % pb])
                return bank(pb), "PS%d" % pb
            post_norm_res(l, 1, tb, 38912, 55296, yfn)
        S.barrier()

    try:
        chk("load")
        for l in range(n_layers):
            compute_mod(l)
            chk("mod")
            if l % 2 == 0:
                even_mixer(l)
            else:
                odd_mixer(l)
            tap_f32("xmix_%d" % l, XT, [xres(c, tb) for c in range(8) for tb in range(3)])
            chk("mixer")
            ffn(l)
            tap_f32("xffn_%d" % l, XT, [xres(c, tb) for c in range(8) for tb in range(3)])
    except _Stop:
        pass
    ydst = DR("yT").rearrange("(c p) t -> p c t", p=128)
    for c in range(8):
        S.dma("sp" if c % 2 == 0 else "act", _call("dma_start", out=ydst[:, c, :], in_=XT[:, c, :]),
              reads=[xres(c, tb) for tb in range(3)], key="out")


def _rope_tables():
    def ang(n, rot):
        rows = n // 64
        row = np.repeat(np.arange(rows, dtype=np.float32), 64)
        col = np.tile(np.arange(64, dtype=np.float32), rows)
        nf = rot // 4
        inv = (10000.0 ** (-np.arange(nf, dtype=np.float32) / nf)).astype(np.float32)
        return np.concatenate([row[:, None] * inv, col[:, None] * inv], axis=-1).astype(np.float32)
    a = ang(1024, 64)
    cosd, sind = np.cos(a), np.sin(a)
    ropeD = np.zeros((128, 2, 1024), np.float32)
    for p in range(128):
        d = p % 64
        j = d % 32
        ropeD[p, 0] = cosd[:, j]
        ropeD[p, 1] = -sind[:, j] if d < 32 else sind[:, j]
    a = ang(1024, 32)
    cosm, sinm = np.cos(a), np.sin(a)
    ropeM = np.zeros((32, 2, 1024), np.float32)
    for p in range(32):
        j = p % 16
        ropeM[p, 0] = cosm[:, j]
        ropeM[p, 1] = -sinm[:, j] if p < 16 else sinm[:, j]
    return ropeD, ropeM


def _swap_halves(w, half):
    return np.concatenate([w[..., half:], w[..., :half]], axis=-1)


def _prep_shared(inp):
    f = np.float32
    sh = {}
    sh["cst"] = np.tile(np.array(EXPS + [0], f)[None, :], (128, 1))
    sh["ropeD"], sh["ropeM"] = _rope_tables()
    sh["w_mod"] = inp["w_mod"]
    sh["wg"], sh["wu"], sh["wd"] = inp["w_ffn_gate"], inp["w_ffn_up"], inp["w_ffn_down"]
    wie = inp["w_in_even"]
    sh["w_in_even"] = wie
    qk = wie[:, :, 512:1536].reshape(2, 1024, 2, 4, 2, 64)
    sh["w_in_even_sw"] = np.ascontiguousarray(_swap_halves(qk, 32).reshape(2, 1024, 1024))
    sh["w_out_even"] = inp["w_out_even"]
    sh["glu_w"] = inp["s5_glu_w"]
    tab = np.zeros((2, 4, 128, 2, 4, 4, 128), f)
    for e in range(2):
        for d in range(2):
            for c in range(4):
                for qq in range(4):
                    for gi in range(2):
                        g = 8 * c + 2 * qq + gi
                        r0 = (2 * qq + gi) * 16
                        tab[e, c, r0:r0 + 16, d, qq, 0, gi * 64:(gi + 1) * 64] = inp["s5_b_re"][e, d, g].T
                        tab[e, c, r0:r0 + 16, d, qq, 1, gi * 64:(gi + 1) * 64] = inp["s5_b_im"][e, d, g].T
                        tab[e, c, gi * 64:(gi + 1) * 64, d, qq, 2, r0:r0 + 16] = inp["s5_c_re"][e, d, g].T
                        tab[e, c, gi * 64:(gi + 1) * 64, d, qq, 3, r0:r0 + 16] = inp["s5_c_im"][e, d, g].T
    sh["s5tab"] = tab.reshape(2, 4, 128, 4096)
    sh["dlam"] = np.stack([inp["diff_lam_q1"], inp["diff_lam_k1"], inp["diff_lam_q2"], inp["diff_lam_k2"]], axis=1).astype(f)
    wio = inp["w_in_odd"]
    sh["w_in_odd"] = wio
    sh["w_kr_sw"] = np.ascontiguousarray(_swap_halves(wio[:, :, 384:416], 16))
    wq = inp["mla_w_q_up"]
    sh["wq_up"] = wq
    sh["wq_up_sw"] = np.ascontiguousarray(_swap_halves(wq.reshape(2, 256, 16, 96)[..., 64:96], 16).reshape(2, 256, 512))
    wkv = inp["mla_w_kv_up"].reshape(2, 128, 16, 128)
    sh["wkv_kn"] = np.ascontiguousarray(wkv[..., :64].reshape(2, 128, 1024))
    sh["wkv_v"] = np.ascontiguousarray(wkv[..., 64:].reshape(2, 128, 1024))
    sh["w_out_odd"] = inp["w_out_odd"]
    return sh


def _prep_core(inp, c):
    f = np.float32
    m = {}
    x = np.concatenate([inp["x_prompt"][2 * c], inp["x_prompt"][2 * c + 1], inp["x_sample"][c]], axis=0)
    m["xT"] = np.ascontiguousarray(x.T)
    vec = np.zeros((128, NV), f)

    def put(name, arr):
        c0, n = VC[name]
        vec[:, c0:c0 + n] = np.asarray(arr, f).reshape(n, 128).T
    cond = np.zeros((8, 2, 128), f)
    cond[:, 0, :] = inp["c_ctx"].reshape(8, 128)
    cond[:, 1, :] = inp["c"][c].reshape(8, 128)
    put("cond", cond.reshape(16 * 128))
    for l in range(4):
        put("bmod%d" % l, inp["b_mod"][l])
        put("gpm%d" % l, inp["g_pre_mix"][l])
        put("gqm%d" % l, inp["g_post_mix"][l])
        put("gpf%d" % l, inp["g_pre_ffn"][l])
        put("gqf%d" % l, inp["g_post_ffn"][l])
    for e in range(2):
        put("s5d%d" % e, inp["s5_d"][e])
        put("glub%d" % e, inp["s5_glu_b"][e])
        put("subg%d" % e, inp["diff_subln_g"][e])
    for o in range(2):
        put("qng%d" % o, inp["mla_q_norm_g"][o])
        put("kvg%d" % o, inp["mla_kv_norm_g"][o])
    m["vecT"] = vec
    s5p = np.zeros((2, 128, 5, 32), f)
    for e in range(2):
        for d in range(2):
            for q in range(16):
                for gi in range(2):
                    g = 2 * q + gi
                    j = d * 16 + q
                    sl = slice(gi * 64, gi * 64 + 64)
                    s5p[e, sl, 0, j] = inp["s5_lam_re"][e, d, g]
                    s5p[e, sl, 1, j] = inp["s5_lam_im"][e, d, g]
                    s5p[e, sl, 2, j] = inp["s5_log_dt"][e, d, g]
                    s5p[e, sl, 3, j] = inp["state_s5_re"][c, e, d, g]
                    s5p[e, sl, 4, j] = inp["state_s5_im"][c, e, d, g]
    m["s5p"] = s5p
    m["cdkT"] = np.ascontiguousarray(inp["cache_diff_k"][c].reshape(2, 512, 512).transpose(0, 2, 1))
    m["cdv"] = np.ascontiguousarray(inp["cache_diff_v"][c].reshape(2, 512, 512))
    m["cckvT"] = np.ascontiguousarray(inp["cache_mla_ckv"][c].transpose(0, 2, 1))
    m["ckrT"] = np.ascontiguousarray(inp["cache_mla_krope"][c].transpose(0, 2, 1))
    return m


_NC_CACHE = {}


def kernel(**inputs):
    inp = {k: np.asarray(v) for k, v in inputs.items()}
    if "nc" not in _NC_CACHE:
        _NC_CACHE["nc"] = build(4)
    nc = _NC_CACHE["nc"]
    sh = _prep_shared(inp)
    in_maps = []
    for c in range(8):
        m = dict(sh)
        m.update(_prep_core(inp, c))
        in_maps.append({k: np.ascontiguousarray(v, dtype=np.float32) for k, v in m.items()})
    res = run_bass_kernel_spmd(nc, in_maps, core_ids=list(range(8)))
    R = res.results
    f = np.float32
    y_prompt = np.zeros((16, 256, 1024), f)
    y_sample = np.zeros((8, 1024, 1024), f)
    ns_re = np.zeros((16, 2, 2, 32, 64), f)
    ns_im = np.zeros((16, 2, 2, 32, 64), f)
    ndk = np.zeros((16, 2, 256, 4, 2, 64), f)
    ndv = np.zeros((16, 2, 256, 4, 128), f)
    nckv = np.zeros((16, 2, 256, 128), f)
    nkr = np.zeros((16, 2, 256, 32), f)
    for c in range(8):
        r = R[c]
        y = r["yT"].T
        y_prompt[2 * c] = y[0:256]
        y_prompt[2 * c + 1] = y[256:512]
        y_sample[c] = y[512:1536]
        ns5 = r["ns5"].reshape(2, 2, 64, 2, 2, 2, 16)
        for s in range(2):
            t = ns5[:, :, :, :, :, s, :].transpose(0, 3, 4, 5, 1, 2)
            ns_re[2 * c + s] = t[:, :, 0].reshape(2, 2, 32, 64)
            ns_im[2 * c + s] = t[:, :, 1].reshape(2, 2, 32, 64)
            ndk[2 * c + s] = r["nkT"][:, :, s * 256:(s + 1) * 256].transpose(0, 2, 1).reshape(2, 256, 4, 2, 64)
            ndv[2 * c + s] = r["nv"][:, s * 256:(s + 1) * 256, :].reshape(2, 256, 4, 128)
            nckv[2 * c + s] = r["nckvT"][:, :, s * 256:(s + 1) * 256].transpose(0, 2, 1)
            nkr[2 * c + s] = r["nkrT"][:, :, s * 256:(s + 1) * 256].transpose(0, 2, 1)
    return (y_prompt, y_sample, ns_re, ns_im, ndk, ndv, nckv, nkr)
```
